# Optimizing a Trainium2 kernel written in Bass

```python
import math
import jax, jax.numpy as jnp
from jax import lax
import numpy as np

D_MODEL = 1024
BATCH = 8
SEQ = 8192
DEPTH = 2
DEC_BATCH = 32
DEC_SEQ = 2048
PAST_LEN = 128

HEAD_DIM = 64
A_Q_HEADS = 8
A_KV_HEADS = 2
A_GROUP = A_Q_HEADS // A_KV_HEADS
WINDOW = 128
BLOCK = 128
B_HEADS = 4
ATTN_HEADS = A_Q_HEADS + B_HEADS
NUM_BUCKETS = 32
MAX_DISTANCE = 128
C_WIDTH = 512
D_WIDTH = 512
SHORT_CONV = 3
CONF_CONV = 31
FFN_HIDDEN = ((8 * D_MODEL // 3 + 255) // 256) * 256
EPS = 1e-6

A_Q_W = A_Q_HEADS * HEAD_DIM
A_KV_W = A_KV_HEADS * HEAD_DIM
B_QK_W = B_HEADS * 2 * HEAD_DIM
B_V_W = B_HEADS * 2 * HEAD_DIM
ATTN_IN = A_Q_W + 2 * A_KV_W + 2 * B_QK_W + B_V_W
ATTN_OUT = A_Q_W + B_V_W
ATTN_SPLITS = [A_Q_W, A_Q_W + A_KV_W, A_Q_W + 2 * A_KV_W,
               A_Q_W + 2 * A_KV_W + B_QK_W, A_Q_W + 2 * A_KV_W + 2 * B_QK_W]
CONV_IN = 3 * C_WIDTH + 2 * D_WIDTH
CONV_OUT = C_WIDTH + D_WIDTH
N_EVEN = (DEPTH + 1) // 2
N_ODD = DEPTH // 2

kernel_name = "hybrid_bidir_encoder_attn_conv"


def rms_norm(x, g):
    xf = x.astype(jnp.float32)
    y = xf * lax.rsqrt(jnp.mean(xf * xf, axis=-1, keepdims=True) + EPS) * g.astype(jnp.float32)
    return y.astype(x.dtype)


def layer_norm(x, g, b):
    xf = x.astype(jnp.float32)
    mu = jnp.mean(xf, axis=-1, keepdims=True)
    xc = xf - mu
    var = jnp.mean(xc * xc, axis=-1, keepdims=True)
    y = xc * lax.rsqrt(var + EPS) * g.astype(jnp.float32) + b.astype(jnp.float32)
    return y.astype(x.dtype)


def rel_bucket(rel):
    half = NUM_BUCKETS // 2
    max_exact = half // 2
    n = jnp.abs(rel)
    large = max_exact + (jnp.log(jnp.maximum(n, 1).astype(jnp.float32) / max_exact)
                         / math.log(MAX_DISTANCE / max_exact) * (half - max_exact)).astype(jnp.int32)
    large = jnp.minimum(large, half - 1)
    return jnp.where(rel > 0, half, 0) + jnp.where(n < max_exact, n, large)


def depthwise_conv(x, w):
    width, ch = w.shape
    return lax.conv_general_dilated(
        x, w[:, None, :].astype(x.dtype), window_strides=(1,),
        padding=[(width // 2, width // 2)],
        dimension_numbers=('NWC', 'WIO', 'NWC'), feature_group_count=ch)


def windowed_gqa(q, k, v, bias_table, sink):
    bsz, s_len = q.shape[0], q.shape[1]
    nb = s_len // BLOCK
    qb = q.reshape(bsz, nb, BLOCK, A_KV_HEADS, A_GROUP, HEAD_DIM)

    def band(t):
        tp = jnp.pad(t, ((0, 0), (BLOCK, BLOCK), (0, 0), (0, 0)))
        tp = tp.reshape(bsz, nb + 2, BLOCK, A_KV_HEADS, HEAD_DIM)
        return jnp.concatenate([tp[:, :-2], tp[:, 1:-1], tp[:, 2:]], axis=2)

    kb, vb = band(k), band(v)
    rel = (jnp.arange(3 * BLOCK)[None, :] - BLOCK) - jnp.arange(BLOCK)[:, None]
    bias = bias_table[rel_bucket(rel)].astype(jnp.float32)
    bias = bias.transpose(2, 0, 1).reshape(A_KV_HEADS, A_GROUP, BLOCK, 3 * BLOCK)
    key_pos = jnp.arange(nb)[:, None] * BLOCK - BLOCK + jnp.arange(3 * BLOCK)[None, :]
    valid = (key_pos >= 0) & (key_pos < s_len)
    mask = (jnp.abs(rel) <= WINDOW)[None, :, :] & valid[:, None, :]
    s = jnp.einsum('bnqhgd,bnkhd->bnhgqk', qb, kb).astype(jnp.float32) * (HEAD_DIM ** -0.5) + bias
    s = jnp.where(mask[None, :, None, None], s, -1e30)
    sk = sink.astype(jnp.float32).reshape(A_KV_HEADS, A_GROUP)[None, None, :, :, None, None]
    m = jnp.maximum(jnp.max(s, axis=-1, keepdims=True), sk)
    e = jnp.exp(s - m)
    p = e / (jnp.sum(e, axis=-1, keepdims=True) + jnp.exp(sk - m))
    o = jnp.einsum('bnhgqk,bnkhd->bnqhgd', p.astype(v.dtype), vb)
    return o.reshape(bsz, s_len, A_Q_HEADS * HEAD_DIM)


def differential_attention(q, k, v, bias_table, lam, subln_g, layer):
    bsz, s_len = q.shape[0], q.shape[1]
    nb = s_len // BLOCK
    lam_init = 0.8 - 0.6 * math.exp(-0.3 * layer)
    lf = lam.astype(jnp.float32)
    lam_full = jnp.exp(jnp.sum(lf[0] * lf[1])) - jnp.exp(jnp.sum(lf[2] * lf[3])) + lam_init
    k_pos = jnp.arange(s_len)
    qblocks = q.reshape(bsz, nb, BLOCK, B_HEADS, 2, HEAD_DIM).transpose(1, 0, 2, 3, 4, 5)

    def block(args):
        qblk, n = args
        q_pos = n * BLOCK + jnp.arange(BLOCK)
        bias = bias_table[rel_bucket(k_pos[None, :] - q_pos[:, None])].astype(jnp.float32)
        bias = bias.transpose(2, 0, 1)[None, :, None]
        s = jnp.einsum('bqhcd,bkhcd->bhcqk', qblk, k).astype(jnp.float32) * (HEAD_DIM ** -0.5) + bias
        p = jax.nn.softmax(s, axis=-1)
        p = p[:, :, 0] - lam_full * p[:, :, 1]
        return jnp.einsum('bhqk,bkhe->bqhe', p.astype(v.dtype), v)

    o = lax.map(block, (qblocks, jnp.arange(nb)))
    o = o.transpose(1, 0, 2, 3, 4).reshape(bsz, s_len, B_HEADS, 2 * HEAD_DIM)
    o = rms_norm(o, subln_g) * (1.0 - lam_init)
    return o.reshape(bsz, s_len, B_HEADS * 2 * HEAD_DIM)


def attention_mixer(h, w_in, w_out, a_qn, a_kn, a_sink, b_qn, b_kn, b_lam, b_subln, rel_bias, layer):
    bsz, s_len, _ = h.shape
    proj = h @ w_in
    qa, ka, va, qb, kb, vb = jnp.split(proj, ATTN_SPLITS, axis=-1)
    qa = rms_norm(qa.reshape(bsz, s_len, A_Q_HEADS, HEAD_DIM), a_qn)
    ka = rms_norm(ka.reshape(bsz, s_len, A_KV_HEADS, HEAD_DIM), a_kn)
    va = va.reshape(bsz, s_len, A_KV_HEADS, HEAD_DIM)
    ya = windowed_gqa(qa, ka, va, rel_bias[:, :A_Q_HEADS], a_sink)
    qb = rms_norm(qb.reshape(bsz, s_len, B_HEADS, 2, HEAD_DIM), b_qn)
    kb = rms_norm(kb.reshape(bsz, s_len, B_HEADS, 2, HEAD_DIM), b_kn)
    vb = vb.reshape(bsz, s_len, B_HEADS, 2 * HEAD_DIM)
    yb = differential_attention(qb, kb, vb, rel_bias[:, A_Q_HEADS:], b_lam, b_subln, layer)
    return jnp.concatenate([ya, yb], axis=-1) @ w_out


def conv_mixer(h, w_in, w_out, sc_w, dw_w, dw_b, ln_g, ln_b):
    proj = h @ w_in
    gb, gc, xc, conf = jnp.split(proj, [C_WIDTH, 2 * C_WIDTH, 3 * C_WIDTH], axis=-1)
    yc = gb * depthwise_conv(gc * xc, sc_w)
    a, gate = jnp.split(conf, 2, axis=-1)
    u = a * jax.nn.sigmoid(gate)
    u = depthwise_conv(u, dw_w) + dw_b
    u = jax.nn.silu(layer_norm(u, ln_g, ln_b))
    return jnp.concatenate([yc, u], axis=-1) @ w_out


def swiglu(h, w_gate, w_up, w_down):
    return (jax.nn.silu(h @ w_gate) * (h @ w_up)) @ w_down


def encoder_trunk(x, rel_bias, mix_norm, ffn_norm, w_gate, w_up, w_down,
                  attn_w_in, attn_w_out, a_q_norm, a_k_norm, a_sink,
                  b_q_norm, b_k_norm, b_lambda, b_subln,
                  conv_w_in, conv_w_out, short_conv_w, conf_dw_w, conf_dw_b, conf_ln_g, conf_ln_b):
    for l in range(DEPTH):
        i = l // 2
        h = rms_norm(x, mix_norm[l])
        if l % 2 == 0:
            mix = attention_mixer(h, attn_w_in[i], attn_w_out[i], a_q_norm[i], a_k_norm[i], a_sink[i],
                                  b_q_norm[i], b_k_norm[i], b_lambda[i], b_subln[i], rel_bias, l)
        else:
            mix = conv_mixer(h, conv_w_in[i], conv_w_out[i], short_conv_w[i], conf_dw_w[i],
                             conf_dw_b[i], conf_ln_g[i], conf_ln_b[i])
        x = x + mix
        x = x + swiglu(rms_norm(x, ffn_norm[l]), w_gate[l], w_up[l], w_down[l])
    return x


def setup_inputs(seed: int = 0) -> dict:
    key = jax.random.key(seed)
    ks = jax.random.split(key, 24)
    f32 = jnp.float32
    nrm = lambda k, shape, scale: jax.random.normal(k, shape, f32) * scale
    gain = lambda k, shape: 1.0 + 0.05 * jax.random.normal(k, shape, f32)
    return {
        "x_prompt": nrm(ks[0], (BATCH, SEQ, D_MODEL), 1.0),
        "x_sample": nrm(ks[1], (DEC_BATCH, DEC_SEQ, D_MODEL), 1.0),
        "rel_bias": nrm(ks[2], (NUM_BUCKETS, ATTN_HEADS), 0.5),
        "mix_norm": gain(ks[3], (DEPTH, D_MODEL)),
        "ffn_norm": gain(ks[4], (DEPTH, D_MODEL)),
        "w_gate": nrm(ks[5], (DEPTH, D_MODEL, FFN_HIDDEN), D_MODEL ** -0.5),
        "w_up": nrm(ks[6], (DEPTH, D_MODEL, FFN_HIDDEN), D_MODEL ** -0.5),
        "w_down": nrm(ks[7], (DEPTH, FFN_HIDDEN, D_MODEL), FFN_HIDDEN ** -0.5),
        "attn_w_in": nrm(ks[8], (N_EVEN, D_MODEL, ATTN_IN), D_MODEL ** -0.5),
        "attn_w_out": nrm(ks[9], (N_EVEN, ATTN_OUT, D_MODEL), ATTN_OUT ** -0.5),
        "a_q_norm": gain(ks[10], (N_EVEN, HEAD_DIM)),
        "a_k_norm": gain(ks[11], (N_EVEN, HEAD_DIM)),
        "a_sink": nrm(ks[12], (N_EVEN, A_Q_HEADS), 0.5),
        "b_q_norm": gain(ks[13], (N_EVEN, HEAD_DIM)),
        "b_k_norm": gain(ks[14], (N_EVEN, HEAD_DIM)),
        "b_lambda": nrm(ks[15], (N_EVEN, 4, HEAD_DIM), 0.1),
        "b_subln": gain(ks[16], (N_EVEN, 2 * HEAD_DIM)),
        "conv_w_in": nrm(ks[17], (N_ODD, D_MODEL, CONV_IN), D_MODEL ** -0.5),
        "conv_w_out": nrm(ks[18], (N_ODD, CONV_OUT, D_MODEL), CONV_OUT ** -0.5),
        "short_conv_w": nrm(ks[19], (N_ODD, SHORT_CONV, C_WIDTH), SHORT_CONV ** -0.5),
        "conf_dw_w": nrm(ks[20], (N_ODD, CONF_CONV, D_WIDTH), CONF_CONV ** -0.5),
        "conf_dw_b": nrm(ks[21], (N_ODD, D_WIDTH), 0.02),
        "conf_ln_g": gain(ks[22], (N_ODD, D_WIDTH)),
        "conf_ln_b": nrm(ks[23], (N_ODD, D_WIDTH), 0.02),
    }


def reference(x_prompt, x_sample, rel_bias, mix_norm, ffn_norm, w_gate, w_up, w_down,
              attn_w_in, attn_w_out, a_q_norm, a_k_norm, a_sink,
              b_q_norm, b_k_norm, b_lambda, b_subln,
              conv_w_in, conv_w_out, short_conv_w, conf_dw_w, conf_dw_b, conf_ln_g, conf_ln_b):
    y_prompt = encoder_trunk(x_prompt, rel_bias, mix_norm, ffn_norm, w_gate, w_up, w_down,
                             attn_w_in, attn_w_out, a_q_norm, a_k_norm, a_sink,
                             b_q_norm, b_k_norm, b_lambda, b_subln,
                             conv_w_in, conv_w_out, short_conv_w, conf_dw_w, conf_dw_b, conf_ln_g, conf_ln_b)
    y_sample = encoder_trunk(x_sample, rel_bias, mix_norm, ffn_norm, w_gate, w_up, w_down,
                             attn_w_in, attn_w_out, a_q_norm, a_k_norm, a_sink,
                             b_q_norm, b_k_norm, b_lambda, b_subln,
                             conv_w_in, conv_w_out, short_conv_w, conf_dw_w, conf_dw_b, conf_ln_g, conf_ln_b)
    return (y_prompt, y_sample)
```

```python
import contextlib
import numpy as np
import ml_dtypes
import concourse.bass as bass
import concourse.mybir as mybir
from concourse.bass_utils import run_bass_kernel_spmd
from concourse.alu_op_type import AluOpType as ALU

F32, BF16 = mybir.dt.float32, mybir.dt.bfloat16
AF = mybir.ActivationFunctionType
AX = mybir.AxisListType

D = 1024
FF = 2816
KC = 8
FC = 22
TB = 512
EPS = 1e-6
ATTN_IN = 2304
CONV_IN = 2560
N_CORES = 8
SEQS_FULL = (8192, 2048, 2048, 2048, 2048)

ENGS = ("pe", "act", "dve", "pool", "sp")
ENG_ATTR = {"pe": "tensor", "act": "scalar", "dve": "vector", "pool": "gpsimd", "sp": "sync"}


class Op:
    __slots__ = ("id", "eng", "fn", "deps", "dma_key", "dma_val", "needs_inc", "seq")

    def __init__(self, id, eng, fn):
        self.id = id
        self.eng = eng
        self.fn = fn
        self.deps = set()
        self.dma_key = None
        self.dma_val = 0
        self.needs_inc = False
        self.seq = 0


class Sched:
    def __init__(self, nc, stack, same_engine_sync=True):
        self.nc = nc
        self.stack = stack
        self.same_engine_sync = same_engine_sync
        self.eng_sem = {e: stack.enter_context(nc.semaphore("sem_" + e)) for e in ENGS}
        self.eng_cnt = {e: 0 for e in ENGS}
        self.dma_sem = {}
        self.dma_cnt = {}
        self.waited = {e: {} for e in ENGS}
        self.stats = []
        self._reset_phase()

    def _reset_phase(self):
        self.ops = []
        self.last_w = {}
        self.readers = {}

    def op(self, eng, fn, r=(), w=(), dma_key=None):
        o = Op(len(self.ops), eng, fn)
        deps = o.deps
        for k in r:
            p = self.last_w.get(k)
            if p is not None:
                deps.add(p)
            if type(k) is tuple and k[0] == "pb":
                for q in self.readers.get(k, ()):
                    if self.ops[q].eng != eng:
                        deps.add(q)
        for k in w:
            p = self.last_w.get(k)
            if p is not None:
                deps.add(p)
            rd = self.readers.get(k)
            if rd:
                deps.update(rd)
        for k in r:
            self.readers.setdefault(k, []).append(o.id)
        for k in w:
            self.last_w[k] = o.id
            self.readers[k] = []
        deps.discard(o.id)
        if dma_key is not None:
            if dma_key not in self.dma_sem:
                self.dma_sem[dma_key] = self.stack.enter_context(
                    self.nc.semaphore("dsem%d" % len(self.dma_sem)))
                self.dma_cnt[dma_key] = 0
            self.dma_cnt[dma_key] += 1
            o.dma_key = dma_key
            o.dma_val = 16 * self.dma_cnt[dma_key]
        self.ops.append(o)
        return o

    def dma(self, eng, out, in_, r, w, key):
        return self.op(eng, lambda e: e.dma_start(out=out, in_=in_), r=r, w=w, dma_key=key)

    def emit(self, name=""):
        ops = self.ops
        for o in ops:
            for d in o.deps:
                p = ops[d]
                if p.dma_key is None and (p.eng != o.eng or (self.same_engine_sync and p.eng != "pe")):
                    p.needs_inc = True
        per_eng = {e: [] for e in ENGS}
        for o in ops:
            per_eng[o.eng].append(o)
        for e in ENGS:
            c = self.eng_cnt[e]
            for o in per_eng[e]:
                if o.dma_key is None and o.needs_inc:
                    c += 1
                    o.seq = c
            self.eng_cnt[e] = c
        nw = [0]
        with self.nc.Block() as block:
            for e in ENGS:
                lst = per_eng[e]
                if not lst and e != "sp":
                    continue

                def body(eng, e=e, lst=lst):
                    waited = self.waited[e]
                    for o in lst:
                        need = {}
                        for d in o.deps:
                            p = ops[d]
                            if p.dma_key is not None:
                                nm = ("d", p.dma_key)
                                sem = self.dma_sem[p.dma_key]
                                val = p.dma_val
                            else:
                                if p.eng == e and (e == "pe" or not self.same_engine_sync):
                                    continue
                                nm = ("e", p.eng)
                                sem = self.eng_sem[p.eng]
                                val = p.seq
                            if val > need.get(nm, (None, 0))[1]:
                                need[nm] = (sem, val)
                        for nm, (sem, val) in need.items():
                            if waited.get(nm, 0) >= val:
                                continue
                            eng.wait_ge(sem, val)
                            waited[nm] = val
                            nw[0] += 1
                        ins = o.fn(eng)
                        if o.dma_key is not None:
                            ins.then_inc(self.dma_sem[o.dma_key], 16)
                        elif o.needs_inc:
                            ins.then_inc(self.eng_sem[e], 1)
                    if e == "sp":
                        for key, sem in self.dma_sem.items():
                            val = 16 * self.dma_cnt[key]
                            if val and waited.get(("d", key), 0) < val:
                                eng.wait_ge(sem, val)
                                waited[("d", key)] = val
                        for e2 in ENGS:
                            if e2 != "sp" and self.eng_cnt[e2] and waited.get(("e", e2), 0) < self.eng_cnt[e2]:
                                eng.wait_ge(self.eng_sem[e2], self.eng_cnt[e2])
                                waited[("e", e2)] = self.eng_cnt[e2]

                getattr(block, ENG_ATTR[e])(body)
        self.stats.append((name, len(ops), nw[0], {e: len(per_eng[e]) for e in ENGS}))
        self._reset_phase()


def _bucket_table():
    import math
    import jax
    import jax.numpy as jnp
    with jax.default_device(jax.devices("cpu")[0]):
        rel = jnp.arange(-255, 256, dtype=jnp.int32)
        half = 16
        max_exact = 8
        n = jnp.abs(rel)
        large = max_exact + (jnp.log(jnp.maximum(n, 1).astype(jnp.float32) / max_exact)
                             / math.log(128 / max_exact) * (half - max_exact)).astype(jnp.int32)
        large = jnp.minimum(large, half - 1)
        b = jnp.where(rel > 0, half, 0) + jnp.where(n < max_exact, n, large)
        return np.asarray(b)


COL_SPEC = [("mixg", 16), ("ffng", 16), ("aq", 1), ("ak", 1), ("bq", 1), ("bk", 1), ("eps", 1),
            ("cfar", 24), ("sink", 8), ("lam", 256), ("subln", 128), ("scw", 12), ("dww", 124),
            ("dwb", 4), ("lng", 4), ("lnb", 4)]
COL_OFF = {}
_o = 0
for _n, _w in COL_SPEC:
    COL_OFF[_n] = _o
    _o += _w
NCOL = _o


def _pack_cols(inp):
    c = np.zeros((128, NCOL), np.float32)

    def put(name, arr):
        arr = np.asarray(arr, np.float32)
        c[:, COL_OFF[name]:COL_OFF[name] + arr.shape[1]] = arr

    fm = lambda v, nch: np.asarray(v, np.float32).reshape(nch, 128).T
    put("mixg", np.concatenate([fm(inp["mix_norm"][l], 8) for l in range(2)], axis=1))
    put("ffng", np.concatenate([fm(inp["ffn_norm"][l], 8) for l in range(2)], axis=1))
    for nm, key in (("aq", "a_q_norm"), ("ak", "a_k_norm"), ("bq", "b_q_norm"), ("bk", "b_k_norm")):
        v = np.asarray(inp[key], np.float32).reshape(64)
        put(nm, np.concatenate([v, v])[:, None])
    put("eps", np.full((128, 1), EPS, np.float32))
    rb = np.asarray(inp["rel_bias"], np.float32)
    cf = np.stack([rb[15, :], rb[31, :]], axis=1).reshape(1, 24)
    put("cfar", np.broadcast_to(cf, (128, 24)))
    put("sink", np.broadcast_to(np.asarray(inp["a_sink"], np.float32).reshape(1, 8), (128, 8)))
    put("lam", np.broadcast_to(np.asarray(inp["b_lambda"], np.float32).reshape(1, 256), (128, 256)))
    put("subln", np.broadcast_to(np.asarray(inp["b_subln"], np.float32).reshape(1, 128), (128, 128)))
    scw = np.asarray(inp["short_conv_w"], np.float32).reshape(3, 4, 128)
    put("scw", scw.transpose(2, 1, 0).reshape(128, 12))
    dww = np.asarray(inp["conf_dw_w"], np.float32).reshape(31, 4, 128)
    put("dww", dww.transpose(2, 1, 0).reshape(128, 124))
    put("dwb", fm(np.asarray(inp["conf_dw_b"]).reshape(512), 4))
    put("lng", fm(np.asarray(inp["conf_ln_g"]).reshape(512), 4))
    put("lnb", fm(np.asarray(inp["conf_ln_b"]).reshape(512), 4))
    return c


def _host_consts(inp):
    bt = _bucket_table()
    k = np.arange(128)[:, None]
    q = np.arange(128)[None, :]
    rb = np.asarray(inp["rel_bias"], np.float32)
    tb = np.zeros((128, 12, 3, 128), np.float32)
    mk = np.zeros((128, 3, 128), np.float32)
    for oi, o in enumerate((-1, 0, 1)):
        rel = 128 * o + k - q
        idx = bt[rel + 255]
        tb[:, :, oi, :] = rb[idx].transpose(0, 2, 1)
        mk[:, oi, :] = np.where(np.abs(rel) <= 128, 0.0, -1e30)
    return {"cols": _pack_cols(inp), "tbias": tb, "maskA": mk,
            "ident": np.eye(128, dtype=np.float32)}


def _attn_groups():
    g = []
    for i in range(4):
        g += [i, i + 4]
    g += [8, 9]
    g += [10, 11]
    for h in range(4):
        g += [12 + 2 * h, 13 + 2 * h]
    for h in range(4):
        g += [20 + 2 * h, 21 + 2 * h]
    for h in range(4):
        g += [28 + 2 * h, 29 + 2 * h]
    return g


QK_CHUNKS = [0, 1, 2, 3, 4, 6, 7, 8, 9, 10, 11, 12, 13]
QK_GAIN = ["aq"] * 4 + ["ak"] + ["bq"] * 4 + ["bk"] * 4


class KB:
    def __init__(self, seqs, dbg=False, phases=("p0", "p1", "p2a", "p2b", "p3")):
        self.seqs = list(seqs)
        self.ntok = sum(seqs)
        self.dbg = dbg
        self.phases = phases
        self.nc = bass.Bass("TRN2", target_bir_lowering=False)

    def din(self, name, shape, dt=F32):
        return self.nc.dram_tensor(name, list(shape), dt, kind="ExternalInput").ap()

    def dscr(self, name, shape, dt):
        kind = "ExternalOutput" if self.dbg else "Internal"
        return self.nc.dram_tensor(name, list(shape), dt, kind=kind).ap()

    def build(self):
        nc = self.nc
        NT = self.ntok
        self.x = self.din("x", [NT, D])
        self.w_gate = self.din("w_gate", [2, D, FF])
        self.w_up = self.din("w_up", [2, D, FF])
        self.w_down = self.din("w_down", [2, FF, D])
        self.attn_w_in = self.din("attn_w_in", [D, ATTN_IN])
        self.attn_w_out = self.din("attn_w_out", [D, D])
        self.conv_w_in = self.din("conv_w_in", [D, CONV_IN])
        self.conv_w_out = self.din("conv_w_out", [D, D])
        self.d_cols = self.din("cols", [128, NCOL])
        self.d_tbias = self.din("tbias", [128, 12, 3, 128])
        self.d_maskA = self.din("maskA", [128, 3, 128])
        self.d_ident = self.din("ident", [128, 128])
        self.y = nc.dram_tensor("y", [NT, D], F32, kind="ExternalOutput").ap()
        self.W = {
            "ain": self.dscr("wb_ain", [18, 128, KC, 128], BF16),
            "aout": self.dscr("wb_aout", [8, 128, KC, 128], BF16),
            "cin": self.dscr("wb_cin", [20, 128, KC, 128], BF16),
            "cout": self.dscr("wb_cout", [8, 128, KC, 128], BF16),
        }
        for l in range(2):
            self.W["g%d" % l] = self.dscr("wb_g%d" % l, [FC, 128, KC, 128], BF16)
            self.W["u%d" % l] = self.dscr("wb_u%d" % l, [FC, 128, KC, 128], BF16)
            self.W["d%d" % l] = self.dscr("wb_d%d" % l, [8, 128, FC, 128], BF16)
        self.QKT = self.dscr("qkt", [13, 128, NT], BF16)
        self.VB = self.dscr("vbs", [NT, 4, 129], BF16)
        self.VA = self.dscr("vas", [NT, 2, 65], BF16)
        self.YT = self.dscr("yt", [8, 128, NT], BF16)
        self.X2T = self.dscr("x2t", [8, 128, NT], F32)

        with contextlib.ExitStack() as st:
            self.st = st
            self.S = Sched(nc, st)
            sb = lambda n, sh, dt: st.enter_context(nc.sbuf_tensor("s_" + n, list(sh), dt))
            self.pb = [st.enter_context(nc.psum_tensor("pb%d" % i, [128, 512], F32)) for i in range(8)]
            self.ident = sb("ident", [128, 128], F32)
            self.identb = sb("identb", [128, 128], BF16)
            self.ones = sb("ones", [128, 128], BF16)
            self.bones = sb("bones", [128, 128], BF16)
            self.zer = sb("zer", [128, 128], BF16)
            self.anyr = sb("anyr", [128, 512], BF16)
            self.cols = sb("cols", [128, NCOL], F32)
            self.neglam = sb("neglam", [128, 1], F32)
            self.subln08 = sb("subln08", [128, 128], F32)
            self.esink = sb("esink", [128, 8], F32)
            if "p0" in self.phases:
                self.phase0()
            if "p1" in self.phases:
                self.phase1()
            if "p2a" in self.phases:
                self.phase2a()
            if "p2b" in self.phases:
                self.phase2b()
            if "p3" in self.phases:
                self.phase3()
        return nc

    def col(self, name, i=0, n=1):
        o = COL_OFF[name] + i
        return self.cols[:, o:o + n]

    def phase0(self, weights=True):
        nc, S = self.nc, self.S
        with contextlib.ExitStack() as ps:
            sb = lambda n, sh, dt: ps.enter_context(nc.sbuf_tensor("s_" + n, list(sh), dt))
            S.dma("sp", self.ident[:], self.d_ident, r=[], w=["ident"], key="ident")
            S.dma("sp", self.cols[:], self.d_cols, r=[], w=["cols"], key="cols")
            S.op("dve", lambda e: e.tensor_copy(self.identb[:], self.ident[:]), r=["ident"], w=["identb"])
            S.op("pool", lambda e: e.memset(self.ones[:], 1.0), w=["ones"])
            S.op("pool", lambda e: e.memset(self.zer[:], 0.0), w=["zer"])
            S.op("pool", lambda e: e.memset(self.anyr[:], 1.0), w=["anyr"])
            S.op("pool", lambda e: e.memset(self.bones[:], 0.0), w=["bones"])
            S.op("pool", lambda e: e.memset(self.bones[0:64, 0:64], 1.0), w=["bones"])
            S.op("pool", lambda e: e.memset(self.bones[64:128, 64:128], 1.0), w=["bones"])
            lt = sb("p0_lt", [128, 2, 64], F32)
            ls = sb("p0_ls", [128, 2], F32)
            le = sb("p0_le", [128, 2], F32)
            lam = self.col("lam", 0, 256)
            for i in range(2):
                S.op("dve", lambda e, i=i: e.tensor_tensor(out=lt[:, i, :], in0=lam[:, (2 * i) * 64:(2 * i + 1) * 64],
                                                          in1=lam[:, (2 * i + 1) * 64:(2 * i + 2) * 64], op=ALU.mult),
                     r=["cols"], w=[("lt", i)])
                S.op("dve", lambda e, i=i: e.reduce_sum(out=ls[:, i:i + 1], in_=lt[:, i, :], axis=AX.X),
                     r=[("lt", i)], w=[("ls", i)])
            S.op("act", lambda e: e.activation(le[:], ls[:], AF.Exp), r=[("ls", 0), ("ls", 1)], w=["le"])
            S.op("dve", lambda e: e.scalar_tensor_tensor(out=self.neglam[:], in0=le[:, 1:2], scalar=-0.2, in1=le[:, 0:1],
                                                         op0=ALU.add, op1=ALU.subtract),
                 r=["le"], w=["neglam"])
            S.op("dve", lambda e: e.tensor_scalar(out=self.subln08[:], in0=self.col("subln", 0, 128), scalar1=0.8, scalar2=None,
                                                  op0=ALU.mult),
                 r=["cols"], w=["subln08"])
            S.op("act", lambda e: e.activation(self.esink[:], self.col("sink", 0, 8), AF.Exp), r=["cols"], w=["esink"])

            if weights:
                NSL = 3
                stf = [sb("p0_stf%d" % i, [128, 4096], F32) for i in range(NSL)]
                stb = [sb("p0_stb%d" % i, [128, 4096], BF16) for i in range(NSL)]
                cnt = [0]

                def convert(src2d, dst, kc, groups=None):
                    nch = dst.shape[0]
                    G = 4 if kc == 8 else 1
                    for j0 in range(0, nch, G):
                        g = min(G, nch - j0)
                        i = cnt[0] % NSL
                        cnt[0] += 1
                        n = g * kc * 128
                        f4 = stf[i][:, 0:n].rearrange("p (g k n) -> p g k n", g=g, k=kc)
                        if groups is None:
                            for gi in range(g):
                                j = j0 + gi
                                src = src2d[:, j * 128:(j + 1) * 128].rearrange("(k p) n -> p k n", p=128)
                                S.dma("sp", f4[:, gi], src, r=[], w=[("stf", i, gi, 0), ("stf", i, gi, 1)], key=("stf", i, gi, 0))
                        else:
                            for gi in range(g):
                                for hf in range(2):
                                    gg = groups[2 * (j0 + gi) + hf]
                                    src = src2d[:, gg * 64:(gg + 1) * 64].rearrange("(k p) n -> p k n", p=128)
                                    S.dma("sp", f4[:, gi, :, hf * 64:(hf + 1) * 64], src, r=[],
                                          w=[("stf", i, gi, hf)], key=("stf", i, gi, hf))
                        rk = [("stf", i, gi, hf) for gi in range(g) for hf in range(2)]
                        eng = "act" if cnt[0] % 2 else "dve"
                        if eng == "act":
                            S.op("act", lambda e, i=i, n=n: e.copy(stb[i][:, 0:n], stf[i][:, 0:n]), r=rk, w=[("stb", i)])
                        else:
                            S.op("dve", lambda e, i=i, n=n: e.tensor_copy(stb[i][:, 0:n], stf[i][:, 0:n]), r=rk, w=[("stb", i)])
                        dd = dst[j0:j0 + g].rearrange("g p k n -> p g (k n)")
                        S.dma("pool", dd, stb[i][:, 0:n].rearrange("p (g m) -> p g m", g=g), r=[("stb", i)],
                              w=[("wscr", cnt[0])], key=("stb_st", i))

                convert(self.attn_w_in, self.W["ain"], 8, groups=_attn_groups())
                convert(self.attn_w_out, self.W["aout"], 8)
                convert(self.conv_w_in, self.W["cin"], 8)
                convert(self.conv_w_out, self.W["cout"], 8)
                for l in range(2):
                    convert(self.w_gate[l], self.W["g%d" % l], 8)
                    convert(self.w_up[l], self.W["u%d" % l], 8)
                    convert(self.w_down[l], self.W["d%d" % l], FC)
            S.emit("p0")

    def blocks(self):
        out = []
        s0 = 0
        for sl in self.seqs:
            for b in range(sl // TB):
                out.append((s0, sl, b, s0 + b * TB))
            s0 += sl
        return out

    def load_xT(self, t0, xtok, xtk, xT, xTk, tb=(0, 1)):
        S = self.S
        src = self.x[t0:t0 + TB].rearrange("(j p) f -> p j f", p=128)
        S.dma("sp", xtok[:], src, r=[], w=[xtk], key=xtk)
        for c in range(KC):
            bi = tb[c % 2]
            bank = self.pb[bi]
            for j in range(4):
                S.op("pe", lambda e, bank=bank, j=j, c=c: e.transpose(bank[:, j * 128:(j + 1) * 128],
                                                                     xtok[:, j, c * 128:(c + 1) * 128], self.ident[:]),
                     r=[xtk, "ident"], w=[("pb", bi)])
            S.op("act", lambda e, bank=bank, c=c: e.copy(xT[:, c, :], bank[:]), r=[("pb", bi)], w=[(xTk, c)])

    def rmsnorm(self, xT, xTk, hT, hTk, gname, gidx, N, sqb, ssb, tmp, tmpk):
        S = self.S
        ss = self.pb[ssb]
        lnt, rstd = tmp
        for c in range(KC):
            sq = sqb[c % 2]
            sqk = (tmpk, "sq", c % 2)
            S.op("act", lambda e, sq=sq, c=c: e.activation(sq[:, 0:N], xT[:, c, 0:N], AF.Square), r=[(xTk, c)], w=[sqk])
            S.op("pe", lambda e, sq=sq, c=c: e.matmul(ss[:, 0:N], self.ones[:], sq[:, 0:N], start=(c == 0), stop=(c == KC - 1)),
                 r=[sqk, "ones"], w=[("pb", ssb)])
        S.op("act", lambda e: e.activation(lnt[:, 0:N], ss[:, 0:N], AF.Ln, bias=self.col("eps"), scale=1.0 / D),
             r=[("pb", ssb), "cols"], w=[(tmpk, "lnt")])
        S.op("act", lambda e: e.activation(rstd[:, 0:N], lnt[:, 0:N], AF.Exp, scale=-0.5), r=[(tmpk, "lnt")], w=[(tmpk, "rstd")])
        for c in range(KC):
            S.op("dve", lambda e, c=c: e.scalar_tensor_tensor(out=hT[:, c, 0:N], in0=xT[:, c, 0:N],
                                                              scalar=self.col(gname, gidx + c), in1=rstd[:, 0:N],
                                                              op0=ALU.mult, op1=ALU.mult),
                 r=[(xTk, c), (tmpk, "rstd"), "cols"], w=[(hTk, c)])

    def phase1(self):
        nc, S = self.nc, self.S
        with contextlib.ExitStack() as ps:
            sb = lambda n, sh, dt: ps.enter_context(nc.sbuf_tensor("s_" + n, list(sh), dt))
            xtok = [sb("p1_xtok%d" % i, [128, 4, D], F32) for i in range(2)]
            xT = sb("p1_xT", [128, KC, TB], F32)
            hT = [sb("p1_hT%d" % i, [128, KC, TB], BF16) for i in range(2)]
            sqb = [sb("p1_sq%d" % i, [128, TB], BF16) for i in range(2)]
            lnt = sb("p1_lnt", [128, TB], F32)
            rstd = sb("p1_rstd", [128, TB], F32)
            win = sb("p1_win", [128, 18, KC, 128], BF16)
            sqq = [sb("p1_sqq%d" % i, [128, TB], BF16) for i in range(2)]
            lq = [sb("p1_lq%d" % i, [128, TB], F32) for i in range(2)]
            rq = [sb("p1_rq%d" % i, [128, TB], F32) for i in range(2)]
            qst = [sb("p1_qst%d" % i, [128, TB], BF16) for i in range(3)]
            vast = [sb("p1_vast%d" % i, [128, 4, 2, 65], BF16) for i in range(2)]
            vbst = [sb("p1_vbst%d" % i, [128, 4, 4, 129], BF16) for i in range(2)]
            for j0 in range(0, 18, 6):
                S.dma("sp", win[:, j0:j0 + 6].rearrange("p j k n -> p j (k n)"),
                      self.W["ain"][j0:j0 + 6].rearrange("j p k n -> p j (k n)"), r=[], w=[("win", j0)], key=("win", j0))
            wink = [("win", 0), ("win", 6), ("win", 12)]
            for i in range(2):
                S.op("pool", lambda e, i=i: e.memset(vast[i][:], 1.0), w=[("vast", i)])
                S.op("pool", lambda e, i=i: e.memset(vbst[i][:], 1.0), w=[("vbst", i)])
            T0, T1, SSB, Q0, Q1, PSB, VAB, VBB = range(8)
            def blk(bi, s0, sl, b, t0):
                hk = ("hT", bi % 2)
                h = hT[bi % 2]
                self.load_xT(t0, xtok[bi % 2], ("xtok", bi % 2), xT, "xT", tb=(T0, T1))
                self.rmsnorm(xT, "xT", h, hk, "mixg", 0, TB, sqb, SSB, (lnt, rstd), "p1n")
                hkeys = [(hk, c) for c in range(KC)]
                for ci, j in enumerate(QK_CHUNKS):
                    qi = (Q0, Q1)[ci % 2]
                    qbank = self.pb[qi]
                    for kc in range(KC):
                        S.op("pe", lambda e, qbank=qbank, j=j, kc=kc: e.matmul(qbank[:], win[:, j, kc, :], h[:, kc, :],
                                                                               start=(kc == 0), stop=(kc == KC - 1)),
                             r=[(hk, kc)] + wink, w=[("pb", qi)])
                    s2 = ci % 2
                    S.op("act", lambda e, qbank=qbank, s2=s2: e.activation(sqq[s2][:], qbank[:], AF.Square),
                         r=[("pb", qi)], w=[("sqq", s2)])
                    S.op("pe", lambda e, s2=s2: e.matmul(self.pb[PSB][:], self.bones[:], sqq[s2][:], start=True, stop=True),
                         r=[("sqq", s2), "bones"], w=[("pb", PSB)])
                    S.op("act", lambda e, s2=s2: e.activation(lq[s2][:], self.pb[PSB][:], AF.Ln, bias=self.col("eps"), scale=1.0 / 64),
                         r=[("pb", PSB), "cols"], w=[("lq", s2)])
                    S.op("act", lambda e, s2=s2: e.activation(rq[s2][:], lq[s2][:], AF.Exp, scale=-0.5),
                         r=[("lq", s2)], w=[("rq", s2)])
                    s3 = ci % 3
                    gn = QK_GAIN[ci]
                    S.op("dve", lambda e, qbank=qbank, s2=s2, s3=s3, gn=gn: e.scalar_tensor_tensor(
                        out=qst[s3][:], in0=qbank[:], scalar=self.col(gn), in1=rq[s2][:], op0=ALU.mult, op1=ALU.mult),
                        r=[("pb", qi), ("rq", s2), "cols"], w=[("qst", s3)])
                    S.dma("pool", self.QKT[ci, :, t0:t0 + TB], qst[s3][:], r=[("qst", s3)], w=[("QKT", ci, bi)],
                          key=("qst_st", s3))
                vs = bi % 2
                for jt in range(4):
                    for kc in range(KC):
                        S.op("pe", lambda e, jt=jt, kc=kc: e.matmul(self.pb[VAB][:, jt * 128:(jt + 1) * 128],
                                                                    h[:, kc, jt * 128:(jt + 1) * 128], win[:, 5, kc, :],
                                                                    start=(kc == 0), stop=(kc == KC - 1)),
                             r=[(hk, kc)] + wink, w=[("pb", VAB)])
                    for kc in range(KC):
                        S.op("pe", lambda e, jt=jt, kc=kc: e.matmul(self.pb[VBB][:].rearrange("p (h n) -> p h n", h=4),
                                                                    h[:, kc, jt * 128:(jt + 1) * 128], win[:, 14:18, kc, :],
                                                                    start=(kc == 0), stop=(kc == KC - 1)),
                             r=[(hk, kc)] + wink, w=[("pb", VBB)])
                    S.op("dve" if jt % 2 else "act",
                         (lambda e, jt=jt, vs=vs: e.tensor_copy(vbst[vs][:, jt, :, 0:128], self.pb[VBB][:].rearrange("p (h n) -> p h n", h=4)))
                         if jt % 2 else
                         (lambda e, jt=jt, vs=vs: e.copy(vbst[vs][:, jt, :, 0:128], self.pb[VBB][:].rearrange("p (h n) -> p h n", h=4))),
                         r=[("pb", VBB)], w=[("vbst", vs)])
                S.op("dve", lambda e, vs=vs: e.tensor_copy(vast[vs][:, :, :, 0:64],
                                                           self.pb[VAB][:].rearrange("p (j g n) -> p j g n", j=4, g=2)),
                     r=[("pb", VAB)], w=[("vast", vs)])
                S.dma("pool", self.VB[t0:t0 + TB].rearrange("(j p) h e -> p j (h e)", p=128),
                      vbst[vs][:].rearrange("p j h e -> p j (h e)"), r=[("vbst", vs)], w=[("VB", bi)], key=("vbst_st", vs))
                S.dma("pool", self.VA[t0:t0 + TB].rearrange("(j p) g e -> p j (g e)", p=128),
                      vast[vs][:].rearrange("p j g e -> p j (g e)"), r=[("vast", vs)], w=[("VA", bi)], key=("vast_st", vs))

            for bi, (s0, sl, b, t0) in enumerate(self.blocks()):
                blk(bi, s0, sl, b, t0)
            S.emit("p1")

    def phase2a(self):
        nc, S = self.nc, self.S
        SMAX = max(self.seqs)
        NKT = SMAX // 128
        with contextlib.ExitStack() as ps:
            sb = lambda n, sh, dt: ps.enter_context(nc.sbuf_tensor("s_" + n, list(sh), dt))
            tbB = sb("p2_tbB", [128, 4, 3, 128], F32)
            TA = sb("p2_TA", [128, 2, 3, 4, 128], F32)
            mk = sb("p2_mk", [128, 3, 128], F32)
            kbuf = [sb("p2_k%d" % i, [128, SMAX], BF16) for i in range(2)]
            vbuf = [sb("p2_v%d" % i, [128, NKT * 130], BF16) for i in range(2)]
            qbuf = [sb("p2_q%d" % i, [128, 4 * TB], BF16) for i in range(3)]
            Eb = [sb("p2_E%d" % i, [128, TB], BF16) for i in range(4)]
            tmp = [sb("p2_tmp%d" % i, [128, TB], F32) for i in range(3)]
            osb = [sb("p2_osb%d" % i, [128, 2, 4, 129], F32) for i in range(2)]
            rr = [sb("p2_rr%d" % i, [128, 4], F32) for i in range(2)]
            yf = [sb("p2_yf%d" % i, [128, 128], F32) for i in range(2)]
            yf2 = [sb("p2_yg%d" % i, [128, 128], F32) for i in range(2)]
            junk = [sb("p2_jk%d" % i, [128, 128], F32) for i in range(2)]
            ssq = [sb("p2_ssq%d" % i, [128, 1], F32) for i in range(2)]
            ltb = [sb("p2_lt%d" % i, [128, 1], F32) for i in range(2)]
            rsb = [sb("p2_rs%d" % i, [128, 1], F32) for i in range(2)]
            ynb = [sb("p2_ynb%d" % i, [128, 128], BF16) for i in range(2)]
            yst = [sb("p2_yst%d" % i, [128, TB], BF16) for i in range(2)]
            ystA = [sb("p2_ystA%d" % i, [128, 4, TB], BF16) for i in range(2)]
            ya = [sb("p2_ya%d" % i, [128, 8, 64], BF16) for i in range(2)]
            den = [sb("p2_den%d" % i, [128, 8], F32) for i in range(2)]
            rden = [sb("p2_rden%d" % i, [128, 8], F32) for i in range(2)]
            pbt7 = self.pb[7][:].bitcast(BF16)
            ctr = {"q": 0, "E": 0, "tmp": 0, "osb": 0, "fs": 0, "yst": 0, "ystA": 0, "sa": 0}

            def nxt(name, n):
                v = ctr[name] % n
                ctr[name] += 1
                return v

            S.dma("sp", tbB[:], self.d_tbias[:, 8:12], r=[], w=["tbB"], key="tbB")
            for g in range(2):
                for o in range(3):
                    S.dma("sp", TA[:, g, o], self.d_tbias[:, 4 * g:4 * g + 4, o, :], r=[],
                          w=[("TA", g)], key=("TA", g, o))
            S.dma("sp", mk[:], self.d_maskA, r=[], w=["mk"], key="mk")
            for g in range(2):
                for o in range(3):
                    for i in range(4):
                        S.op("dve", lambda e, g=g, o=o, i=i: e.tensor_tensor(out=TA[:, g, o, i, :], in0=TA[:, g, o, i, :],
                                                                             in1=mk[:, o, :], op=ALU.add),
                             r=[("TA", g), "mk"], w=[("TA", g)])

            def zero_banks(banks):
                for b in banks:
                    S.op("pe", lambda e, b=b: e.matmul(self.pb[b][:], self.zer[:], self.anyr[:], start=True, stop=True,
                                                       skip_group_check=True),
                         r=["zer", "anyr"], w=[("pb", b)])

            def a_qtile(ks, qs, ysA, jq, t, nkt):
                zero_banks((4, 5))
                q4 = qbuf[qs][:].rearrange("p (i t) -> p i t", i=4)
                va = vbuf[ks][:, 0:nkt * 130].rearrange("p (k e) -> p k e", e=130)
                for g in range(2):
                    lastkt = max(kt for kt in (t - 1, t, t + 1) if 0 <= kt < nkt)
                    for o in (-1, 0, 1):
                        kt = t + o
                        if kt < 0 or kt >= nkt:
                            continue
                        sbk = nxt("sa", 4)
                        S.op("pe", lambda e, sbk=sbk, g=g, kt=kt: e.matmul(
                            self.pb[sbk][:].rearrange("p (i q) -> p i q", i=4),
                            kbuf[ks][64 * g:64 * g + 64, kt * 128:(kt + 1) * 128],
                            q4[64 * g:64 * g + 64, :, jq * 128:(jq + 1) * 128], start=True, stop=True),
                            r=[("kbuf", ks), ("qbuf", qs)], w=[("pb", sbk)])
                        ts = nxt("tmp", 3)
                        S.op("dve", lambda e, sbk=sbk, ts=ts, g=g, o=o: e.scalar_tensor_tensor(
                            out=tmp[ts][:], in0=self.pb[sbk][:], scalar=0.125,
                            in1=TA[:, g, o + 1].rearrange("p i q -> p (i q)"), op0=ALU.mult, op1=ALU.add),
                            r=[("pb", sbk), ("TA", g)], w=[("tmp", ts)])
                        es = nxt("E", 4)
                        S.op("act", lambda e, ts=ts, es=es: e.activation(Eb[es][:], tmp[ts][:], AF.Exp),
                             r=[("tmp", ts)], w=[("E", es)])
                        for i in range(4):
                            S.op("pe", lambda e, es=es, g=g, i=i, kt=kt, lastkt=lastkt: e.matmul(
                                self.pb[4 + g][:, i * 65:(i + 1) * 65], Eb[es][:, i * 128:(i + 1) * 128],
                                va[:, kt, g * 65:(g + 1) * 65], start=False, stop=(kt == lastkt), skip_group_check=True),
                                r=[("E", es), ("vbuf", ks)], w=[("pb", 4 + g)])
                fs = nxt("fs", 2)
                for g in range(2):
                    S.op("dve", lambda e, g=g, fs=fs: e.tensor_tensor(
                        out=den[fs][:, 4 * g:4 * g + 4],
                        in0=self.pb[4 + g][:, 0:260].rearrange("p (i e) -> p i e", e=65)[:, :, 64],
                        in1=self.esink[:, 4 * g:4 * g + 4], op=ALU.add),
                        r=[("pb", 4 + g), "esink"], w=[("den", fs, g)])
                S.op("dve", lambda e, fs=fs: e.reciprocal(rden[fs][:], den[fs][:]),
                     r=[("den", fs, 0), ("den", fs, 1)], w=[("rden", fs)])
                for g in range(2):
                    for i in range(4):
                        hq = 4 * g + i
                        if i % 2:
                            S.op("act", lambda e, g=g, i=i, hq=hq, fs=fs: e.activation(
                                ya[fs][:, hq, :], self.pb[4 + g][:, i * 65:i * 65 + 64], AF.Copy, scale=rden[fs][:, hq:hq + 1]),
                                r=[("pb", 4 + g), ("rden", fs)], w=[("ya", fs, hq)])
                        else:
                            S.op("dve", lambda e, g=g, i=i, hq=hq, fs=fs: e.tensor_scalar(
                                out=ya[fs][:, hq, :], in0=self.pb[4 + g][:, i * 65:i * 65 + 64],
                                scalar1=rden[fs][:, hq:hq + 1], scalar2=None, op0=ALU.mult),
                                r=[("pb", 4 + g), ("rden", fs)], w=[("ya", fs, hq)])
                yav = ya[fs][:].rearrange("p h e -> p (h e)")
                for m in range(4):
                    S.op("pe", lambda e, m=m: e.transpose(pbt7[:, m * 128:(m + 1) * 128], yav[:, m * 128:(m + 1) * 128],
                                                          self.identb[:]),
                         r=[("ya", fs, 2 * m), ("ya", fs, 2 * m + 1), "identb"], w=[("pb", 7)])
                S.op("act", lambda e: e.copy(ystA[ysA][:, :, jq * 128:(jq + 1) * 128],
                                             pbt7[:, 0:512].rearrange("p (m q) -> p m q", m=4)),
                     r=[("pb", 7)], w=[("ystA", ysA)])

            def a_pass(pi, s0, sl):
                ks = pi % 2
                nkt = sl // 128
                S.dma("sp", kbuf[ks][:, 0:sl], self.QKT[4, :, s0:s0 + sl], r=[], w=[("kbuf", ks)], key=("kbuf", ks))
                S.dma("sp", vbuf[ks][:, 0:nkt * 130].rearrange("p (k e) -> p k e", e=130),
                      self.VA[s0:s0 + sl].rearrange("(k p) g e -> p k (g e)", p=128), r=[], w=[("vbuf", ks)],
                      key=("vbuf", ks))
                for qb in range(sl // TB):
                    t0 = s0 + qb * TB
                    qs = nxt("q", 3)
                    S.dma("pool", qbuf[qs][:].rearrange("p (i t) -> p i t", i=4),
                          self.QKT[0:4, :, t0:t0 + TB].rearrange("i p t -> p i t"), r=[], w=[("qbuf", qs)], key=("qbuf", qs))
                    ysA = nxt("ystA", 2)
                    for jq in range(4):
                        a_qtile(ks, qs, ysA, jq, 4 * qb + jq, nkt)
                    S.dma("pool", self.YT[0:4, :, t0:t0 + TB].rearrange("m p t -> p m t"), ystA[ysA][:],
                          r=[("ystA", ysA)], w=[("YT", "a", t0)], key=("ystA_st", ysA))

            accs = [(4 + idx // 3, (idx % 3) * 129) for idx in range(8)]

            def b_qblock(ks, h, s0, qb, nkt):
                t0 = s0 + qb * TB
                qs = nxt("q", 3)
                S.dma("pool", qbuf[qs][:, 0:TB], self.QKT[5 + h, :, t0:t0 + TB], r=[], w=[("qbuf", qs)], key=("qbuf", qs))
                zero_banks((4, 5, 6))
                vb = vbuf[ks][:, 0:nkt * 129].rearrange("p (k e) -> p k e", e=129)
                for kt in range(nkt):
                    d = kt - 4 * qb
                    far = d < -1 or d > 4
                    ess = []
                    for c in range(2):
                        sbk = c * 2 + (kt % 2)
                        S.op("pe", lambda e, sbk=sbk, c=c, kt=kt: e.matmul(
                            self.pb[sbk][:], kbuf[ks][64 * c:64 * c + 64, kt * 128:(kt + 1) * 128],
                            qbuf[qs][64 * c:64 * c + 64, 0:TB], start=True, stop=True),
                            r=[("kbuf", ks), ("qbuf", qs)], w=[("pb", sbk)])
                        es = nxt("E", 4)
                        ess.append(es)
                        if far:
                            cc = self.col("cfar", (8 + h) * 2 + (0 if d < 0 else 1))
                            S.op("act", lambda e, sbk=sbk, es=es, cc=cc: e.activation(Eb[es][:], self.pb[sbk][:], AF.Exp,
                                                                                       bias=cc, scale=0.125),
                                 r=[("pb", sbk), "cols"], w=[("E", es)])
                        else:
                            ts = nxt("tmp", 3)
                            for jq in range(4):
                                o = d - jq
                                if -1 <= o <= 1:
                                    S.op("dve", lambda e, sbk=sbk, ts=ts, jq=jq, o=o: e.scalar_tensor_tensor(
                                        out=tmp[ts][:, jq * 128:(jq + 1) * 128], in0=self.pb[sbk][:, jq * 128:(jq + 1) * 128],
                                        scalar=0.125, in1=tbB[:, h, o + 1, :], op0=ALU.mult, op1=ALU.add),
                                        r=[("pb", sbk), "tbB"], w=[("tmp", ts)])
                                else:
                                    cc = self.col("cfar", (8 + h) * 2 + (0 if o < 0 else 1))
                                    S.op("dve", lambda e, sbk=sbk, ts=ts, jq=jq, cc=cc: e.tensor_scalar(
                                        out=tmp[ts][:, jq * 128:(jq + 1) * 128], in0=self.pb[sbk][:, jq * 128:(jq + 1) * 128],
                                        scalar1=0.125, scalar2=cc, op0=ALU.mult, op1=ALU.add),
                                        r=[("pb", sbk), "cols"], w=[("tmp", ts)])
                            S.op("act", lambda e, ts=ts, es=es: e.activation(Eb[es][:], tmp[ts][:], AF.Exp),
                                 r=[("tmp", ts)], w=[("E", es)])
                    for c in range(2):
                        es = ess[c]
                        for jq in range(4):
                            bank, off = accs[c * 4 + jq]
                            S.op("pe", lambda e, es=es, jq=jq, bank=bank, off=off, kt=kt: e.matmul(
                                self.pb[bank][:, off:off + 129], Eb[es][:, jq * 128:(jq + 1) * 128], vb[:, kt, :],
                                start=False, stop=(kt == nkt - 1), skip_group_check=True),
                                r=[("E", es), ("vbuf", ks)], w=[("pb", bank)])
                osl = nxt("osb", 2)
                for c in range(2):
                    for jq in range(4):
                        bank, off = accs[c * 4 + jq]
                        if (c * 4 + jq) % 2:
                            S.op("act", lambda e, c=c, jq=jq, bank=bank, off=off: e.copy(osb[osl][:, c, jq, :],
                                                                                         self.pb[bank][:, off:off + 129]),
                                 r=[("pb", bank)], w=[("osb", osl, c, jq)])
                        else:
                            S.op("dve", lambda e, c=c, jq=jq, bank=bank, off=off: e.tensor_copy(osb[osl][:, c, jq, :],
                                                                                                self.pb[bank][:, off:off + 129]),
                                 r=[("pb", bank)], w=[("osb", osl, c, jq)])
                ys = nxt("yst", 2)
                for jq in range(4):
                    fs = nxt("fs", 2)
                    ok = [("osb", osl, 0, jq), ("osb", osl, 1, jq)]
                    S.op("dve", lambda e, jq=jq, fs=fs: e.reciprocal(rr[fs][:, 0:2], osb[osl][:, :, jq, 128]),
                         r=ok, w=[("rr", fs)])
                    S.op("dve", lambda e, fs=fs: e.tensor_tensor(out=rr[fs][:, 2:3], in0=rr[fs][:, 1:2], in1=self.neglam[:],
                                                                 op=ALU.mult),
                         r=[("rr", fs), "neglam"], w=[("rr2", fs)])
                    S.op("dve", lambda e, jq=jq, fs=fs: e.tensor_scalar(out=yf[fs][:], in0=osb[osl][:, 0, jq, 0:128],
                                                                        scalar1=rr[fs][:, 0:1], scalar2=None, op0=ALU.mult),
                         r=ok + [("rr", fs)], w=[("yf", fs)])
                    S.op("dve", lambda e, jq=jq, fs=fs: e.scalar_tensor_tensor(
                        out=yf2[fs][:], in0=osb[osl][:, 1, jq, 0:128], scalar=rr[fs][:, 2:3], in1=yf[fs][:],
                        op0=ALU.mult, op1=ALU.add),
                        r=ok + [("rr2", fs), ("yf", fs)], w=[("yf2", fs)])
                    S.op("act", lambda e, fs=fs: e.activation(junk[fs][:], yf2[fs][:], AF.Square, accum_out=ssq[fs][:]),
                         r=[("yf2", fs)], w=[("ssq", fs), ("junk", fs)])
                    S.op("act", lambda e, fs=fs: e.activation(ltb[fs][:], ssq[fs][:], AF.Ln, bias=self.col("eps"), scale=1.0 / 128),
                         r=[("ssq", fs), "cols"], w=[("lt", fs)])
                    S.op("act", lambda e, fs=fs: e.activation(rsb[fs][:], ltb[fs][:], AF.Exp, scale=-0.5),
                         r=[("lt", fs)], w=[("rs", fs)])
                    S.op("dve", lambda e, fs=fs: e.scalar_tensor_tensor(
                        out=ynb[fs][:], in0=yf2[fs][:], scalar=rsb[fs][:, 0:1], in1=self.subln08[:], op0=ALU.mult, op1=ALU.mult),
                        r=[("yf2", fs), ("rs", fs), "subln08"], w=[("ynb", fs)])
                    S.op("pe", lambda e, jq=jq, fs=fs: e.transpose(pbt7[:, jq * 128:(jq + 1) * 128], ynb[fs][:], self.identb[:]),
                         r=[("ynb", fs), "identb"], w=[("pb", 7)])
                S.op("act", lambda e: e.copy(yst[ys][:], pbt7[:, 0:TB]), r=[("pb", 7)], w=[("yst", ys)])
                S.dma("pool", self.YT[4 + h, :, t0:t0 + TB], yst[ys][:], r=[("yst", ys)], w=[("YT", h, t0)],
                      key=("yst_st", ys))

            def b_pass(pi, s0, sl, h):
                ks = pi % 2
                nkt = sl // 128
                S.dma("sp", kbuf[ks][:, 0:sl], self.QKT[9 + h, :, s0:s0 + sl], r=[], w=[("kbuf", ks)], key=("kbuf", ks))
                S.dma("sp", vbuf[ks][:, 0:nkt * 129].rearrange("p (k e) -> p k e", e=129),
                      self.VB[s0:s0 + sl, h, :].rearrange("(k p) e -> p k e", p=128), r=[], w=[("vbuf", ks)],
                      key=("vbuf", ks))
                for qb in range(sl // TB):
                    b_qblock(ks, h, s0, qb, nkt)

            pi = 0
            s0 = 0
            import os
            parts = os.environ.get("P2A_PARTS", "ab")
            for sl in self.seqs:
                if "a" in parts:
                    a_pass(pi, s0, sl)
                    pi += 1
                for h in range(4):
                    if "b" in parts:
                        b_pass(pi, s0, sl, h)
                        pi += 1
                s0 += sl
            S.emit("p2a")

    def ffn_bufs(self, sb, pfx):
        B = {"act": sb(pfx + "act", [128, FC, TB], BF16),
             "sg": [sb(pfx + "sg%d" % i, [128, TB], F32) for i in range(2)],
             "wg": [sb(pfx + "wg%d" % i, [128, 2, KC, 128], BF16) for i in range(3)],
             "wu": [sb(pfx + "wu%d" % i, [128, 2, KC, 128], BF16) for i in range(3)],
             "wd": [sb(pfx + "wd%d" % i, [128, FC, 128], BF16) for i in range(2)],
             "cg": 0, "cd": 0}
        return B

    def ffn(self, l, hT, hTk, xres, xresk, B):
        S = self.S
        Wg, Wu, Wd = self.W["g%d" % l], self.W["u%d" % l], self.W["d%d" % l]
        act, sg = B["act"], B["sg"]
        dslots = {}

        def load_wd(oc):
            ds = B["cd"] % 2
            B["cd"] += 1
            dslots[oc] = ds
            S.dma("sp", B["wd"][ds][:].rearrange("p k n -> p (k n)"), Wd[oc].rearrange("p k n -> p (k n)"), r=[],
                  w=[("wd", ds)], key=("wd", ds))

        for jp in range(FC // 2):
            ws = B["cg"] % 3
            B["cg"] += 1
            wg, wu = B["wg"][ws], B["wu"][ws]
            S.dma("sp", wg[:].rearrange("p j k n -> p j (k n)"), Wg[2 * jp:2 * jp + 2].rearrange("j p k n -> p j (k n)"),
                  r=[], w=[("wg", ws)], key=("wg", ws))
            S.dma("sp", wu[:].rearrange("p j k n -> p j (k n)"), Wu[2 * jp:2 * jp + 2].rearrange("j p k n -> p j (k n)"),
                  r=[], w=[("wu", ws)], key=("wu", ws))
            if jp == 8:
                load_wd(0)
            if jp == 10:
                load_wd(1)
            for jj in range(2):
                j = 2 * jp + jj
                gb = (4, 5)[j % 2]
                ub = (6, 7)[j % 2]
                for kc in range(KC):
                    S.op("pe", lambda e, wg=wg, jj=jj, kc=kc, gb=gb: e.matmul(self.pb[gb][:], wg[:, jj, kc, :], hT[:, kc, :],
                                                                             start=(kc == 0), stop=(kc == KC - 1)),
                         r=[(hTk, kc), ("wg", ws)], w=[("pb", gb)])
                for kc in range(KC):
                    S.op("pe", lambda e, wu=wu, jj=jj, kc=kc, ub=ub: e.matmul(self.pb[ub][:], wu[:, jj, kc, :], hT[:, kc, :],
                                                                             start=(kc == 0), stop=(kc == KC - 1)),
                         r=[(hTk, kc), ("wu", ws)], w=[("pb", ub)])
                ss = j % 2
                S.op("act", lambda e, ss=ss, gb=gb: e.activation(sg[ss][:], self.pb[gb][:], AF.Silu),
                     r=[("pb", gb)], w=[("sg", ss)])
                S.op("dve", lambda e, ss=ss, ub=ub, j=j: e.tensor_tensor(out=act[:, j, :], in0=self.pb[ub][:], in1=sg[ss][:],
                                                                         op=ALU.mult),
                     r=[("pb", ub), ("sg", ss)], w=[("act", j)])
        for oc in range(KC):
            if oc not in dslots:
                load_wd(oc)
            ds = dslots[oc]
            wd = B["wd"][ds]
            db = (0, 1)[oc % 2]
            for jc in range(FC):
                S.op("pe", lambda e, wd=wd, jc=jc, db=db: e.matmul(self.pb[db][:], wd[:, jc, :], act[:, jc, :],
                                                                   start=(jc == 0), stop=(jc == FC - 1)),
                     r=[("act", jc), ("wd", ds)], w=[("pb", db)])
            S.op("dve", lambda e, oc=oc, db=db: e.tensor_tensor(out=xres[:, oc, :], in0=self.pb[db][:], in1=xres[:, oc, :],
                                                                op=ALU.add),
                 r=[("pb", db), (xresk, oc)], w=[(xresk, oc)])

    def phase2b(self):
        nc, S = self.nc, self.S
        with contextlib.ExitStack() as ps:
            sb = lambda n, sh, dt: ps.enter_context(nc.sbuf_tensor("s_" + n, list(sh), dt))
            xtok = [sb("pb_xtok%d" % i, [128, 4, D], F32) for i in range(2)]
            xT = [sb("pb_xT%d" % i, [128, KC, TB], F32) for i in range(2)]
            ytb = [sb("pb_yt%d" % i, [128, KC, TB], BF16) for i in range(2)]
            hT = [sb("pb_hT%d" % i, [128, KC, TB], BF16) for i in range(2)]
            sqb = [sb("pb_sq%d" % i, [128, TB], BF16) for i in range(2)]
            lnt = sb("pb_lnt", [128, TB], F32)
            rstd = sb("pb_rstd", [128, TB], F32)
            wout = sb("pb_wout", [128, KC, KC, 128], BF16)
            B = self.ffn_bufs(sb, "pb_")
            S.dma("sp", wout[:].rearrange("p j k n -> p j (k n)"), self.W["aout"][:].rearrange("j p k n -> p j (k n)"),
                  r=[], w=["wout"], key="wout")

            def stage_a(bi, t0):
                xs = bi % 2
                xk = ("xT", xs)
                self.load_xT(t0, xtok[xs], ("xtok", xs), xT[xs], xk, tb=(0, 1))
                S.dma("sp", ytb[xs][:], self.YT[:, :, t0:t0 + TB].rearrange("m p t -> p m t"), r=[], w=[("ytb", xs)],
                      key=("ytb", xs))
                for oc in range(KC):
                    bk = (2, 3)[oc % 2]
                    for kc in range(KC):
                        S.op("pe", lambda e, oc=oc, kc=kc, bk=bk: e.matmul(self.pb[bk][:], wout[:, oc, kc, :], ytb[xs][:, kc, :],
                                                                           start=(kc == 0), stop=(kc == KC - 1)),
                             r=[("ytb", xs), "wout"], w=[("pb", bk)])
                    S.op("dve", lambda e, oc=oc, bk=bk: e.tensor_tensor(out=xT[xs][:, oc, :], in0=self.pb[bk][:],
                                                                        in1=xT[xs][:, oc, :], op=ALU.add),
                         r=[("pb", bk), (xk, oc)], w=[(xk, oc)])
                self.rmsnorm(xT[xs], xk, hT[xs], ("hT", xs), "ffng", 0, TB, sqb, 2, (lnt, rstd), "pbn")

            def stage_b(bi, t0):
                xs = bi % 2
                xk = ("xT", xs)
                self.ffn(0, hT[xs], ("hT", xs), xT[xs], xk, B)
                S.dma("pool", self.X2T[:, :, t0:t0 + TB].rearrange("c p t -> p c t"), xT[xs][:],
                      r=[(xk, c) for c in range(KC)], w=[("X2T", bi)], key=("x2t_st", xs))

            blks = self.blocks()
            for bi, (s0, sl, b, t0) in enumerate(blks):
                stage_a(bi, t0)
                stage_b(bi, t0)
            S.emit("p2b")

    def phase3(self):
        nc, S = self.nc, self.S
        with contextlib.ExitStack() as ps:
            sb = lambda n, sh, dt: ps.enter_context(nc.sbuf_tensor("s_" + n, list(sh), dt))
            xm = sb("p3_xm", [128, KC, TB], F32)
            xh = sb("p3_xh", [128, KC, 32], F32)
            hTm = sb("p3_hTm", [128, KC, TB], BF16)
            hTh = sb("p3_hTh", [128, KC, 32], BF16)
            sqb = [sb("p3_sq%d" % i, [128, TB], BF16) for i in range(2)]
            lnt = sb("p3_lnt", [128, TB], F32)
            rstd = sb("p3_rstd", [128, TB], F32)
            wc = [sb("p3_wc%d" % i, [128, KC, 128], BF16) for i in range(6)]
            ddw = sb("p3_ddw", [128, 4, 31, 128], BF16)
            dsc = sb("p3_dsc", [128, 4, 3, 128], BF16)
            gx = sb("p3_gx", [128, 4, 544], BF16)
            ub = sb("p3_u", [128, 4, 544], BF16)
            gcs = [sb("p3_gcs%d" % i, [128, TB], F32) for i in range(2)]
            hs = [sb("p3_hs%d" % i, [128, 32], F32) for i in range(2)]
            scs = sb("p3_scs", [128, TB], F32)
            vbf = [sb("p3_vbf%d" % i, [128, TB], BF16) for i in range(2)]
            vsq = [sb("p3_vsq%d" % i, [128, TB], BF16) for i in range(2)]
            yT = sb("p3_yT", [128, KC, TB], BF16)
            otok = [sb("p3_otok%d" % i, [128, D], F32) for i in range(2)]
            B = self.ffn_bufs(sb, "p3_")
            actf = B["act"][:].rearrange("p j t -> p (j t)").bitcast(F32)
            vb = actf[:, 0:4 * TB].rearrange("p (c t) -> p c t", c=4)
            vbk = lambda ch: [("act", 2 * ch), ("act", 2 * ch + 1)]
            mean = actf[:, 4 * TB:5 * TB]
            meank = [("act", 8), ("act", 9)]
            msq = actf[:, 5 * TB:6 * TB]
            msqk = [("act", 10), ("act", 11)]
            var = actf[:, 6 * TB:7 * TB]
            vark = [("act", 12), ("act", 13)]
            rs2 = actf[:, 7 * TB:8 * TB]
            rs2k = [("act", 14), ("act", 15)]
            t1 = [actf[:, (8 + i) * TB:(9 + i) * TB] for i in range(2)]
            t1k = [[("act", 16 + 2 * i), ("act", 17 + 2 * i)] for i in range(2)]
            ctr = {"wc": 0, "ot": 0}
            P, Q, H, R, C1, C2, S1, S2 = range(8)

            for ch in range(4):
                for j in range(31):
                    S.op("dve", lambda e, ch=ch, j=j: e.tensor_scalar(out=ddw[:, ch, j, :], in0=self.identb[:],
                                                                      scalar1=self.col("dww", ch * 31 + j), scalar2=None,
                                                                      op0=ALU.mult),
                         r=["identb", "cols"], w=["ddw"])
                for j in range(3):
                    S.op("dve", lambda e, ch=ch, j=j: e.tensor_scalar(out=dsc[:, ch, j, :], in0=self.identb[:],
                                                                      scalar1=self.col("scw", ch * 3 + j), scalar2=None,
                                                                      op0=ALU.mult),
                         r=["identb", "cols"], w=["dsc"])

            def load_w(src):
                i = ctr["wc"] % 6
                ctr["wc"] += 1
                S.dma("sp", wc[i][:].rearrange("p k n -> p (k n)"), src.rearrange("p k n -> p (k n)"), r=[], w=[("wc", i)],
                      key=("wc", i))
                return i

            def blk(bi, s0, sl, b, t0):
                xk = "xm"
                S.dma("sp", xm[:], self.X2T[:, :, t0:t0 + TB].rearrange("c p t -> p c t"), r=[],
                      w=[(xk, c) for c in range(KC)], key="xm")
                hk = [("xh", c) for c in range(KC)]
                if b > 0:
                    S.dma("sp", xh[:, :, 0:16], self.X2T[:, :, t0 - 16:t0].rearrange("c p t -> p c t"), r=[], w=hk, key="xhL")
                else:
                    S.op("dve", lambda e: e.memset(xh[:, :, 0:16], 0.0), w=hk)
                if (b + 1) * TB < sl:
                    S.dma("sp", xh[:, :, 16:32], self.X2T[:, :, t0 + TB:t0 + TB + 16].rearrange("c p t -> p c t"), r=[], w=hk,
                          key="xhR")
                else:
                    S.op("dve", lambda e: e.memset(xh[:, :, 16:32], 0.0), w=hk)
                self.rmsnorm(xm, xk, hTm, "hTm", "mixg", 8, TB, sqb, S1, (lnt, rstd), "p3n")
                self.rmsnorm(xh, "xh", hTh, "hTh", "mixg", 8, 32, sqb, S2, (lnt, rstd), "p3n")
                hmk = [("hTm", c) for c in range(KC)]
                hhk = [("hTh", c) for c in range(KC)]

                def proj(j, bank, hcol):
                    wi = load_w(self.W["cin"][j])
                    w = wc[wi]
                    for kc in range(KC):
                        S.op("pe", lambda e, w=w, kc=kc: e.matmul(self.pb[bank][:], w[:, kc, :], hTm[:, kc, :],
                                                                  start=(kc == 0), stop=(kc == KC - 1)),
                             r=[("hTm", kc), ("wc", wi)], w=[("pb", bank)])
                    if hcol is not None:
                        for kc in range(KC):
                            S.op("pe", lambda e, w=w, kc=kc: e.matmul(self.pb[H][:, hcol:hcol + 32], w[:, kc, :], hTh[:, kc, :],
                                                                      start=(kc == 0), stop=(kc == KC - 1)),
                                 r=[("hTh", kc), ("wc", wi)], w=[("pb", H)])

                def prod(dst, dk, ch, s, func, pa, pbk, ha, hb):
                    if func is None:
                        S.op("act", lambda e: e.copy(gcs[s][:], self.pb[pbk][:]), r=[("pb", pbk)], w=[("gcs", s)])
                        S.op("act", lambda e: e.copy(hs[s][:], self.pb[H][:, hb:hb + 32]), r=[("pb", H)], w=[("hs", s)])
                    else:
                        S.op("act", lambda e: e.activation(gcs[s][:], self.pb[pbk][:], func), r=[("pb", pbk)], w=[("gcs", s)])
                        S.op("act", lambda e: e.activation(hs[s][:], self.pb[H][:, hb:hb + 32], func), r=[("pb", H)],
                             w=[("hs", s)])
                    S.op("dve", lambda e: e.tensor_tensor(out=dst[:, ch, 16:16 + TB], in0=self.pb[pa][:], in1=gcs[s][:], op=ALU.mult),
                         r=[("pb", pa), ("gcs", s)], w=[(dk, ch)])
                    S.op("dve", lambda e: e.tensor_tensor(out=dst[:, ch, 0:16], in0=self.pb[H][:, ha:ha + 16], in1=hs[s][:, 0:16],
                                                          op=ALU.mult),
                         r=[("pb", H), ("hs", s)], w=[(dk, ch)])
                    S.op("dve", lambda e: e.tensor_tensor(out=dst[:, ch, 16 + TB:32 + TB], in0=self.pb[H][:, ha + 16:ha + 32],
                                                          in1=hs[s][:, 16:32], op=ALU.mult),
                         r=[("pb", H), ("hs", s)], w=[(dk, ch)])

                for ch in range(4):
                    s = ch % 2
                    proj(4 + ch, P, 0)
                    proj(8 + ch, Q, 32)
                    prod(gx, "gx", ch, 0, None, Q, P, 32, 0)
                    proj(12 + ch, P, 64)
                    proj(16 + ch, Q, 96)
                    prod(ub, "u", ch, 1, AF.Sigmoid, P, Q, 64, 96)
                    proj(ch, R, None)
                    for j in range(3):
                        S.op("pe", lambda e, ch=ch, j=j: e.matmul(self.pb[C1][:], dsc[:, ch, j, :], gx[:, ch, 15 + j:15 + j + TB],
                                                                  start=(j == 0), stop=(j == 2)),
                             r=[("gx", ch), "dsc"], w=[("pb", C1)])
                    S.op("act", lambda e: e.copy(scs[:], self.pb[C1][:]), r=[("pb", C1)], w=["scs"])
                    S.op("dve", lambda e, ch=ch: e.tensor_tensor(out=yT[:, ch, :], in0=self.pb[R][:], in1=scs[:], op=ALU.mult),
                         r=[("pb", R), "scs"], w=[("yT", ch)])
                    for j in range(31):
                        S.op("pe", lambda e, ch=ch, j=j: e.matmul(self.pb[C2][:], ddw[:, ch, j, :], ub[:, ch, 1 + j:1 + j + TB],
                                                                  start=(j == 0), stop=(j == 30)),
                             r=[("u", ch), "ddw"], w=[("pb", C2)])
                    bcol = self.col("dwb", ch)
                    S.op("act", lambda e, ch=ch, bcol=bcol: e.activation(vb[:, ch, :], self.pb[C2][:], AF.Identity, bias=bcol),
                         r=[("pb", C2), "cols"], w=vbk(ch))
                    S.op("act", lambda e, s=s, bcol=bcol: e.activation(vbf[s][:], self.pb[C2][:], AF.Identity, bias=bcol),
                         r=[("pb", C2), "cols"], w=[("vbf", s)])
                    S.op("act", lambda e, s=s, bcol=bcol: e.activation(vsq[s][:], self.pb[C2][:], AF.Square, bias=bcol),
                         r=[("pb", C2), "cols"], w=[("vsq", s)])
                    S.op("pe", lambda e, s=s, ch=ch: e.matmul(self.pb[S1][:], self.ones[:], vbf[s][:], start=(ch == 0), stop=(ch == 3)),
                         r=[("vbf", s), "ones"], w=[("pb", S1)])
                    S.op("pe", lambda e, s=s, ch=ch: e.matmul(self.pb[S2][:], self.ones[:], vsq[s][:], start=(ch == 0), stop=(ch == 3)),
                         r=[("vsq", s), "ones"], w=[("pb", S2)])
                S.op("dve", lambda e: e.tensor_scalar(out=mean, in0=self.pb[S1][:], scalar1=1.0 / 512, scalar2=None, op0=ALU.mult),
                     r=[("pb", S1)], w=meank)
                S.op("dve", lambda e: e.tensor_tensor(out=msq, in0=mean, in1=mean, op=ALU.mult), r=meank, w=msqk)
                S.op("dve", lambda e: e.scalar_tensor_tensor(out=var, in0=self.pb[S2][:], scalar=1.0 / 512, in1=msq,
                                                             op0=ALU.mult, op1=ALU.subtract),
                     r=[("pb", S2)] + msqk, w=vark)
                S.op("dve", lambda e: e.tensor_scalar(out=msq, in0=var, scalar1=0.0, scalar2=None, op0=ALU.max), r=vark, w=msqk)
                S.op("act", lambda e: e.activation(var, msq, AF.Ln, bias=self.col("eps")), r=msqk + ["cols"], w=vark)
                S.op("act", lambda e: e.activation(rs2, var, AF.Exp, scale=-0.5), r=vark, w=rs2k)
                for ch in range(4):
                    s = ch % 2
                    S.op("dve", lambda e, ch=ch, s=s: e.tensor_tensor(out=t1[s], in0=vb[:, ch, :], in1=mean, op=ALU.subtract),
                         r=vbk(ch) + meank, w=t1k[s])
                    S.op("dve", lambda e, s=s: e.tensor_tensor(out=gcs[s][:], in0=t1[s], in1=rs2, op=ALU.mult),
                         r=t1k[s] + rs2k, w=[("gcs", s)])
                    S.op("act", lambda e, ch=ch, s=s: e.activation(yT[:, 4 + ch, :], gcs[s][:], AF.Silu,
                                                                   bias=self.col("lnb", ch), scale=self.col("lng", ch)),
                         r=[("gcs", s), "cols"], w=[("yT", 4 + ch)])
                for oc in range(KC):
                    wi = load_w(self.W["cout"][oc])
                    w = wc[wi]
                    bk = (C1, C2)[oc % 2]
                    for kc in range(KC):
                        S.op("pe", lambda e, w=w, kc=kc, bk=bk: e.matmul(self.pb[bk][:], w[:, kc, :], yT[:, kc, :],
                                                                         start=(kc == 0), stop=(kc == KC - 1)),
                             r=[("yT", kc), ("wc", wi)], w=[("pb", bk)])
                    S.op("dve", lambda e, oc=oc, bk=bk: e.tensor_tensor(out=xm[:, oc, :], in0=self.pb[bk][:], in1=xm[:, oc, :],
                                                                        op=ALU.add),
                         r=[("pb", bk), (xk, oc)], w=[(xk, oc)])
                self.rmsnorm(xm, xk, hTm, "hTm", "ffng", 8, TB, sqb, S1, (lnt, rstd), "p3n")
                self.ffn(1, hTm, "hTm", xm, xk, B)
                for jt in range(4):
                    osl = ctr["ot"] % 2
                    ctr["ot"] += 1
                    for half in range(2):
                        bank = (H, R)[half]
                        for cc in range(4):
                            c = half * 4 + cc
                            S.op("pe", lambda e, bank=bank, cc=cc, c=c, jt=jt: e.transpose(
                                self.pb[bank][:, cc * 128:(cc + 1) * 128], xm[:, c, jt * 128:(jt + 1) * 128], self.ident[:]),
                                r=[(xk, c), "ident"], w=[("pb", bank)])
                        if half:
                            S.op("act", lambda e, bank=bank, osl=osl: e.copy(otok[osl][:, 512:1024], self.pb[bank][:]),
                                 r=[("pb", bank)], w=[("otok", osl, 1)])
                        else:
                            S.op("dve", lambda e, bank=bank, osl=osl: e.tensor_copy(otok[osl][:, 0:512], self.pb[bank][:]),
                                 r=[("pb", bank)], w=[("otok", osl, 0)])
                    S.dma("pool", self.y[t0 + jt * 128:t0 + (jt + 1) * 128, :], otok[osl][:],
                          r=[("otok", osl, 0), ("otok", osl, 1)], w=[("y", bi, jt)], key=("otok_st", osl))

            for bi, (s0, sl, b, t0) in enumerate(self.blocks()):
                blk(bi, s0, sl, b, t0)
            S.emit("p3")


_PROGRAM_CACHE = {}


def _get_program(seqs):
    key = tuple(seqs)
    if key not in _PROGRAM_CACHE:
        kb = KB(seqs)
        _PROGRAM_CACHE[key] = kb.build()
    return _PROGRAM_CACHE[key]


def _core_inputs(inputs, consts, xcore):
    m = dict(x=xcore,
             w_gate=np.ascontiguousarray(inputs["w_gate"], np.float32),
             w_up=np.ascontiguousarray(inputs["w_up"], np.float32),
             w_down=np.ascontiguousarray(inputs["w_down"], np.float32),
             attn_w_in=np.ascontiguousarray(inputs["attn_w_in"][0], np.float32),
             attn_w_out=np.ascontiguousarray(inputs["attn_w_out"][0], np.float32),
             conv_w_in=np.ascontiguousarray(inputs["conv_w_in"][0], np.float32),
             conv_w_out=np.ascontiguousarray(inputs["conv_w_out"][0], np.float32))
    m.update(consts)
    return m


def kernel(**inputs):
    inputs = {k: np.asarray(v) for k, v in inputs.items()}
    xp = inputs["x_prompt"]
    xs = inputs["x_sample"]
    n = N_CORES
    nsp = xs.shape[0] // n
    seqs = [xp.shape[1]] + [xs.shape[1]] * nsp
    consts = _host_consts(inputs)
    nc = _get_program(seqs)
    in_maps = []
    for c in range(n):
        xcore = np.concatenate([xp[c].reshape(-1, D)] + [xs[c * nsp + i].reshape(-1, D) for i in range(nsp)], axis=0)
        in_maps.append(_core_inputs(inputs, consts, np.ascontiguousarray(xcore, np.float32)))
    res = run_bass_kernel_spmd(nc, in_maps, core_ids=list(range(n)))
    yp = np.empty(xp.shape, np.float32)
    ys = np.empty(xs.shape, np.float32)
    sp = xp.shape[1]
    ss = xs.shape[1]
    for c in range(n):
        y = res.results[c]["y"]
        yp[c] = y[0:sp]
        for i in range(nsp):
            ys[c * nsp + i] = y[sp + i * ss:sp + (i + 1) * ss]
    return (yp, ys)
```

```python
import contextlib
import numpy as np
import ml_dtypes
import concourse.bass as bass
import concourse.mybir as mybir
from concourse.bass_utils import run_bass_kernel_spmd
from concourse.alu_op_type import AluOpType as ALU

F32, BF16 = mybir.dt.float32, mybir.dt.bfloat16
AF = mybir.ActivationFunctionType
AX = mybir.AxisListType

D = 1024
FF = 2816
KC = 8
FC = 22
TB = 512
EPS = 1e-6
ATTN_IN = 2304
CONV_IN = 2560
N_CORES = 8
SEQS_FULL = (8192, 2048, 2048, 2048, 2048)

ENGS = ("pe", "act", "dve", "pool", "sp")
ENG_ATTR = {"pe": "tensor", "act": "scalar", "dve": "vector", "pool": "gpsimd", "sp": "sync"}


class Op:
    __slots__ = ("id", "eng", "fn", "deps", "dma_key", "dma_val", "needs_inc", "seq")

    def __init__(self, id, eng, fn):
        self.id = id
        self.eng = eng
        self.fn = fn
        self.deps = set()
        self.dma_key = None
        self.dma_val = 0
        self.needs_inc = False
        self.seq = 0


class Sched:
    def __init__(self, nc, stack, same_engine_sync=True):
        self.nc = nc
        self.stack = stack
        self.same_engine_sync = same_engine_sync
        self.eng_sem = {e: stack.enter_context(nc.semaphore("sem_" + e)) for e in ENGS}
        self.eng_cnt = {e: 0 for e in ENGS}
        self.dma_sem = {}
        self.dma_cnt = {}
        self.waited = {e: {} for e in ENGS}
        self.stats = []
        self._reset_phase()

    def _reset_phase(self):
        self.ops = []
        self.last_w = {}
        self.readers = {}

    def op(self, eng, fn, r=(), w=(), dma_key=None):
        o = Op(len(self.ops), eng, fn)
        deps = o.deps
        for k in r:
            p = self.last_w.get(k)
            if p is not None:
                deps.add(p)
            if type(k) is tuple and k[0] == "pb":
                for q in self.readers.get(k, ()):
                    if self.ops[q].eng != eng:
                        deps.add(q)
        for k in w:
            p = self.last_w.get(k)
            if p is not None:
                deps.add(p)
            rd = self.readers.get(k)
            if rd:
                deps.update(rd)
        for k in r:
            self.readers.setdefault(k, []).append(o.id)
        for k in w:
            self.last_w[k] = o.id
            self.readers[k] = []
        deps.discard(o.id)
        if dma_key is not None:
            if dma_key not in self.dma_sem:
                self.dma_sem[dma_key] = self.stack.enter_context(
                    self.nc.semaphore("dsem%d" % len(self.dma_sem)))
                self.dma_cnt[dma_key] = 0
            self.dma_cnt[dma_key] += 1
            o.dma_key = dma_key
            o.dma_val = 16 * self.dma_cnt[dma_key]
        self.ops.append(o)
        return o

    def dma(self, eng, out, in_, r, w, key):
        return self.op(eng, lambda e: e.dma_start(out=out, in_=in_), r=r, w=w, dma_key=key)

    def emit(self, name=""):
        ops = self.ops
        for o in ops:
            for d in o.deps:
                p = ops[d]
                if p.dma_key is None and (p.eng != o.eng or (self.same_engine_sync and p.eng != "pe")):
                    p.needs_inc = True
        per_eng = {e: [] for e in ENGS}
        for o in ops:
            per_eng[o.eng].append(o)
        for e in ENGS:
            c = self.eng_cnt[e]
            for o in per_eng[e]:
                if o.dma_key is None and o.needs_inc:
                    c += 1
                    o.seq = c
            self.eng_cnt[e] = c
        nw = [0]
        with self.nc.Block() as block:
            for e in ENGS:
                lst = per_eng[e]
                if not lst and e != "sp":
                    continue

                def body(eng, e=e, lst=lst):
                    waited = self.waited[e]
                    for o in lst:
                        need = {}
                        for d in o.deps:
                            p = ops[d]
                            if p.dma_key is not None:
                                nm = ("d", p.dma_key)
                                sem = self.dma_sem[p.dma_key]
                                val = p.dma_val
                            else:
                                if p.eng == e and (e == "pe" or not self.same_engine_sync):
                                    continue
                                nm = ("e", p.eng)
                                sem = self.eng_sem[p.eng]
                                val = p.seq
                            if val > need.get(nm, (None, 0))[1]:
                                need[nm] = (sem, val)
                        for nm, (sem, val) in need.items():
                            if waited.get(nm, 0) >= val:
                                continue
                            eng.wait_ge(sem, val)
                            waited[nm] = val
                            nw[0] += 1
                        ins = o.fn(eng)
                        if o.dma_key is not None:
                            ins.then_inc(self.dma_sem[o.dma_key], 16)
                        elif o.needs_inc:
                            ins.then_inc(self.eng_sem[e], 1)
                    if e == "sp":
                        for key, sem in self.dma_sem.items():
                            val = 16 * self.dma_cnt[key]
                            if val and waited.get(("d", key), 0) < val:
                                eng.wait_ge(sem, val)
                                waited[("d", key)] = val
                        for e2 in ENGS:
                            if e2 != "sp" and self.eng_cnt[e2] and waited.get(("e", e2), 0) < self.eng_cnt[e2]:
                                eng.wait_ge(self.eng_sem[e2], self.eng_cnt[e2])
                                waited[("e", e2)] = self.eng_cnt[e2]

                getattr(block, ENG_ATTR[e])(body)
        self.stats.append((name, len(ops), nw[0], {e: len(per_eng[e]) for e in ENGS}))
        self._reset_phase()


def _bucket_table():
    import math
    import jax
    import jax.numpy as jnp
    with jax.default_device(jax.devices("cpu")[0]):
        rel = jnp.arange(-255, 256, dtype=jnp.int32)
        half = 16
        max_exact = 8
        n = jnp.abs(rel)
        large = max_exact + (jnp.log(jnp.maximum(n, 1).astype(jnp.float32) / max_exact)
                             / math.log(128 / max_exact) * (half - max_exact)).astype(jnp.int32)
        large = jnp.minimum(large, half - 1)
        b = jnp.where(rel > 0, half, 0) + jnp.where(n < max_exact, n, large)
        return np.asarray(b)


COL_SPEC = [("mixg", 16), ("ffng", 16), ("aq", 1), ("ak", 1), ("bq", 1), ("bk", 1), ("eps", 1),
            ("cfar", 24), ("sink", 8), ("lam", 256), ("subln", 128), ("scw", 12), ("dww", 124),
            ("dwb", 4), ("lng", 4), ("lnb", 4)]
COL_OFF = {}
_o = 0
for _n, _w in COL_SPEC:
    COL_OFF[_n] = _o
    _o += _w
NCOL = _o


def _pack_cols(inp):
    c = np.zeros((128, NCOL), np.float32)

    def put(name, arr):
        arr = np.asarray(arr, np.float32)
        c[:, COL_OFF[name]:COL_OFF[name] + arr.shape[1]] = arr

    fm = lambda v, nch: np.asarray(v, np.float32).reshape(nch, 128).T
    put("mixg", np.concatenate([fm(inp["mix_norm"][l], 8) for l in range(2)], axis=1))
    put("ffng", np.concatenate([fm(inp["ffn_norm"][l], 8) for l in range(2)], axis=1))
    for nm, key in (("aq", "a_q_norm"), ("ak", "a_k_norm"), ("bq", "b_q_norm"), ("bk", "b_k_norm")):
        v = np.asarray(inp[key], np.float32).reshape(64)
        put(nm, np.concatenate([v, v])[:, None])
    put("eps", np.full((128, 1), EPS, np.float32))
    rb = np.asarray(inp["rel_bias"], np.float32)
    cf = np.stack([rb[15, :], rb[31, :]], axis=1).reshape(1, 24)
    put("cfar", np.broadcast_to(cf, (128, 24)))
    put("sink", np.broadcast_to(np.asarray(inp["a_sink"], np.float32).reshape(1, 8), (128, 8)))
    put("lam", np.broadcast_to(np.asarray(inp["b_lambda"], np.float32).reshape(1, 256), (128, 256)))
    put("subln", np.broadcast_to(np.asarray(inp["b_subln"], np.float32).reshape(1, 128), (128, 128)))
    scw = np.asarray(inp["short_conv_w"], np.float32).reshape(3, 4, 128)
    put("scw", scw.transpose(2, 1, 0).reshape(128, 12))
    dww = np.asarray(inp["conf_dw_w"], np.float32).reshape(31, 4, 128)
    put("dww", dww.transpose(2, 1, 0).reshape(128, 124))
    put("dwb", fm(np.asarray(inp["conf_dw_b"]).reshape(512), 4))
    put("lng", fm(np.asarray(inp["conf_ln_g"]).reshape(512), 4))
    put("lnb", fm(np.asarray(inp["conf_ln_b"]).reshape(512), 4))
    return c


def _host_consts(inp):
    bt = _bucket_table()
    k = np.arange(128)[:, None]
    q = np.arange(128)[None, :]
    rb = np.asarray(inp["rel_bias"], np.float32)
    tb = np.zeros((128, 12, 3, 128), np.float32)
    mk = np.zeros((128, 3, 128), np.float32)
    for oi, o in enumerate((-1, 0, 1)):
        rel = 128 * o + k - q
        idx = bt[rel + 255]
        tb[:, :, oi, :] = rb[idx].transpose(0, 2, 1)
        mk[:, oi, :] = np.where(np.abs(rel) <= 128, 0.0, -1e30)
    return {"cols": _pack_cols(inp), "tbias": tb, "maskA": mk,
            "ident": np.eye(128, dtype=np.float32)}


def _attn_groups():
    g = []
    for i in range(4):
        g += [i, i + 4]
    g += [8, 9]
    g += [10, 11]
    for h in range(4):
        g += [12 + 2 * h, 13 + 2 * h]
    for h in range(4):
        g += [20 + 2 * h, 21 + 2 * h]
    for h in range(4):
        g += [28 + 2 * h, 29 + 2 * h]
    return g


QK_CHUNKS = [0, 1, 2, 3, 4, 6, 7, 8, 9, 10, 11, 12, 13]
QK_GAIN = ["aq"] * 4 + ["ak"] + ["bq"] * 4 + ["bk"] * 4


class KB:
    def __init__(self, seqs, dbg=False, phases=("p0", "p1", "p2a", "p2b", "p3")):
        self.seqs = list(seqs)
        self.ntok = sum(seqs)
        self.dbg = dbg
        self.phases = phases
        self.nc = bass.Bass("TRN2", target_bir_lowering=False)

    def din(self, name, shape, dt=F32):
        return self.nc.dram_tensor(name, list(shape), dt, kind="ExternalInput").ap()

    def dscr(self, name, shape, dt):
        kind = "ExternalOutput" if self.dbg else "Internal"
        return self.nc.dram_tensor(name, list(shape), dt, kind=kind).ap()

    def build(self):
        nc = self.nc
        NT = self.ntok
        self.x = self.din("x", [NT, D])
        self.w_gate = self.din("w_gate", [2, D, FF])
        self.w_up = self.din("w_up", [2, D, FF])
        self.w_down = self.din("w_down", [2, FF, D])
        self.attn_w_in = self.din("attn_w_in", [D, ATTN_IN])
        self.attn_w_out = self.din("attn_w_out", [D, D])
        self.conv_w_in = self.din("conv_w_in", [D, CONV_IN])
        self.conv_w_out = self.din("conv_w_out", [D, D])
        self.d_cols = self.din("cols", [128, NCOL])
        self.d_tbias = self.din("tbias", [128, 12, 3, 128])
        self.d_maskA = self.din("maskA", [128, 3, 128])
        self.d_ident = self.din("ident", [128, 128])
        self.y = nc.dram_tensor("y", [NT, D], F32, kind="ExternalOutput").ap()
        self.W = {
            "ain": self.dscr("wb_ain", [18, 128, KC, 128], BF16),
            "aout": self.dscr("wb_aout", [8, 128, KC, 128], BF16),
            "cin": self.dscr("wb_cin", [20, 128, KC, 128], BF16),
            "cout": self.dscr("wb_cout", [8, 128, KC, 128], BF16),
        }
        for l in range(2):
            self.W["g%d" % l] = self.dscr("wb_g%d" % l, [FC, 128, KC, 128], BF16)
            self.W["u%d" % l] = self.dscr("wb_u%d" % l, [FC, 128, KC, 128], BF16)
            self.W["d%d" % l] = self.dscr("wb_d%d" % l, [8, 128, FC, 128], BF16)
        self.QKT = self.dscr("qkt", [13, 128, NT], BF16)
        self.VB = self.dscr("vbs", [NT, 4, 129], BF16)
        self.VA = self.dscr("vas", [NT, 2, 65], BF16)
        self.YT = self.dscr("yt", [8, 128, NT], BF16)
        self.X2T = self.dscr("x2t", [8, 128, NT], F32)

        with contextlib.ExitStack() as st:
            self.st = st
            self.S = Sched(nc, st)
            sb = lambda n, sh, dt: st.enter_context(nc.sbuf_tensor("s_" + n, list(sh), dt))
            self.pbb = [st.enter_context(nc.psum_tensor("pbb%d" % i, [128, 1024], F32)) for i in range(4)]
            self.pb = [self.pbb[i // 2][:, (i % 2) * 512:(i % 2 + 1) * 512] for i in range(8)]
            self.ident = sb("ident", [128, 128], F32)
            self.identb = sb("identb", [128, 128], BF16)
            self.ones = sb("ones", [128, 128], BF16)
            self.bones = sb("bones", [128, 128], BF16)
            self.zer = sb("zer", [128, 128], BF16)
            self.anyr = sb("anyr", [128, 512], BF16)
            self.cols = sb("cols", [128, NCOL], F32)
            self.neglam = sb("neglam", [128, 1], F32)
            self.subln08 = sb("subln08", [128, 128], F32)
            self.esink = sb("esink", [128, 8], F32)
            if "p0" in self.phases:
                self.phase0()
            if "p1" in self.phases:
                self.phase1()
            if "p2a" in self.phases:
                self.phase2a()
            if "p2b" in self.phases:
                self.phase2b()
            if "p3" in self.phases:
                self.phase3()
        return nc

    def col(self, name, i=0, n=1):
        o = COL_OFF[name] + i
        return self.cols[:, o:o + n]

    def phase0(self, weights=True):
        nc, S = self.nc, self.S
        with contextlib.ExitStack() as ps:
            sb = lambda n, sh, dt: ps.enter_context(nc.sbuf_tensor("s_" + n, list(sh), dt))
            S.dma("sp", self.ident[:], self.d_ident, r=[], w=["ident"], key="ident")
            S.dma("sp", self.cols[:], self.d_cols, r=[], w=["cols"], key="cols")
            S.op("dve", lambda e: e.tensor_copy(self.identb[:], self.ident[:]), r=["ident"], w=["identb"])
            S.op("pool", lambda e: e.memset(self.ones[:], 1.0), w=["ones"])
            S.op("pool", lambda e: e.memset(self.zer[:], 0.0), w=["zer"])
            S.op("pool", lambda e: e.memset(self.anyr[:], 1.0), w=["anyr"])
            S.op("pool", lambda e: e.memset(self.bones[:], 0.0), w=["bones"])
            S.op("pool", lambda e: e.memset(self.bones[0:64, 0:64], 1.0), w=["bones"])
            S.op("pool", lambda e: e.memset(self.bones[64:128, 64:128], 1.0), w=["bones"])
            lt = sb("p0_lt", [128, 2, 64], F32)
            ls = sb("p0_ls", [128, 2], F32)
            le = sb("p0_le", [128, 2], F32)
            lam = self.col("lam", 0, 256)
            for i in range(2):
                S.op("dve", lambda e, i=i: e.tensor_tensor(out=lt[:, i, :], in0=lam[:, (2 * i) * 64:(2 * i + 1) * 64],
                                                          in1=lam[:, (2 * i + 1) * 64:(2 * i + 2) * 64], op=ALU.mult),
                     r=["cols"], w=[("lt", i)])
                S.op("dve", lambda e, i=i: e.reduce_sum(out=ls[:, i:i + 1], in_=lt[:, i, :], axis=AX.X),
                     r=[("lt", i)], w=[("ls", i)])
            S.op("act", lambda e: e.activation(le[:], ls[:], AF.Exp), r=[("ls", 0), ("ls", 1)], w=["le"])
            S.op("dve", lambda e: e.scalar_tensor_tensor(out=self.neglam[:], in0=le[:, 1:2], scalar=-0.2, in1=le[:, 0:1],
                                                         op0=ALU.add, op1=ALU.subtract),
                 r=["le"], w=["neglam"])
            S.op("dve", lambda e: e.tensor_scalar(out=self.subln08[:], in0=self.col("subln", 0, 128), scalar1=0.8, scalar2=None,
                                                  op0=ALU.mult),
                 r=["cols"], w=["subln08"])
            S.op("act", lambda e: e.activation(self.esink[:], self.col("sink", 0, 8), AF.Exp), r=["cols"], w=["esink"])

            if weights:
                NSL = 3
                stf = [sb("p0_stf%d" % i, [128, 4096], F32) for i in range(NSL)]
                stb = [sb("p0_stb%d" % i, [128, 4096], BF16) for i in range(NSL)]
                cnt = [0]

                def convert(src2d, dst, kc, groups=None):
                    nch = dst.shape[0]
                    G = 4 if kc == 8 else 1
                    for j0 in range(0, nch, G):
                        g = min(G, nch - j0)
                        i = cnt[0] % NSL
                        cnt[0] += 1
                        n = g * kc * 128
                        f4 = stf[i][:, 0:n].rearrange("p (g k n) -> p g k n", g=g, k=kc)
                        if groups is None:
                            for gi in range(g):
                                j = j0 + gi
                                src = src2d[:, j * 128:(j + 1) * 128].rearrange("(k p) n -> p k n", p=128)
                                S.dma("sp", f4[:, gi], src, r=[], w=[("stf", i, gi, 0), ("stf", i, gi, 1)], key=("stf", i, gi, 0))
                        else:
                            for gi in range(g):
                                for hf in range(2):
                                    gg = groups[2 * (j0 + gi) + hf]
                                    src = src2d[:, gg * 64:(gg + 1) * 64].rearrange("(k p) n -> p k n", p=128)
                                    S.dma("sp", f4[:, gi, :, hf * 64:(hf + 1) * 64], src, r=[],
                                          w=[("stf", i, gi, hf)], key=("stf", i, gi, hf))
                        rk = [("stf", i, gi, hf) for gi in range(g) for hf in range(2)]
                        eng = "act" if cnt[0] % 2 else "dve"
                        if eng == "act":
                            S.op("act", lambda e, i=i, n=n: e.copy(stb[i][:, 0:n], stf[i][:, 0:n]), r=rk, w=[("stb", i)])
                        else:
                            S.op("dve", lambda e, i=i, n=n: e.tensor_copy(stb[i][:, 0:n], stf[i][:, 0:n]), r=rk, w=[("stb", i)])
                        dd = dst[j0:j0 + g].rearrange("g p k n -> p g (k n)")
                        S.dma("pool", dd, stb[i][:, 0:n].rearrange("p (g m) -> p g m", g=g), r=[("stb", i)],
                              w=[("wscr", cnt[0])], key=("stb_st", i))

                convert(self.attn_w_in, self.W["ain"], 8, groups=_attn_groups())
                convert(self.attn_w_out, self.W["aout"], 8)
                convert(self.conv_w_in, self.W["cin"], 8)
                convert(self.conv_w_out, self.W["cout"], 8)
                for l in range(2):
                    convert(self.w_gate[l], self.W["g%d" % l], 8)
                    convert(self.w_up[l], self.W["u%d" % l], 8)
                    convert(self.w_down[l], self.W["d%d" % l], FC)
            S.emit("p0")

    def blocks(self):
        out = []
        s0 = 0
        for sl in self.seqs:
            for b in range(sl // TB):
                out.append((s0, sl, b, s0 + b * TB))
            s0 += sl
        return out

    def load_xT(self, t0, xtok, xtk, xT, xTk, tb=(0, 1)):
        S = self.S
        src = self.x[t0:t0 + TB].rearrange("(j p) f -> p j f", p=128)
        S.dma("sp", xtok[:], src, r=[], w=[xtk], key=xtk)
        for c in range(KC):
            bi = tb[c % 2]
            bank = self.pb[bi]
            for j in range(4):
                S.op("pe", lambda e, bank=bank, j=j, c=c: e.transpose(bank[:, j * 128:(j + 1) * 128],
                                                                     xtok[:, j, c * 128:(c + 1) * 128], self.ident[:]),
                     r=[xtk, "ident"], w=[("pb", bi)])
            S.op("act", lambda e, bank=bank, c=c: e.copy(xT[:, c, :], bank[:]), r=[("pb", bi)], w=[(xTk, c)])

    def rmsnorm(self, xT, xTk, hT, hTk, gname, gidx, N, sqb, ssb, tmp, tmpk):
        S = self.S
        ss = self.pb[ssb]
        lnt, rstd = tmp
        for c in range(KC):
            sq = sqb[c % 2]
            sqk = (tmpk, "sq", c % 2)
            S.op("act", lambda e, sq=sq, c=c: e.activation(sq[:, 0:N], xT[:, c, 0:N], AF.Square), r=[(xTk, c)], w=[sqk])
            S.op("pe", lambda e, sq=sq, c=c: e.matmul(ss[:, 0:N], self.ones[:], sq[:, 0:N], start=(c == 0), stop=(c == KC - 1)),
                 r=[sqk, "ones"], w=[("pb", ssb)])
        S.op("act", lambda e: e.activation(lnt[:, 0:N], ss[:, 0:N], AF.Ln, bias=self.col("eps"), scale=1.0 / D),
             r=[("pb", ssb), "cols"], w=[(tmpk, "lnt")])
        S.op("act", lambda e: e.activation(rstd[:, 0:N], lnt[:, 0:N], AF.Exp, scale=-0.5), r=[(tmpk, "lnt")], w=[(tmpk, "rstd")])
        for c in range(KC):
            S.op("dve", lambda e, c=c: e.scalar_tensor_tensor(out=hT[:, c, 0:N], in0=xT[:, c, 0:N],
                                                              scalar=self.col(gname, gidx + c), in1=rstd[:, 0:N],
                                                              op0=ALU.mult, op1=ALU.mult),
                 r=[(xTk, c), (tmpk, "rstd"), "cols"], w=[(hTk, c)])

    def phase1(self):
        nc, S = self.nc, self.S
        with contextlib.ExitStack() as ps:
            sb = lambda n, sh, dt: ps.enter_context(nc.sbuf_tensor("s_" + n, list(sh), dt))
            xtok = [sb("p1_xtok%d" % i, [128, 4, D], F32) for i in range(2)]
            xT = sb("p1_xT", [128, KC, TB], F32)
            hT = [sb("p1_hT%d" % i, [128, KC, TB], BF16) for i in range(2)]
            sqb = [sb("p1_sq%d" % i, [128, TB], BF16) for i in range(2)]
            lnt = sb("p1_lnt", [128, TB], F32)
            rstd = sb("p1_rstd", [128, TB], F32)
            win = sb("p1_win", [128, 18, KC, 128], BF16)
            sqq = [sb("p1_sqq%d" % i, [128, TB], BF16) for i in range(2)]
            lq = [sb("p1_lq%d" % i, [128, TB], F32) for i in range(2)]
            rq = [sb("p1_rq%d" % i, [128, TB], F32) for i in range(2)]
            qst = [sb("p1_qst%d" % i, [128, TB], BF16) for i in range(3)]
            vast = [sb("p1_vast%d" % i, [128, 4, 2, 65], BF16) for i in range(2)]
            vbst = [sb("p1_vbst%d" % i, [128, 4, 4, 129], BF16) for i in range(2)]
            for j0 in range(0, 18, 6):
                S.dma("sp", win[:, j0:j0 + 6].rearrange("p j k n -> p j (k n)"),
                      self.W["ain"][j0:j0 + 6].rearrange("j p k n -> p j (k n)"), r=[], w=[("win", j0)], key=("win", j0))
            wink = [("win", 0), ("win", 6), ("win", 12)]
            for i in range(2):
                S.op("pool", lambda e, i=i: e.memset(vast[i][:], 1.0), w=[("vast", i)])
                S.op("pool", lambda e, i=i: e.memset(vbst[i][:], 1.0), w=[("vbst", i)])
            T0, T1, SSB, Q0, Q1, PSB, VAB, VBB = range(8)
            def blk(bi, s0, sl, b, t0):
                hk = ("hT", bi % 2)
                h = hT[bi % 2]
                self.load_xT(t0, xtok[bi % 2], ("xtok", bi % 2), xT, "xT", tb=(T0, T1))
                self.rmsnorm(xT, "xT", h, hk, "mixg", 0, TB, sqb, SSB, (lnt, rstd), "p1n")
                hkeys = [(hk, c) for c in range(KC)]
                for ci, j in enumerate(QK_CHUNKS):
                    qi = (Q0, Q1)[ci % 2]
                    qbank = self.pb[qi]
                    for kc in range(KC):
                        S.op("pe", lambda e, qbank=qbank, j=j, kc=kc: e.matmul(qbank[:], win[:, j, kc, :], h[:, kc, :],
                                                                               start=(kc == 0), stop=(kc == KC - 1)),
                             r=[(hk, kc)] + wink, w=[("pb", qi)])
                    s2 = ci % 2
                    S.op("act", lambda e, qbank=qbank, s2=s2: e.activation(sqq[s2][:], qbank[:], AF.Square),
                         r=[("pb", qi)], w=[("sqq", s2)])
                    S.op("pe", lambda e, s2=s2: e.matmul(self.pb[PSB][:], self.bones[:], sqq[s2][:], start=True, stop=True),
                         r=[("sqq", s2), "bones"], w=[("pb", PSB)])
                    S.op("act", lambda e, s2=s2: e.activation(lq[s2][:], self.pb[PSB][:], AF.Ln, bias=self.col("eps"), scale=1.0 / 64),
                         r=[("pb", PSB), "cols"], w=[("lq", s2)])
                    S.op("act", lambda e, s2=s2: e.activation(rq[s2][:], lq[s2][:], AF.Exp, scale=-0.5),
                         r=[("lq", s2)], w=[("rq", s2)])
                    s3 = ci % 3
                    gn = QK_GAIN[ci]
                    S.op("dve", lambda e, qbank=qbank, s2=s2, s3=s3, gn=gn: e.scalar_tensor_tensor(
                        out=qst[s3][:], in0=qbank[:], scalar=self.col(gn), in1=rq[s2][:], op0=ALU.mult, op1=ALU.mult),
                        r=[("pb", qi), ("rq", s2), "cols"], w=[("qst", s3)])
                    S.dma("pool", self.QKT[ci, :, t0:t0 + TB], qst[s3][:], r=[("qst", s3)], w=[("QKT", ci, bi)],
                          key=("qst_st", s3))
                vs = bi % 2
                for jt in range(4):
                    for kc in range(KC):
                        S.op("pe", lambda e, jt=jt, kc=kc: e.matmul(self.pb[VAB][:, jt * 128:(jt + 1) * 128],
                                                                    h[:, kc, jt * 128:(jt + 1) * 128], win[:, 5, kc, :],
                                                                    start=(kc == 0), stop=(kc == KC - 1)),
                             r=[(hk, kc)] + wink, w=[("pb", VAB)])
                    for kc in range(KC):
                        S.op("pe", lambda e, jt=jt, kc=kc: e.matmul(self.pb[VBB][:].rearrange("p (h n) -> p h n", h=4),
                                                                    h[:, kc, jt * 128:(jt + 1) * 128], win[:, 14:18, kc, :],
                                                                    start=(kc == 0), stop=(kc == KC - 1)),
                             r=[(hk, kc)] + wink, w=[("pb", VBB)])
                    S.op("dve" if jt % 2 else "act",
                         (lambda e, jt=jt, vs=vs: e.tensor_copy(vbst[vs][:, jt, :, 0:128], self.pb[VBB][:].rearrange("p (h n) -> p h n", h=4)))
                         if jt % 2 else
                         (lambda e, jt=jt, vs=vs: e.copy(vbst[vs][:, jt, :, 0:128], self.pb[VBB][:].rearrange("p (h n) -> p h n", h=4))),
                         r=[("pb", VBB)], w=[("vbst", vs)])
                S.op("dve", lambda e, vs=vs: e.tensor_copy(vast[vs][:, :, :, 0:64],
                                                           self.pb[VAB][:].rearrange("p (j g n) -> p j g n", j=4, g=2)),
                     r=[("pb", VAB)], w=[("vast", vs)])
                S.dma("pool", self.VB[t0:t0 + TB].rearrange("(j p) h e -> p j (h e)", p=128),
                      vbst[vs][:].rearrange("p j h e -> p j (h e)"), r=[("vbst", vs)], w=[("VB", bi)], key=("vbst_st", vs))
                S.dma("pool", self.VA[t0:t0 + TB].rearrange("(j p) g e -> p j (g e)", p=128),
                      vast[vs][:].rearrange("p j g e -> p j (g e)"), r=[("vast", vs)], w=[("VA", bi)], key=("vast_st", vs))

            for bi, (s0, sl, b, t0) in enumerate(self.blocks()):
                blk(bi, s0, sl, b, t0)
            S.emit("p1")

    def phase2a(self):
        nc, S = self.nc, self.S
        SMAX = max(self.seqs)
        NKT = SMAX // 128
        with contextlib.ExitStack() as ps:
            sb = lambda n, sh, dt: ps.enter_context(nc.sbuf_tensor("s_" + n, list(sh), dt))
            tbB = sb("p2_tbB", [128, 4, 3, 128], F32)
            TA = sb("p2_TA", [128, 2, 3, 4, 128], F32)
            mk = sb("p2_mk", [128, 3, 128], F32)
            bfull = [sb("p2_bf%d" % i, [128, 6, TB], F32) for i in range(2)]
            kbuf = [sb("p2_k%d" % i, [128, SMAX], BF16) for i in range(2)]
            vbuf = [sb("p2_v%d" % i, [128, NKT * 130], BF16) for i in range(2)]
            qbuf = [sb("p2_q%d" % i, [128, 4 * TB], BF16) for i in range(3)]
            Eb = [sb("p2_E%d" % i, [128, 2 * TB], BF16) for i in range(4)]
            tmp = [sb("p2_tmp%d" % i, [128, 2 * TB], F32) for i in range(3)]
            osb = [sb("p2_osb%d" % i, [128, 2, 4, 129], F32) for i in range(2)]
            osa = [sb("p2_osa%d" % i, [128, 2, 4, 65], F32) for i in range(2)]
            rr = [sb("p2_rr%d" % i, [128, 4], F32) for i in range(2)]
            yf = [sb("p2_yf%d" % i, [128, 128], F32) for i in range(2)]
            yf2 = [sb("p2_yg%d" % i, [128, 128], F32) for i in range(2)]
            junk = [sb("p2_jk%d" % i, [128, 128], F32) for i in range(2)]
            ssq = [sb("p2_ssq%d" % i, [128, 1], F32) for i in range(2)]
            ltb = [sb("p2_lt%d" % i, [128, 1], F32) for i in range(2)]
            rsb = [sb("p2_rs%d" % i, [128, 1], F32) for i in range(2)]
            ynb = [sb("p2_ynb%d" % i, [128, 128], BF16) for i in range(2)]
            yst = [sb("p2_yst%d" % i, [128, TB], BF16) for i in range(2)]
            ystA = [sb("p2_ystA%d" % i, [128, 4, TB], BF16) for i in range(2)]
            ya = [sb("p2_ya%d" % i, [128, 8, 64], BF16) for i in range(2)]
            den = [sb("p2_den%d" % i, [128, 8], F32) for i in range(2)]
            rden = [sb("p2_rden%d" % i, [128, 8], F32) for i in range(2)]
            pbt7 = self.pb[7].bitcast(BF16)
            glob = {"q": 0, "yst": 0, "ystA": 0, "fs": 0, "tmp": 0, "pass": 0}

            S.dma("sp", tbB[:], self.d_tbias[:, 8:12], r=[], w=["tbB"], key="tbB")
            for g in range(2):
                for o in range(3):
                    S.dma("sp", TA[:, g, o], self.d_tbias[:, 4 * g:4 * g + 4, o, :], r=[], w=[("TA", g)], key=("TA", g, o))
            S.dma("sp", mk[:], self.d_maskA, r=[], w=["mk"], key="mk")
            for g in range(2):
                for o in range(3):
                    for i in range(4):
                        S.op("dve", lambda e, g=g, o=o, i=i: e.tensor_tensor(out=TA[:, g, o, i, :], in0=TA[:, g, o, i, :],
                                                                             in1=mk[:, o, :], op=ALU.add),
                             r=[("TA", g), "mk"], w=[("TA", g)])

            def zero_banks(banks):
                for b in banks:
                    S.op("pe", lambda e, b=b: e.matmul(self.pb[b], self.zer[:], self.anyr[:], start=True, stop=True,
                                                       skip_group_check=True),
                         r=["zer", "anyr"], w=[("pb", b)])

            def run_stream(nunits, la, front, back, deferred, rate):
                for i in range(nunits + la):
                    if i < nunits:
                        front(i)
                    if i >= la:
                        back(i - la)
                    for _ in range(rate):
                        if deferred:
                            deferred.pop(0)()
                while deferred:
                    deferred.pop(0)()

            def a_stream(pi, s0, sl):
                ks = pi % 2
                nkt = sl // 128
                S.dma("sp", kbuf[ks][:, 0:sl], self.QKT[4, :, s0:s0 + sl], r=[], w=[("kbuf", ks)], key=("kbuf", ks))
                S.dma("sp", vbuf[ks][:, 0:nkt * 130].rearrange("p (k e) -> p k e", e=130),
                      self.VA[s0:s0 + sl].rearrange("(k p) g e -> p k (g e)", p=128), r=[], w=[("vbuf", ks)],
                      key=("vbuf", ks))
                va = vbuf[ks][:, 0:nkt * 130].rearrange("p (k e) -> p k e", e=130)
                units = []
                for t in range(nkt):
                    us = [(t, g, o) for g in range(2) for o in (-1, 0, 1) if 0 <= t + o < nkt]
                    for k, (t_, g, o) in enumerate(us):
                        units.append(dict(t=t, g=g, o=o, first=(k == 0), last=(k == len(us) - 1),
                                          glast=(o == max(oo for (_, gg, oo) in us if gg == g))))
                deferred = []
                qslot = {}
                yslot = {}

                def front(i):
                    u = units[i]
                    t, g, o = u["t"], u["g"], u["o"]
                    qb, jq = t // 4, t % 4
                    if u["first"] and jq == 0:
                        qs = glob["q"] % 3
                        glob["q"] += 1
                        qslot[qb] = qs
                        t0 = s0 + qb * TB
                        S.dma("pool", qbuf[qs][:].rearrange("p (i t) -> p i t", i=4),
                              self.QKT[0:4, :, t0:t0 + TB].rearrange("i p t -> p i t"), r=[], w=[("qbuf", qs)], key=("qbuf", qs))
                    qs = qslot[qb]
                    q4 = qbuf[qs][:].rearrange("p (i t) -> p i t", i=4)
                    kt = t + o
                    sbk = i % 4
                    S.op("pe", lambda e: e.matmul(self.pb[sbk].rearrange("p (i q) -> p i q", i=4),
                                                  kbuf[ks][64 * g:64 * g + 64, kt * 128:(kt + 1) * 128],
                                                  q4[64 * g:64 * g + 64, :, jq * 128:(jq + 1) * 128], start=True, stop=True),
                         r=[("kbuf", ks), ("qbuf", qs)], w=[("pb", sbk)])
                    ts = glob["tmp"] % 3
                    glob["tmp"] += 1
                    es = i % 4
                    u["es"] = es
                    S.op("dve", lambda e: e.scalar_tensor_tensor(out=tmp[ts][:, 0:TB], in0=self.pb[sbk], scalar=0.125,
                                                                 in1=TA[:, g, o + 1].rearrange("p i q -> p (i q)"),
                                                                 op0=ALU.mult, op1=ALU.add),
                         r=[("pb", sbk), ("TA", g)], w=[("tmp", ts)])
                    S.op("act", lambda e: e.activation(Eb[es][:, 0:TB], tmp[ts][:, 0:TB], AF.Exp), r=[("tmp", ts)], w=[("E", es)])

                def back(i):
                    u = units[i]
                    t, g, o, es = u["t"], u["g"], u["o"], u["es"]
                    qb, jq = t // 4, t % 4
                    kt = t + o
                    if u["first"]:
                        zero_banks((4, 5))
                    for hi in range(4):
                        S.op("pe", lambda e, hi=hi: e.matmul(self.pb[4 + g][:, hi * 65:(hi + 1) * 65], Eb[es][:, hi * 128:(hi + 1) * 128],
                                                             va[:, kt, g * 65:(g + 1) * 65], start=False, stop=u["glast"],
                                                             skip_group_check=True),
                             r=[("E", es), ("vbuf", ks)], w=[("pb", 4 + g)])
                    if not u["last"]:
                        return
                    while deferred:
                        deferred.pop(0)()
                    fs = glob["fs"] % 2
                    glob["fs"] += 1
                    for g2 in range(2):
                        S.op("dve", lambda e, g2=g2: e.tensor_copy(osa[fs][:, g2].rearrange("p i e -> p (i e)"),
                                                                   self.pb[4 + g2][:, 0:260]),
                             r=[("pb", 4 + g2)], w=[("osa", fs, g2)])
                    if jq == 0:
                        yslot[qb] = glob["ystA"] % 2
                        glob["ystA"] += 1
                    ysA = yslot[qb]
                    ok = [("osa", fs, 0), ("osa", fs, 1)]
                    ops = []
                    for g2 in range(2):
                        ops.append(lambda g2=g2: S.op("dve", lambda e: e.tensor_tensor(
                            out=den[fs][:, 4 * g2:4 * g2 + 4], in0=osa[fs][:, g2, :, 64], in1=self.esink[:, 4 * g2:4 * g2 + 4],
                            op=ALU.add), r=ok + ["esink"], w=[("den", fs, g2)]))
                    ops.append(lambda: S.op("dve", lambda e: e.reciprocal(rden[fs][:], den[fs][:]),
                                            r=[("den", fs, 0), ("den", fs, 1)], w=[("rden", fs)]))
                    for g2 in range(2):
                        for hi in range(4):
                            hq = 4 * g2 + hi
                            ops.append(lambda g2=g2, hi=hi, hq=hq: S.op("dve", lambda e: e.tensor_scalar(
                                out=ya[fs][:, hq, :], in0=osa[fs][:, g2, hi, 0:64], scalar1=rden[fs][:, hq:hq + 1], scalar2=None,
                                op0=ALU.mult), r=ok + [("rden", fs)], w=[("ya", fs, hq)]))
                    yav = ya[fs][:].rearrange("p h e -> p (h e)")
                    for m in range(4):
                        ops.append(lambda m=m: S.op("pe", lambda e: e.transpose(pbt7[:, m * 128:(m + 1) * 128],
                                                                                yav[:, m * 128:(m + 1) * 128], self.identb[:]),
                                                    r=[("ya", fs, 2 * m), ("ya", fs, 2 * m + 1), "identb"], w=[("pb", 7)]))
                    ops.append(lambda: S.op("dve", lambda e: e.tensor_copy(ystA[ysA][:, :, jq * 128:(jq + 1) * 128],
                                                                           pbt7[:, 0:512].rearrange("p (m q) -> p m q", m=4)),
                                            r=[("pb", 7)], w=[("ystA", ysA)]))
                    if jq == 3:
                        t0 = s0 + qb * TB
                        ops.append(lambda: S.dma("pool", self.YT[0:4, :, t0:t0 + TB].rearrange("m p t -> p m t"), ystA[ysA][:],
                                                 r=[("ystA", ysA)], w=[("YT", "a", t0)], key=("ystA_st", ysA)))
                    deferred.extend(ops)

                run_stream(len(units), 3, front, back, deferred, 4)

            accs = [(4 + idx // 3, (idx % 3) * 129) for idx in range(8)]

            def b_stream(passes):
                NE = 4
                qbs = []
                for (pi, s0, sl, h) in passes:
                    n = sl // TB
                    for qb in range(n):
                        qbs.append(dict(pi=pi, ks=pi % 2, s0=s0, sl=sl, h=h, qb=qb, nkt=sl // 128, t0=s0 + qb * TB,
                                        first=(qb == 0), last=(qb == n - 1)))
                units = [(Qi, kt) for Qi, Qd in enumerate(qbs) for kt in range(Qd["nkt"])]
                deferred = []
                pidx_of = {p[0]: k for k, p in enumerate(passes)}

                def load_kv(pi, s0, sl, h):
                    ks = pi % 2
                    nkt = sl // 128
                    S.dma("sp", kbuf[ks][:, 0:sl], self.QKT[9 + h, :, s0:s0 + sl], r=[], w=[("kbuf", ks)], key=("kbuf", ks))
                    S.dma("sp", vbuf[ks][:, 0:nkt * 129].rearrange("p (k e) -> p k e", e=129),
                          self.VB[s0:s0 + sl, h, :].rearrange("(k p) e -> p k e", p=128), r=[], w=[("vbuf", ks)],
                          key=("vbuf", ks))
                    bf = bfull[ks]
                    for d in range(-1, 5):
                        for jq in range(4):
                            o = d - jq
                            dst = bf[:, d + 1, jq * 128:(jq + 1) * 128]
                            if -1 <= o <= 1:
                                S.op("pool", lambda e, dst=dst, o=o: e.tensor_copy(dst, tbB[:, h, o + 1, :]),
                                     r=["tbB"], w=[("bfull", ks)])
                            else:
                                cc = self.col("cfar", (8 + h) * 2 + (0 if o < 0 else 1))
                                S.op("pool", lambda e, dst=dst, cc=cc: e.tensor_scalar(out=dst, in0=self.subln08[:], scalar1=0.0,
                                                                                       scalar2=cc, op0=ALU.mult, op1=ALU.add),
                                     r=["cols", "subln08"], w=[("bfull", ks)])

                def front(i):
                    Qi, kt = units[i]
                    Qd = qbs[Qi]
                    if kt == 0:
                        qs = glob["q"] % 3
                        glob["q"] += 1
                        Qd["qs"] = qs
                        S.dma("pool", qbuf[qs][:, 0:TB], self.QKT[5 + Qd["h"], :, Qd["t0"]:Qd["t0"] + TB], r=[],
                              w=[("qbuf", qs)], key=("qbuf", qs))
                    ks, qs, h = Qd["ks"], Qd["qs"], Qd["h"]
                    r2 = i % 2
                    es = i % NE
                    sp2 = self.pbb[r2]
                    for c in range(2):
                        S.op("pe", lambda e, c=c: e.matmul(sp2[:, c * TB:(c + 1) * TB], kbuf[ks][64 * c:64 * c + 64, kt * 128:(kt + 1) * 128],
                                                           qbuf[qs][64 * c:64 * c + 64, 0:TB], start=True, stop=True),
                             r=[("kbuf", ks), ("qbuf", qs)], w=[("pb", 2 * r2 + c)])
                    pk = [("pb", 2 * r2), ("pb", 2 * r2 + 1)]
                    d = kt - 4 * Qd["qb"]
                    if d < -1 or d > 4:
                        cc = self.col("cfar", (8 + h) * 2 + (0 if d < 0 else 1))
                        S.op("act", lambda e: e.activation(Eb[es][:], sp2[:], AF.Exp, bias=cc, scale=0.125),
                             r=pk + ["cols"], w=[("E", es)])
                    else:
                        ts = glob["tmp"] % 3
                        glob["tmp"] += 1
                        for c in range(2):
                            S.op("dve", lambda e, c=c: e.scalar_tensor_tensor(
                                out=tmp[ts][:, c * TB:(c + 1) * TB], in0=sp2[:, c * TB:(c + 1) * TB], scalar=0.125,
                                in1=bfull[ks][:, d + 1, :], op0=ALU.mult, op1=ALU.add),
                                r=[("pb", 2 * r2 + c), ("bfull", ks)], w=[("tmp", ts)])
                        S.op("act", lambda e: e.activation(Eb[es][:], tmp[ts][:], AF.Exp), r=[("tmp", ts)], w=[("E", es)])

                def finalize(Qi):
                    Qd = qbs[Qi]
                    h, t0 = Qd["h"], Qd["t0"]
                    while deferred:
                        deferred.pop(0)()
                    osl = Qi % 2
                    of = osb[osl][:].rearrange("p c j e -> p (c j e)")
                    for bi_, (bank, n) in enumerate(((4, 387), (5, 387), (6, 258))):
                        S.op("dve", lambda e, bi_=bi_, bank=bank, n=n: e.tensor_copy(of[:, bi_ * 387:bi_ * 387 + n],
                                                                                      self.pb[bank][:, 0:n]),
                             r=[("pb", bank)], w=[("osb", osl, bi_)])
                    ok = [("osb", osl, 0), ("osb", osl, 1), ("osb", osl, 2)]
                    ys = glob["yst"] % 2
                    glob["yst"] += 1
                    for jq in range(4):
                        fs = glob["fs"] % 2
                        glob["fs"] += 1

                        def mkops(jq=jq, fs=fs):
                            return [
                                lambda: S.op("dve", lambda e: e.reciprocal(rr[fs][:, 0:2], osb[osl][:, :, jq, 128]),
                                             r=ok, w=[("rr", fs)]),
                                lambda: S.op("dve", lambda e: e.tensor_tensor(out=rr[fs][:, 2:3], in0=rr[fs][:, 1:2],
                                                                              in1=self.neglam[:], op=ALU.mult),
                                             r=[("rr", fs), "neglam"], w=[("rr2", fs)]),
                                lambda: S.op("dve", lambda e: e.tensor_scalar(out=yf[fs][:], in0=osb[osl][:, 0, jq, 0:128],
                                                                              scalar1=rr[fs][:, 0:1], scalar2=None, op0=ALU.mult),
                                             r=ok + [("rr", fs)], w=[("yf", fs)]),
                                lambda: S.op("dve", lambda e: e.scalar_tensor_tensor(
                                    out=yf2[fs][:], in0=osb[osl][:, 1, jq, 0:128], scalar=rr[fs][:, 2:3], in1=yf[fs][:],
                                    op0=ALU.mult, op1=ALU.add), r=ok + [("rr2", fs), ("yf", fs)], w=[("yf2", fs)]),
                                lambda: S.op("dve", lambda e: e.tensor_tensor(out=junk[fs][:], in0=yf2[fs][:], in1=yf2[fs][:],
                                                                              op=ALU.mult),
                                             r=[("yf2", fs)], w=[("junk", fs)]),
                                lambda: S.op("dve", lambda e: e.reduce_sum(out=ssq[fs][:], in_=junk[fs][:], axis=AX.X),
                                             r=[("junk", fs)], w=[("ssq", fs)]),
                                lambda: S.op("act", lambda e: e.activation(ltb[fs][:], ssq[fs][:], AF.Ln, bias=self.col("eps"),
                                                                           scale=1.0 / 128),
                                             r=[("ssq", fs), "cols"], w=[("lt", fs)]),
                                lambda: S.op("act", lambda e: e.activation(rsb[fs][:], ltb[fs][:], AF.Exp, scale=-0.5),
                                             r=[("lt", fs)], w=[("rs", fs)]),
                                lambda: S.op("dve", lambda e: e.scalar_tensor_tensor(
                                    out=ynb[fs][:], in0=yf2[fs][:], scalar=rsb[fs][:, 0:1], in1=self.subln08[:],
                                    op0=ALU.mult, op1=ALU.mult), r=[("yf2", fs), ("rs", fs), "subln08"], w=[("ynb", fs)]),
                                lambda: S.op("pe", lambda e: e.transpose(pbt7[:, jq * 128:(jq + 1) * 128], ynb[fs][:],
                                                                         self.identb[:]),
                                             r=[("ynb", fs), "identb"], w=[("pb", 7)]),
                            ]
                        deferred.extend(mkops())
                    deferred.append(lambda: S.op("dve", lambda e: e.tensor_copy(yst[ys][:], pbt7[:, 0:TB]), r=[("pb", 7)],
                                                 w=[("yst", ys)]))
                    deferred.append(lambda: S.dma("pool", self.YT[4 + h, :, t0:t0 + TB], yst[ys][:], r=[("yst", ys)],
                                                  w=[("YT", h, t0)], key=("yst_st", ys)))

                def back(j):
                    Qi, kt = units[j]
                    Qd = qbs[Qi]
                    ks, nkt = Qd["ks"], Qd["nkt"]
                    es = j % NE
                    if kt == 0:
                        zero_banks((4, 5, 6))
                    vb = vbuf[ks][:, 0:nkt * 129].rearrange("p (k e) -> p k e", e=129)
                    for c in range(2):
                        for jq in range(4):
                            bank, off = accs[c * 4 + jq]
                            S.op("pe", lambda e, c=c, jq=jq, bank=bank, off=off: e.matmul(
                                self.pb[bank][:, off:off + 129], Eb[es][:, c * TB + jq * 128:c * TB + (jq + 1) * 128], vb[:, kt, :],
                                start=False, stop=(kt == nkt - 1), skip_group_check=True),
                                r=[("E", es), ("vbuf", ks)], w=[("pb", bank)])
                    pidx = pidx_of[Qd["pi"]]
                    if kt == 0 and Qd["first"] and pidx == 0 and len(passes) > 1:
                        load_kv(*passes[1])
                    if kt == nkt - 1:
                        finalize(Qi)
                        if Qd["last"] and pidx + 2 < len(passes):
                            load_kv(*passes[pidx + 2])

                load_kv(*passes[0])
                rate = max(2, -(-48 // qbs[0]["nkt"]) + 1)
                run_stream(len(units), 2, front, back, deferred, rate)

            import os
            parts = os.environ.get("P2A_PARTS", "ab")
            pi = 0
            s0 = 0
            for sl in self.seqs:
                if "a" in parts:
                    a_stream(pi, s0, sl)
                    pi += 1
                if "b" in parts:
                    passes = []
                    for h in range(4):
                        passes.append((pi, s0, sl, h))
                        pi += 1
                    b_stream(passes)
                s0 += sl
            S.emit("p2a")

    def ffn_bufs(self, sb, pfx):
        B = {"act": sb(pfx + "act", [128, FC, TB], BF16),
             "sg": [sb(pfx + "sg%d" % i, [128, TB], F32) for i in range(2)],
             "wg": [sb(pfx + "wg%d" % i, [128, 2, KC, 128], BF16) for i in range(3)],
             "wu": [sb(pfx + "wu%d" % i, [128, 2, KC, 128], BF16) for i in range(3)],
             "wd": [sb(pfx + "wd%d" % i, [128, FC, 128], BF16) for i in range(2)],
             "cg": 0, "cd": 0}
        return B

    def ffn(self, l, hT, hTk, xres, xresk, B):
        S = self.S
        Wg, Wu, Wd = self.W["g%d" % l], self.W["u%d" % l], self.W["d%d" % l]
        act, sg = B["act"], B["sg"]
        dslots = {}

        def load_wd(oc):
            ds = B["cd"] % 2
            B["cd"] += 1
            dslots[oc] = ds
            S.dma("sp", B["wd"][ds][:].rearrange("p k n -> p (k n)"), Wd[oc].rearrange("p k n -> p (k n)"), r=[],
                  w=[("wd", ds)], key=("wd", ds))

        for jp in range(FC // 2):
            ws = B["cg"] % 3
            B["cg"] += 1
            wg, wu = B["wg"][ws], B["wu"][ws]
            S.dma("sp", wg[:].rearrange("p j k n -> p j (k n)"), Wg[2 * jp:2 * jp + 2].rearrange("j p k n -> p j (k n)"),
                  r=[], w=[("wg", ws)], key=("wg", ws))
            S.dma("sp", wu[:].rearrange("p j k n -> p j (k n)"), Wu[2 * jp:2 * jp + 2].rearrange("j p k n -> p j (k n)"),
                  r=[], w=[("wu", ws)], key=("wu", ws))
            if jp == 8:
                load_wd(0)
            if jp == 10:
                load_wd(1)
            for jj in range(2):
                j = 2 * jp + jj
                gb = (4, 5)[j % 2]
                ub = (6, 7)[j % 2]
                for kc in range(KC):
                    S.op("pe", lambda e, wg=wg, jj=jj, kc=kc, gb=gb: e.matmul(self.pb[gb][:], wg[:, jj, kc, :], hT[:, kc, :],
                                                                             start=(kc == 0), stop=(kc == KC - 1)),
                         r=[(hTk, kc), ("wg", ws)], w=[("pb", gb)])
                for kc in range(KC):
                    S.op("pe", lambda e, wu=wu, jj=jj, kc=kc, ub=ub: e.matmul(self.pb[ub][:], wu[:, jj, kc, :], hT[:, kc, :],
                                                                             start=(kc == 0), stop=(kc == KC - 1)),
                         r=[(hTk, kc), ("wu", ws)], w=[("pb", ub)])
                ss = j % 2
                S.op("act", lambda e, ss=ss, gb=gb: e.activation(sg[ss][:], self.pb[gb][:], AF.Silu),
                     r=[("pb", gb)], w=[("sg", ss)])
                S.op("dve", lambda e, ss=ss, ub=ub, j=j: e.tensor_tensor(out=act[:, j, :], in0=self.pb[ub][:], in1=sg[ss][:],
                                                                         op=ALU.mult),
                     r=[("pb", ub), ("sg", ss)], w=[("act", j)])
        for oc in range(KC):
            if oc not in dslots:
                load_wd(oc)
            ds = dslots[oc]
            wd = B["wd"][ds]
            db = (0, 1)[oc % 2]
            for jc in range(FC):
                S.op("pe", lambda e, wd=wd, jc=jc, db=db: e.matmul(self.pb[db][:], wd[:, jc, :], act[:, jc, :],
                                                                   start=(jc == 0), stop=(jc == FC - 1)),
                     r=[("act", jc), ("wd", ds)], w=[("pb", db)])
            S.op("dve", lambda e, oc=oc, db=db: e.tensor_tensor(out=xres[:, oc, :], in0=self.pb[db][:], in1=xres[:, oc, :],
                                                                op=ALU.add),
                 r=[("pb", db), (xresk, oc)], w=[(xresk, oc)])

    def phase2b(self):
        nc, S = self.nc, self.S
        with contextlib.ExitStack() as ps:
            sb = lambda n, sh, dt: ps.enter_context(nc.sbuf_tensor("s_" + n, list(sh), dt))
            xtok = [sb("pb_xtok%d" % i, [128, 4, D], F32) for i in range(2)]
            xT = [sb("pb_xT%d" % i, [128, KC, TB], F32) for i in range(2)]
            ytb = [sb("pb_yt%d" % i, [128, KC, TB], BF16) for i in range(2)]
            hT = [sb("pb_hT%d" % i, [128, KC, TB], BF16) for i in range(2)]
            sqb = [sb("pb_sq%d" % i, [128, TB], BF16) for i in range(2)]
            lnt = sb("pb_lnt", [128, TB], F32)
            rstd = sb("pb_rstd", [128, TB], F32)
            wout = sb("pb_wout", [128, KC, KC, 128], BF16)
            B = self.ffn_bufs(sb, "pb_")
            S.dma("sp", wout[:].rearrange("p j k n -> p j (k n)"), self.W["aout"][:].rearrange("j p k n -> p j (k n)"),
                  r=[], w=["wout"], key="wout")

            def stage_a(bi, t0):
                xs = bi % 2
                xk = ("xT", xs)
                self.load_xT(t0, xtok[xs], ("xtok", xs), xT[xs], xk, tb=(0, 1))
                S.dma("sp", ytb[xs][:], self.YT[:, :, t0:t0 + TB].rearrange("m p t -> p m t"), r=[], w=[("ytb", xs)],
                      key=("ytb", xs))
                for oc in range(KC):
                    bk = (2, 3)[oc % 2]
                    for kc in range(KC):
                        S.op("pe", lambda e, oc=oc, kc=kc, bk=bk: e.matmul(self.pb[bk][:], wout[:, oc, kc, :], ytb[xs][:, kc, :],
                                                                           start=(kc == 0), stop=(kc == KC - 1)),
                             r=[("ytb", xs), "wout"], w=[("pb", bk)])
                    S.op("dve", lambda e, oc=oc, bk=bk: e.tensor_tensor(out=xT[xs][:, oc, :], in0=self.pb[bk][:],
                                                                        in1=xT[xs][:, oc, :], op=ALU.add),
                         r=[("pb", bk), (xk, oc)], w=[(xk, oc)])
                self.rmsnorm(xT[xs], xk, hT[xs], ("hT", xs), "ffng", 0, TB, sqb, 2, (lnt, rstd), "pbn")

            def stage_b(bi, t0):
                xs = bi % 2
                xk = ("xT", xs)
                self.ffn(0, hT[xs], ("hT", xs), xT[xs], xk, B)
                S.dma("pool", self.X2T[:, :, t0:t0 + TB].rearrange("c p t -> p c t"), xT[xs][:],
                      r=[(xk, c) for c in range(KC)], w=[("X2T", bi)], key=("x2t_st", xs))

            blks = self.blocks()
            for bi, (s0, sl, b, t0) in enumerate(blks):
                stage_a(bi, t0)
                stage_b(bi, t0)
            S.emit("p2b")

    def phase3(self):
        nc, S = self.nc, self.S
        with contextlib.ExitStack() as ps:
            sb = lambda n, sh, dt: ps.enter_context(nc.sbuf_tensor("s_" + n, list(sh), dt))
            xm = sb("p3_xm", [128, KC, TB], F32)
            xh = sb("p3_xh", [128, KC, 32], F32)
            hTm = sb("p3_hTm", [128, KC, TB], BF16)
            hTh = sb("p3_hTh", [128, KC, 32], BF16)
            sqb = [sb("p3_sq%d" % i, [128, TB], BF16) for i in range(2)]
            lnt = sb("p3_lnt", [128, TB], F32)
            rstd = sb("p3_rstd", [128, TB], F32)
            wc = [sb("p3_wc%d" % i, [128, KC, 128], BF16) for i in range(6)]
            ddw = sb("p3_ddw", [128, 4, 31, 128], BF16)
            dsc = sb("p3_dsc", [128, 4, 3, 128], BF16)
            gx = sb("p3_gx", [128, 4, 544], BF16)
            ub = sb("p3_u", [128, 4, 544], BF16)
            gcs = [sb("p3_gcs%d" % i, [128, TB], F32) for i in range(2)]
            hs = [sb("p3_hs%d" % i, [128, 32], F32) for i in range(2)]
            scs = sb("p3_scs", [128, TB], F32)
            vbf = [sb("p3_vbf%d" % i, [128, TB], BF16) for i in range(2)]
            vsq = [sb("p3_vsq%d" % i, [128, TB], BF16) for i in range(2)]
            yT = sb("p3_yT", [128, KC, TB], BF16)
            otok = [sb("p3_otok%d" % i, [128, D], F32) for i in range(2)]
            B = self.ffn_bufs(sb, "p3_")
            actf = B["act"][:].rearrange("p j t -> p (j t)").bitcast(F32)
            vb = actf[:, 0:4 * TB].rearrange("p (c t) -> p c t", c=4)
            vbk = lambda ch: [("act", 2 * ch), ("act", 2 * ch + 1)]
            mean = actf[:, 4 * TB:5 * TB]
            meank = [("act", 8), ("act", 9)]
            msq = actf[:, 5 * TB:6 * TB]
            msqk = [("act", 10), ("act", 11)]
            var = actf[:, 6 * TB:7 * TB]
            vark = [("act", 12), ("act", 13)]
            rs2 = actf[:, 7 * TB:8 * TB]
            rs2k = [("act", 14), ("act", 15)]
            t1 = [actf[:, (8 + i) * TB:(9 + i) * TB] for i in range(2)]
            t1k = [[("act", 16 + 2 * i), ("act", 17 + 2 * i)] for i in range(2)]
            ctr = {"wc": 0, "ot": 0}
            P, Q, H, R, C1, C2, S1, S2 = range(8)

            for ch in range(4):
                for j in range(31):
                    S.op("dve", lambda e, ch=ch, j=j: e.tensor_scalar(out=ddw[:, ch, j, :], in0=self.identb[:],
                                                                      scalar1=self.col("dww", ch * 31 + j), scalar2=None,
                                                                      op0=ALU.mult),
                         r=["identb", "cols"], w=["ddw"])
                for j in range(3):
                    S.op("dve", lambda e, ch=ch, j=j: e.tensor_scalar(out=dsc[:, ch, j, :], in0=self.identb[:],
                                                                      scalar1=self.col("scw", ch * 3 + j), scalar2=None,
                                                                      op0=ALU.mult),
                         r=["identb", "cols"], w=["dsc"])

            def load_w(src):
                i = ctr["wc"] % 6
                ctr["wc"] += 1
                S.dma("sp", wc[i][:].rearrange("p k n -> p (k n)"), src.rearrange("p k n -> p (k n)"), r=[], w=[("wc", i)],
                      key=("wc", i))
                return i

            def blk(bi, s0, sl, b, t0):
                xk = "xm"
                S.dma("sp", xm[:], self.X2T[:, :, t0:t0 + TB].rearrange("c p t -> p c t"), r=[],
                      w=[(xk, c) for c in range(KC)], key="xm")
                hk = [("xh", c) for c in range(KC)]
                if b > 0:
                    S.dma("sp", xh[:, :, 0:16], self.X2T[:, :, t0 - 16:t0].rearrange("c p t -> p c t"), r=[], w=hk, key="xhL")
                else:
                    S.op("dve", lambda e: e.memset(xh[:, :, 0:16], 0.0), w=hk)
                if (b + 1) * TB < sl:
                    S.dma("sp", xh[:, :, 16:32], self.X2T[:, :, t0 + TB:t0 + TB + 16].rearrange("c p t -> p c t"), r=[], w=hk,
                          key="xhR")
                else:
                    S.op("dve", lambda e: e.memset(xh[:, :, 16:32], 0.0), w=hk)
                self.rmsnorm(xm, xk, hTm, "hTm", "mixg", 8, TB, sqb, S1, (lnt, rstd), "p3n")
                self.rmsnorm(xh, "xh", hTh, "hTh", "mixg", 8, 32, sqb, S2, (lnt, rstd), "p3n")
                hmk = [("hTm", c) for c in range(KC)]
                hhk = [("hTh", c) for c in range(KC)]

                def proj(j, bank, hcol):
                    wi = load_w(self.W["cin"][j])
                    w = wc[wi]
                    for kc in range(KC):
                        S.op("pe", lambda e, w=w, kc=kc: e.matmul(self.pb[bank][:], w[:, kc, :], hTm[:, kc, :],
                                                                  start=(kc == 0), stop=(kc == KC - 1)),
                             r=[("hTm", kc), ("wc", wi)], w=[("pb", bank)])
                    if hcol is not None:
                        for kc in range(KC):
                            S.op("pe", lambda e, w=w, kc=kc: e.matmul(self.pb[H][:, hcol:hcol + 32], w[:, kc, :], hTh[:, kc, :],
                                                                      start=(kc == 0), stop=(kc == KC - 1)),
                                 r=[("hTh", kc), ("wc", wi)], w=[("pb", H)])

                def prod(dst, dk, ch, s, func, pa, pbk, ha, hb):
                    if func is None:
                        S.op("act", lambda e: e.copy(gcs[s][:], self.pb[pbk][:]), r=[("pb", pbk)], w=[("gcs", s)])
                        S.op("act", lambda e: e.copy(hs[s][:], self.pb[H][:, hb:hb + 32]), r=[("pb", H)], w=[("hs", s)])
                    else:
                        S.op("act", lambda e: e.activation(gcs[s][:], self.pb[pbk][:], func), r=[("pb", pbk)], w=[("gcs", s)])
                        S.op("act", lambda e: e.activation(hs[s][:], self.pb[H][:, hb:hb + 32], func), r=[("pb", H)],
                             w=[("hs", s)])
                    S.op("dve", lambda e: e.tensor_tensor(out=dst[:, ch, 16:16 + TB], in0=self.pb[pa][:], in1=gcs[s][:], op=ALU.mult),
                         r=[("pb", pa), ("gcs", s)], w=[(dk, ch)])
                    S.op("dve", lambda e: e.tensor_tensor(out=dst[:, ch, 0:16], in0=self.pb[H][:, ha:ha + 16], in1=hs[s][:, 0:16],
                                                          op=ALU.mult),
                         r=[("pb", H), ("hs", s)], w=[(dk, ch)])
                    S.op("dve", lambda e: e.tensor_tensor(out=dst[:, ch, 16 + TB:32 + TB], in0=self.pb[H][:, ha + 16:ha + 32],
                                                          in1=hs[s][:, 16:32], op=ALU.mult),
                         r=[("pb", H), ("hs", s)], w=[(dk, ch)])

                for ch in range(4):
                    s = ch % 2
                    proj(4 + ch, P, 0)
                    proj(8 + ch, Q, 32)
                    prod(gx, "gx", ch, 0, None, Q, P, 32, 0)
                    proj(12 + ch, P, 64)
                    proj(16 + ch, Q, 96)
                    prod(ub, "u", ch, 1, AF.Sigmoid, P, Q, 64, 96)
                    proj(ch, R, None)
                    for j in range(3):
                        S.op("pe", lambda e, ch=ch, j=j: e.matmul(self.pb[C1][:], dsc[:, ch, j, :], gx[:, ch, 15 + j:15 + j + TB],
                                                                  start=(j == 0), stop=(j == 2)),
                             r=[("gx", ch), "dsc"], w=[("pb", C1)])
                    S.op("act", lambda e: e.copy(scs[:], self.pb[C1][:]), r=[("pb", C1)], w=["scs"])
                    S.op("dve", lambda e, ch=ch: e.tensor_tensor(out=yT[:, ch, :], in0=self.pb[R][:], in1=scs[:], op=ALU.mult),
                         r=[("pb", R), "scs"], w=[("yT", ch)])
                    for j in range(31):
                        S.op("pe", lambda e, ch=ch, j=j: e.matmul(self.pb[C2][:], ddw[:, ch, j, :], ub[:, ch, 1 + j:1 + j + TB],
                                                                  start=(j == 0), stop=(j == 30)),
                             r=[("u", ch), "ddw"], w=[("pb", C2)])
                    bcol = self.col("dwb", ch)
                    S.op("act", lambda e, ch=ch, bcol=bcol: e.activation(vb[:, ch, :], self.pb[C2][:], AF.Identity, bias=bcol),
                         r=[("pb", C2), "cols"], w=vbk(ch))
                    S.op("act", lambda e, s=s, bcol=bcol: e.activation(vbf[s][:], self.pb[C2][:], AF.Identity, bias=bcol),
                         r=[("pb", C2), "cols"], w=[("vbf", s)])
                    S.op("act", lambda e, s=s, bcol=bcol: e.activation(vsq[s][:], self.pb[C2][:], AF.Square, bias=bcol),
                         r=[("pb", C2), "cols"], w=[("vsq", s)])
                    S.op("pe", lambda e, s=s, ch=ch: e.matmul(self.pb[S1][:], self.ones[:], vbf[s][:], start=(ch == 0), stop=(ch == 3)),
                         r=[("vbf", s), "ones"], w=[("pb", S1)])
                    S.op("pe", lambda e, s=s, ch=ch: e.matmul(self.pb[S2][:], self.ones[:], vsq[s][:], start=(ch == 0), stop=(ch == 3)),
                         r=[("vsq", s), "ones"], w=[("pb", S2)])
                S.op("dve", lambda e: e.tensor_scalar(out=mean, in0=self.pb[S1][:], scalar1=1.0 / 512, scalar2=None, op0=ALU.mult),
                     r=[("pb", S1)], w=meank)
                S.op("dve", lambda e: e.tensor_tensor(out=msq, in0=mean, in1=mean, op=ALU.mult), r=meank, w=msqk)
                S.op("dve", lambda e: e.scalar_tensor_tensor(out=var, in0=self.pb[S2][:], scalar=1.0 / 512, in1=msq,
                                                             op0=ALU.mult, op1=ALU.subtract),
                     r=[("pb", S2)] + msqk, w=vark)
                S.op("dve", lambda e: e.tensor_scalar(out=msq, in0=var, scalar1=0.0, scalar2=None, op0=ALU.max), r=vark, w=msqk)
                S.op("act", lambda e: e.activation(var, msq, AF.Ln, bias=self.col("eps")), r=msqk + ["cols"], w=vark)
                S.op("act", lambda e: e.activation(rs2, var, AF.Exp, scale=-0.5), r=vark, w=rs2k)
                for ch in range(4):
                    s = ch % 2
                    S.op("dve", lambda e, ch=ch, s=s: e.tensor_tensor(out=t1[s], in0=vb[:, ch, :], in1=mean, op=ALU.subtract),
                         r=vbk(ch) + meank, w=t1k[s])
                    S.op("dve", lambda e, s=s: e.tensor_tensor(out=gcs[s][:], in0=t1[s], in1=rs2, op=ALU.mult),
                         r=t1k[s] + rs2k, w=[("gcs", s)])
                    S.op("act", lambda e, ch=ch, s=s: e.activation(yT[:, 4 + ch, :], gcs[s][:], AF.Silu,
                                                                   bias=self.col("lnb", ch), scale=self.col("lng", ch)),
                         r=[("gcs", s), "cols"], w=[("yT", 4 + ch)])
                for oc in range(KC):
                    wi = load_w(self.W["cout"][oc])
                    w = wc[wi]
                    bk = (C1, C2)[oc % 2]
                    for kc in range(KC):
                        S.op("pe", lambda e, w=w, kc=kc, bk=bk: e.matmul(self.pb[bk][:], w[:, kc, :], yT[:, kc, :],
                                                                         start=(kc == 0), stop=(kc == KC - 1)),
                             r=[("yT", kc), ("wc", wi)], w=[("pb", bk)])
                    S.op("dve", lambda e, oc=oc, bk=bk: e.tensor_tensor(out=xm[:, oc, :], in0=self.pb[bk][:], in1=xm[:, oc, :],
                                                                        op=ALU.add),
                         r=[("pb", bk), (xk, oc)], w=[(xk, oc)])
                self.rmsnorm(xm, xk, hTm, "hTm", "ffng", 8, TB, sqb, S1, (lnt, rstd), "p3n")
                self.ffn(1, hTm, "hTm", xm, xk, B)
                for jt in range(4):
                    osl = ctr["ot"] % 2
                    ctr["ot"] += 1
                    for half in range(2):
                        bank = (H, R)[half]
                        for cc in range(4):
                            c = half * 4 + cc
                            S.op("pe", lambda e, bank=bank, cc=cc, c=c, jt=jt: e.transpose(
                                self.pb[bank][:, cc * 128:(cc + 1) * 128], xm[:, c, jt * 128:(jt + 1) * 128], self.ident[:]),
                                r=[(xk, c), "ident"], w=[("pb", bank)])
                        if half:
                            S.op("act", lambda e, bank=bank, osl=osl: e.copy(otok[osl][:, 512:1024], self.pb[bank][:]),
                                 r=[("pb", bank)], w=[("otok", osl, 1)])
                        else:
                            S.op("dve", lambda e, bank=bank, osl=osl: e.tensor_copy(otok[osl][:, 0:512], self.pb[bank][:]),
                                 r=[("pb", bank)], w=[("otok", osl, 0)])
                    S.dma("pool", self.y[t0 + jt * 128:t0 + (jt + 1) * 128, :], otok[osl][:],
                          r=[("otok", osl, 0), ("otok", osl, 1)], w=[("y", bi, jt)], key=("otok_st", osl))

            for bi, (s0, sl, b, t0) in enumerate(self.blocks()):
                blk(bi, s0, sl, b, t0)
            S.emit("p3")


_PROGRAM_CACHE = {}


def _get_program(seqs):
    key = tuple(seqs)
    if key not in _PROGRAM_CACHE:
        kb = KB(seqs)
        _PROGRAM_CACHE[key] = kb.build()
    return _PROGRAM_CACHE[key]


def _core_inputs(inputs, consts, xcore):
    m = dict(x=xcore,
             w_gate=np.ascontiguousarray(inputs["w_gate"], np.float32),
             w_up=np.ascontiguousarray(inputs["w_up"], np.float32),
             w_down=np.ascontiguousarray(inputs["w_down"], np.float32),
             attn_w_in=np.ascontiguousarray(inputs["attn_w_in"][0], np.float32),
             attn_w_out=np.ascontiguousarray(inputs["attn_w_out"][0], np.float32),
             conv_w_in=np.ascontiguousarray(inputs["conv_w_in"][0], np.float32),
             conv_w_out=np.ascontiguousarray(inputs["conv_w_out"][0], np.float32))
    m.update(consts)
    return m


def kernel(**inputs):
    inputs = {k: np.asarray(v) for k, v in inputs.items()}
    xp = inputs["x_prompt"]
    xs = inputs["x_sample"]
    n = N_CORES
    nsp = xs.shape[0] // n
    seqs = [xp.shape[1]] + [xs.shape[1]] * nsp
    consts = _host_consts(inputs)
    nc = _get_program(seqs)
    in_maps = []
    for c in range(n):
        xcore = np.concatenate([xp[c].reshape(-1, D)] + [xs[c * nsp + i].reshape(-1, D) for i in range(nsp)], axis=0)
        in_maps.append(_core_inputs(inputs, consts, np.ascontiguousarray(xcore, np.float32)))
    res = run_bass_kernel_spmd(nc, in_maps, core_ids=list(range(n)))
    yp = np.empty(xp.shape, np.float32)
    ys = np.empty(xs.shape, np.float32)
    sp = xp.shape[1]
    ss = xs.shape[1]
    for c in range(n):
        y = res.results[c]["y"]
        yp[c] = y[0:sp]
        for i in range(nsp):
            ys[c * nsp + i] = y[sp + i * ss:sp + (i + 1) * ss]
    return (yp, ys)
```

```python
import contextlib
import numpy as np
import ml_dtypes
import concourse.bass as bass
import concourse.mybir as mybir
from concourse.bass_utils import run_bass_kernel_spmd
from concourse.alu_op_type import AluOpType as ALU

F32, BF16 = mybir.dt.float32, mybir.dt.bfloat16
AF = mybir.ActivationFunctionType
AX = mybir.AxisListType

D = 1024
FF = 2816
KC = 8
FC = 22
TB = 512
EPS = 1e-6
ATTN_IN = 2304
CONV_IN = 2560
N_CORES = 8
SEQS_FULL = (8192, 2048, 2048, 2048, 2048)

ENGS = ("pe", "act", "dve", "pool", "sp")
ENG_ATTR = {"pe": "tensor", "act": "scalar", "dve": "vector", "pool": "gpsimd", "sp": "sync"}


class Op:
    __slots__ = ("id", "eng", "fn", "deps", "dma_key", "dma_val", "needs_inc", "seq")

    def __init__(self, id, eng, fn):
        self.id = id
        self.eng = eng
        self.fn = fn
        self.deps = set()
        self.dma_key = None
        self.dma_val = 0
        self.needs_inc = False
        self.seq = 0


class Sched:
    def __init__(self, nc, stack, same_engine_sync=True):
        self.nc = nc
        self.stack = stack
        self.same_engine_sync = same_engine_sync
        self.eng_sem = {e: stack.enter_context(nc.semaphore("sem_" + e)) for e in ENGS}
        self.eng_cnt = {e: 0 for e in ENGS}
        self.dma_sem = {}
        self.dma_cnt = {}
        self.waited = {e: {} for e in ENGS}
        self.stats = []
        self.capture = None
        self._reset_phase()

    def _reset_phase(self):
        self.ops = []
        self.last_w = {}
        self.readers = {}

    def begin_capture(self):
        self.capture = []

    def end_capture(self):
        lst, self.capture = self.capture, None
        return lst

    def drain(self, lst, n=None):
        k = len(lst) if n is None else min(n, len(lst))
        for _ in range(k):
            args = lst.pop(0)
            self.op(*args)

    def op(self, eng, fn, r=(), w=(), dma_key=None):
        if self.capture is not None:
            self.capture.append((eng, fn, list(r), list(w), dma_key))
            return None
        o = Op(len(self.ops), eng, fn)
        deps = o.deps
        for k in r:
            p = self.last_w.get(k)
            if p is not None:
                deps.add(p)
            if type(k) is tuple and k[0] == "pb":
                for q in self.readers.get(k, ()):
                    if self.ops[q].eng != eng:
                        deps.add(q)
        for k in w:
            p = self.last_w.get(k)
            if p is not None:
                deps.add(p)
            rd = self.readers.get(k)
            if rd:
                deps.update(rd)
        for k in r:
            self.readers.setdefault(k, []).append(o.id)
        for k in w:
            self.last_w[k] = o.id
            self.readers[k] = []
        deps.discard(o.id)
        if dma_key is not None:
            if dma_key not in self.dma_sem:
                self.dma_sem[dma_key] = self.stack.enter_context(
                    self.nc.semaphore("dsem%d" % len(self.dma_sem)))
                self.dma_cnt[dma_key] = 0
            self.dma_cnt[dma_key] += 1
            o.dma_key = dma_key
            o.dma_val = 16 * self.dma_cnt[dma_key]
        self.ops.append(o)
        return o

    def dma(self, eng, out, in_, r, w, key):
        return self.op(eng, lambda e: e.dma_start(out=out, in_=in_), r=r, w=w, dma_key=key)

    def emit(self, name=""):
        ops = self.ops
        for o in ops:
            latest = {}
            for d in o.deps:
                p = ops[d]
                if p.dma_key is None and (p.eng != o.eng or (self.same_engine_sync and p.eng != "pe")):
                    if d > latest.get(p.eng, -1):
                        latest[p.eng] = d
            for d in latest.values():
                ops[d].needs_inc = True
        per_eng = {e: [] for e in ENGS}
        for o in ops:
            per_eng[o.eng].append(o)
        for e in ENGS:
            c = self.eng_cnt[e]
            for o in per_eng[e]:
                if o.dma_key is None and o.needs_inc:
                    c += 1
                    o.seq = c
            self.eng_cnt[e] = c
        nw = [0]
        with self.nc.Block() as block:
            for e in ENGS:
                lst = per_eng[e]
                if not lst and e != "sp":
                    continue

                def body(eng, e=e, lst=lst):
                    waited = self.waited[e]
                    for o in lst:
                        need = {}
                        for d in o.deps:
                            p = ops[d]
                            if p.dma_key is not None:
                                nm = ("d", p.dma_key)
                                sem = self.dma_sem[p.dma_key]
                                val = p.dma_val
                            else:
                                if p.eng == e and (e == "pe" or not self.same_engine_sync):
                                    continue
                                nm = ("e", p.eng)
                                sem = self.eng_sem[p.eng]
                                val = p.seq
                            if val > need.get(nm, (None, 0))[1]:
                                need[nm] = (sem, val)
                        for nm, (sem, val) in need.items():
                            if waited.get(nm, 0) >= val:
                                continue
                            eng.wait_ge(sem, val)
                            waited[nm] = val
                            nw[0] += 1
                        ins = o.fn(eng)
                        if o.dma_key is not None:
                            ins.then_inc(self.dma_sem[o.dma_key], 16)
                        elif o.needs_inc:
                            ins.then_inc(self.eng_sem[e], 1)
                    if e == "sp":
                        for key, sem in self.dma_sem.items():
                            val = 16 * self.dma_cnt[key]
                            if val and waited.get(("d", key), 0) < val:
                                eng.wait_ge(sem, val)
                                waited[("d", key)] = val
                        for e2 in ENGS:
                            if e2 != "sp" and self.eng_cnt[e2] and waited.get(("e", e2), 0) < self.eng_cnt[e2]:
                                eng.wait_ge(self.eng_sem[e2], self.eng_cnt[e2])
                                waited[("e", e2)] = self.eng_cnt[e2]

                getattr(block, ENG_ATTR[e])(body)
        self.stats.append((name, len(ops), nw[0], {e: len(per_eng[e]) for e in ENGS}))
        self._reset_phase()


def _bucket_table():
    import math
    import jax
    import jax.numpy as jnp
    with jax.default_device(jax.devices("cpu")[0]):
        rel = jnp.arange(-255, 256, dtype=jnp.int32)
        half = 16
        max_exact = 8
        n = jnp.abs(rel)
        large = max_exact + (jnp.log(jnp.maximum(n, 1).astype(jnp.float32) / max_exact)
                             / math.log(128 / max_exact) * (half - max_exact)).astype(jnp.int32)
        large = jnp.minimum(large, half - 1)
        b = jnp.where(rel > 0, half, 0) + jnp.where(n < max_exact, n, large)
        return np.asarray(b)


COL_SPEC = [("mixg", 16), ("ffng", 16), ("aq", 1), ("ak", 1), ("bq", 1), ("bk", 1), ("eps", 1),
            ("cfar", 24), ("sink", 8), ("lam", 256), ("subln", 128), ("scw", 12), ("dww", 124),
            ("dwb", 4), ("lng", 4), ("lnb", 4)]
COL_OFF = {}
_o = 0
for _n, _w in COL_SPEC:
    COL_OFF[_n] = _o
    _o += _w
NCOL = _o


def _pack_cols(inp):
    c = np.zeros((128, NCOL), np.float32)

    def put(name, arr):
        arr = np.asarray(arr, np.float32)
        c[:, COL_OFF[name]:COL_OFF[name] + arr.shape[1]] = arr

    fm = lambda v, nch: np.asarray(v, np.float32).reshape(nch, 128).T
    put("mixg", np.concatenate([fm(inp["mix_norm"][l], 8) for l in range(2)], axis=1))
    put("ffng", np.concatenate([fm(inp["ffn_norm"][l], 8) for l in range(2)], axis=1))
    for nm, key in (("aq", "a_q_norm"), ("ak", "a_k_norm"), ("bq", "b_q_norm"), ("bk", "b_k_norm")):
        v = np.asarray(inp[key], np.float32).reshape(64)
        put(nm, np.concatenate([v, v])[:, None])
    put("eps", np.full((128, 1), EPS, np.float32))
    rb = np.asarray(inp["rel_bias"], np.float32)
    cf = np.stack([rb[15, :], rb[31, :]], axis=1).reshape(1, 24)
    put("cfar", np.broadcast_to(cf, (128, 24)))
    put("sink", np.broadcast_to(np.asarray(inp["a_sink"], np.float32).reshape(1, 8), (128, 8)))
    put("lam", np.broadcast_to(np.asarray(inp["b_lambda"], np.float32).reshape(1, 256), (128, 256)))
    put("subln", np.broadcast_to(np.asarray(inp["b_subln"], np.float32).reshape(1, 128), (128, 128)))
    scw = np.asarray(inp["short_conv_w"], np.float32).reshape(3, 4, 128)
    put("scw", scw.transpose(2, 1, 0).reshape(128, 12))
    dww = np.asarray(inp["conf_dw_w"], np.float32).reshape(31, 4, 128)
    put("dww", dww.transpose(2, 1, 0).reshape(128, 124))
    put("dwb", fm(np.asarray(inp["conf_dw_b"]).reshape(512), 4))
    put("lng", fm(np.asarray(inp["conf_ln_g"]).reshape(512), 4))
    put("lnb", fm(np.asarray(inp["conf_ln_b"]).reshape(512), 4))
    return c


def _host_consts(inp):
    bt = _bucket_table()
    k = np.arange(128)[:, None]
    q = np.arange(128)[None, :]
    rb = np.asarray(inp["rel_bias"], np.float32)
    tb = np.zeros((128, 12, 3, 128), np.float32)
    mk = np.zeros((128, 3, 128), np.float32)
    for oi, o in enumerate((-1, 0, 1)):
        rel = 128 * o + k - q
        idx = bt[rel + 255]
        tb[:, :, oi, :] = rb[idx].transpose(0, 2, 1)
        mk[:, oi, :] = np.where(np.abs(rel) <= 128, 0.0, -1e30)
    return {"cols": _pack_cols(inp), "tbias": tb, "maskA": mk,
            "ident": np.eye(128, dtype=np.float32)}


def _attn_groups():
    g = []
    for i in range(4):
        g += [i, i + 4]
    g += [8, 9]
    g += [10, 11]
    for h in range(4):
        g += [12 + 2 * h, 13 + 2 * h]
    for h in range(4):
        g += [20 + 2 * h, 21 + 2 * h]
    for h in range(4):
        g += [28 + 2 * h, 29 + 2 * h]
    return g


QK_CHUNKS = [0, 1, 2, 3, 4, 6, 7, 8, 9, 10, 11, 12, 13]
QK_GAIN = ["aq"] * 4 + ["ak"] + ["bq"] * 4 + ["bk"] * 4


class KB:
    def __init__(self, seqs, dbg=False, phases=("p0", "p1", "p2a", "p2b", "p3")):
        self.seqs = list(seqs)
        self.ntok = sum(seqs)
        self.dbg = dbg
        self.phases = phases
        self.nc = bass.Bass("TRN2", target_bir_lowering=False)

    def din(self, name, shape, dt=F32):
        return self.nc.dram_tensor(name, list(shape), dt, kind="ExternalInput").ap()

    def dscr(self, name, shape, dt):
        kind = "ExternalOutput" if self.dbg else "Internal"
        return self.nc.dram_tensor(name, list(shape), dt, kind=kind).ap()

    def build(self):
        nc = self.nc
        NT = self.ntok
        self.x = self.din("x", [NT, D])
        self.w_gate = self.din("w_gate", [2, D, FF])
        self.w_up = self.din("w_up", [2, D, FF])
        self.w_down = self.din("w_down", [2, FF, D])
        self.attn_w_in = self.din("attn_w_in", [D, ATTN_IN])
        self.attn_w_out = self.din("attn_w_out", [D, D])
        self.conv_w_in = self.din("conv_w_in", [D, CONV_IN])
        self.conv_w_out = self.din("conv_w_out", [D, D])
        self.d_cols = self.din("cols", [128, NCOL])
        self.d_tbias = self.din("tbias", [128, 12, 3, 128])
        self.d_maskA = self.din("maskA", [128, 3, 128])
        self.d_ident = self.din("ident", [128, 128])
        self.y = nc.dram_tensor("y", [NT, D], F32, kind="ExternalOutput").ap()
        self.W = {
            "ain": self.dscr("wb_ain", [18, 128, KC, 128], BF16),
            "aout": self.dscr("wb_aout", [8, 128, KC, 128], BF16),
            "cin": self.dscr("wb_cin", [20, 128, KC, 128], BF16),
            "cout": self.dscr("wb_cout", [8, 128, KC, 128], BF16),
        }
        for l in range(2):
            self.W["g%d" % l] = self.dscr("wb_g%d" % l, [FC, 128, KC, 128], BF16)
            self.W["u%d" % l] = self.dscr("wb_u%d" % l, [FC, 128, KC, 128], BF16)
            self.W["d%d" % l] = self.dscr("wb_d%d" % l, [8, 128, FC, 128], BF16)
        self.QKT = self.dscr("qkt", [13, 128, NT], BF16)
        self.VB = self.dscr("vbs", [NT, 4, 129], BF16)
        self.VA = self.dscr("vas", [NT, 2, 65], BF16)
        self.YT = self.dscr("yt", [8, 128, NT], BF16)
        self.X2T = self.dscr("x2t", [8, 128, NT], F32)

        with contextlib.ExitStack() as st:
            self.st = st
            self.S = Sched(nc, st)
            sb = lambda n, sh, dt: st.enter_context(nc.sbuf_tensor("s_" + n, list(sh), dt))
            self.pbb = [st.enter_context(nc.psum_tensor("pbb%d" % i, [128, 1024], F32)) for i in range(4)]
            self.pb = [self.pbb[i // 2][:, (i % 2) * 512:(i % 2 + 1) * 512] for i in range(8)]
            self.ident = sb("ident", [128, 128], F32)
            self.identb = sb("identb", [128, 128], BF16)
            self.ones = sb("ones", [128, 128], BF16)
            self.bones = sb("bones", [128, 128], BF16)
            self.zer = sb("zer", [128, 128], BF16)
            self.anyr = sb("anyr", [128, 512], BF16)
            self.cols = sb("cols", [128, NCOL], F32)
            self.neglam = sb("neglam", [128, 1], F32)
            self.subln08 = sb("subln08", [128, 128], F32)
            self.esink = sb("esink", [128, 8], F32)
            if "p0" in self.phases:
                self.phase0()
            if "p1" in self.phases:
                self.phase1()
            if "p2a" in self.phases:
                self.phase2a()
            if "p2b" in self.phases:
                self.phase2b()
            if "p3" in self.phases:
                self.phase3()
        return nc

    def col(self, name, i=0, n=1):
        o = COL_OFF[name] + i
        return self.cols[:, o:o + n]

    def phase0(self, weights=True):
        nc, S = self.nc, self.S
        with contextlib.ExitStack() as ps:
            sb = lambda n, sh, dt: ps.enter_context(nc.sbuf_tensor("s_" + n, list(sh), dt))
            S.dma("sp", self.ident[:], self.d_ident, r=[], w=["ident"], key="ident")
            S.dma("sp", self.cols[:], self.d_cols, r=[], w=["cols"], key="cols")
            S.op("dve", lambda e: e.tensor_copy(self.identb[:], self.ident[:]), r=["ident"], w=["identb"])
            S.op("pool", lambda e: e.memset(self.ones[:], 1.0), w=["ones"])
            S.op("pool", lambda e: e.memset(self.zer[:], 0.0), w=["zer"])
            S.op("pool", lambda e: e.memset(self.anyr[:], 1.0), w=["anyr"])
            S.op("pool", lambda e: e.memset(self.bones[:], 0.0), w=["bones"])
            S.op("pool", lambda e: e.memset(self.bones[0:64, 0:64], 1.0), w=["bones"])
            S.op("pool", lambda e: e.memset(self.bones[64:128, 64:128], 1.0), w=["bones"])
            lt = sb("p0_lt", [128, 2, 64], F32)
            ls = sb("p0_ls", [128, 2], F32)
            le = sb("p0_le", [128, 2], F32)
            lam = self.col("lam", 0, 256)
            for i in range(2):
                S.op("dve", lambda e, i=i: e.tensor_tensor(out=lt[:, i, :], in0=lam[:, (2 * i) * 64:(2 * i + 1) * 64],
                                                          in1=lam[:, (2 * i + 1) * 64:(2 * i + 2) * 64], op=ALU.mult),
                     r=["cols"], w=[("lt", i)])
                S.op("dve", lambda e, i=i: e.reduce_sum(out=ls[:, i:i + 1], in_=lt[:, i, :], axis=AX.X),
                     r=[("lt", i)], w=[("ls", i)])
            S.op("act", lambda e: e.activation(le[:], ls[:], AF.Exp), r=[("ls", 0), ("ls", 1)], w=["le"])
            S.op("dve", lambda e: e.scalar_tensor_tensor(out=self.neglam[:], in0=le[:, 1:2], scalar=-0.2, in1=le[:, 0:1],
                                                         op0=ALU.add, op1=ALU.subtract),
                 r=["le"], w=["neglam"])
            S.op("dve", lambda e: e.tensor_scalar(out=self.subln08[:], in0=self.col("subln", 0, 128), scalar1=0.8, scalar2=None,
                                                  op0=ALU.mult),
                 r=["cols"], w=["subln08"])
            S.op("act", lambda e: e.activation(self.esink[:], self.col("sink", 0, 8), AF.Exp), r=["cols"], w=["esink"])

            if weights:
                NSL = 3
                stf = [sb("p0_stf%d" % i, [128, 4096], F32) for i in range(NSL)]
                stb = [sb("p0_stb%d" % i, [128, 4096], BF16) for i in range(NSL)]
                cnt = [0]

                def convert(src2d, dst, kc, groups=None):
                    nch = dst.shape[0]
                    G = 4 if kc == 8 else 1
                    for j0 in range(0, nch, G):
                        g = min(G, nch - j0)
                        i = cnt[0] % NSL
                        cnt[0] += 1
                        n = g * kc * 128
                        f4 = stf[i][:, 0:n].rearrange("p (g k n) -> p g k n", g=g, k=kc)
                        if groups is None:
                            for gi in range(g):
                                j = j0 + gi
                                src = src2d[:, j * 128:(j + 1) * 128].rearrange("(k p) n -> p k n", p=128)
                                S.dma("sp", f4[:, gi], src, r=[], w=[("stf", i, gi, 0), ("stf", i, gi, 1)], key=("stf", i, gi, 0))
                        else:
                            for gi in range(g):
                                for hf in range(2):
                                    gg = groups[2 * (j0 + gi) + hf]
                                    src = src2d[:, gg * 64:(gg + 1) * 64].rearrange("(k p) n -> p k n", p=128)
                                    S.dma("sp", f4[:, gi, :, hf * 64:(hf + 1) * 64], src, r=[],
                                          w=[("stf", i, gi, hf)], key=("stf", i, gi, hf))
                        rk = [("stf", i, gi, hf) for gi in range(g) for hf in range(2)]
                        eng = "act" if cnt[0] % 2 else "dve"
                        if eng == "act":
                            S.op("act", lambda e, i=i, n=n: e.copy(stb[i][:, 0:n], stf[i][:, 0:n]), r=rk, w=[("stb", i)])
                        else:
                            S.op("dve", lambda e, i=i, n=n: e.tensor_copy(stb[i][:, 0:n], stf[i][:, 0:n]), r=rk, w=[("stb", i)])
                        dd = dst[j0:j0 + g].rearrange("g p k n -> p g (k n)")
                        S.dma("pool", dd, stb[i][:, 0:n].rearrange("p (g m) -> p g m", g=g), r=[("stb", i)],
                              w=[("wscr", cnt[0])], key=("stb_st", i))

                convert(self.attn_w_in, self.W["ain"], 8, groups=_attn_groups())
                convert(self.attn_w_out, self.W["aout"], 8)
                convert(self.conv_w_in, self.W["cin"], 8)
                convert(self.conv_w_out, self.W["cout"], 8)
                for l in range(2):
                    convert(self.w_gate[l], self.W["g%d" % l], 8)
                    convert(self.w_up[l], self.W["u%d" % l], 8)
                    convert(self.w_down[l], self.W["d%d" % l], FC)
            S.emit("p0")

    def blocks(self):
        out = []
        s0 = 0
        for sl in self.seqs:
            for b in range(sl // TB):
                out.append((s0, sl, b, s0 + b * TB))
            s0 += sl
        return out

    def load_xT(self, t0, xtok, xtk, xT, xTk, tb=(0, 1)):
        S = self.S
        src = self.x[t0:t0 + TB].rearrange("(j p) f -> p j f", p=128)
        S.dma("sp", xtok[:], src, r=[], w=[xtk], key=xtk)
        for c in range(KC):
            bi = tb[c % 2]
            bank = self.pb[bi]
            for j in range(4):
                S.op("pe", lambda e, bank=bank, j=j, c=c: e.transpose(bank[:, j * 128:(j + 1) * 128],
                                                                     xtok[:, j, c * 128:(c + 1) * 128], self.ident[:]),
                     r=[xtk, "ident"], w=[("pb", bi)])
            S.op("act", lambda e, bank=bank, c=c: e.copy(xT[:, c, :], bank[:]), r=[("pb", bi)], w=[(xTk, c)])

    def rmsnorm(self, xT, xTk, hT, hTk, gname, gidx, N, sqb, ssb, tmp, tmpk):
        S = self.S
        ss = self.pb[ssb]
        lnt, rstd = tmp
        for c in range(KC):
            sq = sqb[c % 2]
            sqk = (tmpk, "sq", c % 2)
            S.op("act", lambda e, sq=sq, c=c: e.activation(sq[:, 0:N], xT[:, c, 0:N], AF.Square), r=[(xTk, c)], w=[sqk])
            S.op("pe", lambda e, sq=sq, c=c: e.matmul(ss[:, 0:N], self.ones[:], sq[:, 0:N], start=(c == 0), stop=(c == KC - 1)),
                 r=[sqk, "ones"], w=[("pb", ssb)])
        S.op("act", lambda e: e.activation(lnt[:, 0:N], ss[:, 0:N], AF.Ln, bias=self.col("eps"), scale=1.0 / D),
             r=[("pb", ssb), "cols"], w=[(tmpk, "lnt")])
        S.op("act", lambda e: e.activation(rstd[:, 0:N], lnt[:, 0:N], AF.Exp, scale=-0.5), r=[(tmpk, "lnt")], w=[(tmpk, "rstd")])
        for c in range(KC):
            S.op("dve", lambda e, c=c: e.scalar_tensor_tensor(out=hT[:, c, 0:N], in0=xT[:, c, 0:N],
                                                              scalar=self.col(gname, gidx + c), in1=rstd[:, 0:N],
                                                              op0=ALU.mult, op1=ALU.mult),
                 r=[(xTk, c), (tmpk, "rstd"), "cols"], w=[(hTk, c)])

    def phase1(self):
        nc, S = self.nc, self.S
        with contextlib.ExitStack() as ps:
            sb = lambda n, sh, dt: ps.enter_context(nc.sbuf_tensor("s_" + n, list(sh), dt))
            xtok = [sb("p1_xtok%d" % i, [128, 4, D], F32) for i in range(2)]
            xT = sb("p1_xT", [128, KC, TB], F32)
            hT = [sb("p1_hT%d" % i, [128, KC, TB], BF16) for i in range(2)]
            sqb = [sb("p1_sq%d" % i, [128, TB], BF16) for i in range(2)]
            lnt = sb("p1_lnt", [128, TB], F32)
            rstd = sb("p1_rstd", [128, TB], F32)
            win = sb("p1_win", [128, 18, KC, 128], BF16)
            sqq = [sb("p1_sqq%d" % i, [128, TB], BF16) for i in range(2)]
            lq = [sb("p1_lq%d" % i, [128, TB], F32) for i in range(2)]
            rq = [sb("p1_rq%d" % i, [128, TB], F32) for i in range(2)]
            qst = [sb("p1_qst%d" % i, [128, TB], BF16) for i in range(3)]
            qf = [sb("p1_qf%d" % i, [128, TB], F32) for i in range(2)]
            vast = [sb("p1_vast%d" % i, [128, 4, 2, 65], BF16) for i in range(2)]
            vbst = [sb("p1_vbst%d" % i, [128, 4, 4, 129], BF16) for i in range(2)]
            for j0 in range(0, 18, 6):
                S.dma("sp", win[:, j0:j0 + 6].rearrange("p j k n -> p j (k n)"),
                      self.W["ain"][j0:j0 + 6].rearrange("j p k n -> p j (k n)"), r=[], w=[("win", j0)], key=("win", j0))
            wink = [("win", 0), ("win", 6), ("win", 12)]
            for i in range(2):
                S.op("pool", lambda e, i=i: e.memset(vast[i][:], 1.0), w=[("vast", i)])
                S.op("pool", lambda e, i=i: e.memset(vbst[i][:], 1.0), w=[("vbst", i)])
            T0, T1, SSB, Q0, Q1, PSB, VAB, VBB = range(8)
            def stage_a(bi, t0):
                hk = ("hT", bi % 2)
                h = hT[bi % 2]
                self.load_xT(t0, xtok[bi % 2], ("xtok", bi % 2), xT, "xT", tb=(T0, T1))
                self.rmsnorm(xT, "xT", h, hk, "mixg", 0, TB, sqb, SSB, (lnt, rstd), "p1n")

            def blk(bi, s0, sl, b, t0, bg):
                hk = ("hT", bi % 2)
                h = hT[bi % 2]
                PSS = (PSB, VAB)

                def front(ci):
                    j = QK_CHUNKS[ci]
                    qi = (Q0, Q1)[ci % 2]
                    s2 = ci % 2
                    for kc in range(KC):
                        S.op("pe", lambda e, kc=kc: e.matmul(self.pb[qi][:], win[:, j, kc, :], h[:, kc, :],
                                                             start=(kc == 0), stop=(kc == KC - 1)),
                             r=[(hk, kc)] + wink, w=[("pb", qi)])
                    S.op("act", lambda e: e.activation(sqq[s2][:], self.pb[qi][:], AF.Square), r=[("pb", qi)], w=[("sqq", s2)])
                    S.op("dve", lambda e: e.tensor_copy(qf[s2][:], self.pb[qi][:]), r=[("pb", qi)], w=[("qf", s2)])
                    S.op("pe", lambda e: e.matmul(self.pb[PSS[s2]][:], self.bones[:], sqq[s2][:], start=True, stop=True),
                         r=[("sqq", s2), "bones"], w=[("pb", PSS[s2])])

                def back(ci):
                    s2 = ci % 2
                    s3 = ci % 3
                    gn = QK_GAIN[ci]
                    S.op("act", lambda e: e.activation(lq[s2][:], self.pb[PSS[s2]][:], AF.Ln, bias=self.col("eps"), scale=1.0 / 64),
                         r=[("pb", PSS[s2]), "cols"], w=[("lq", s2)])
                    S.op("act", lambda e: e.activation(rq[s2][:], lq[s2][:], AF.Exp, scale=-0.5), r=[("lq", s2)], w=[("rq", s2)])
                    S.op("dve", lambda e: e.scalar_tensor_tensor(out=qst[s3][:], in0=qf[s2][:], scalar=self.col(gn), in1=rq[s2][:],
                                                                 op0=ALU.mult, op1=ALU.mult),
                         r=[("qf", s2), ("rq", s2), "cols"], w=[("qst", s3)])
                    S.dma("pool", self.QKT[ci, :, t0:t0 + TB], qst[s3][:], r=[("qst", s3)], w=[("QKT", ci, bi)],
                          key=("qst_st", s3))

                for ci in range(len(QK_CHUNKS) + 1):
                    S.drain(bg, 5)
                    if ci < len(QK_CHUNKS):
                        front(ci)
                    if ci >= 1:
                        back(ci - 1)
                vs = bi % 2
                for jt in range(4):
                    S.drain(bg, 5)
                    for kc in range(KC):
                        S.op("pe", lambda e, jt=jt, kc=kc: e.matmul(self.pb[VAB][:, jt * 128:(jt + 1) * 128],
                                                                    h[:, kc, jt * 128:(jt + 1) * 128], win[:, 5, kc, :],
                                                                    start=(kc == 0), stop=(kc == KC - 1)),
                             r=[(hk, kc)] + wink, w=[("pb", VAB)])
                    for kc in range(KC):
                        S.op("pe", lambda e, jt=jt, kc=kc: e.matmul(self.pb[VBB][:].rearrange("p (h n) -> p h n", h=4),
                                                                    h[:, kc, jt * 128:(jt + 1) * 128], win[:, 14:18, kc, :],
                                                                    start=(kc == 0), stop=(kc == KC - 1)),
                             r=[(hk, kc)] + wink, w=[("pb", VBB)])
                    S.op("dve" if jt % 2 else "act",
                         (lambda e, jt=jt, vs=vs: e.tensor_copy(vbst[vs][:, jt, :, 0:128], self.pb[VBB][:].rearrange("p (h n) -> p h n", h=4)))
                         if jt % 2 else
                         (lambda e, jt=jt, vs=vs: e.copy(vbst[vs][:, jt, :, 0:128], self.pb[VBB][:].rearrange("p (h n) -> p h n", h=4))),
                         r=[("pb", VBB)], w=[("vbst", vs)])
                S.op("dve", lambda e, vs=vs: e.tensor_copy(vast[vs][:, :, :, 0:64],
                                                           self.pb[VAB][:].rearrange("p (j g n) -> p j g n", j=4, g=2)),
                     r=[("pb", VAB)], w=[("vast", vs)])
                S.dma("pool", self.VB[t0:t0 + TB].rearrange("(j p) h e -> p j (h e)", p=128),
                      vbst[vs][:].rearrange("p j h e -> p j (h e)"), r=[("vbst", vs)], w=[("VB", bi)], key=("vbst_st", vs))
                S.dma("pool", self.VA[t0:t0 + TB].rearrange("(j p) g e -> p j (g e)", p=128),
                      vast[vs][:].rearrange("p j g e -> p j (g e)"), r=[("vast", vs)], w=[("VA", bi)], key=("vast_st", vs))

                S.drain(bg)

            blks = self.blocks()
            stage_a(0, blks[0][3])
            for bi, (s0, sl, b, t0) in enumerate(blks):
                bg = []
                if bi + 1 < len(blks):
                    S.begin_capture()
                    stage_a(bi + 1, blks[bi + 1][3])
                    bg = S.end_capture()
                blk(bi, s0, sl, b, t0, bg)
            S.emit("p1")

    def phase2a(self):
        nc, S = self.nc, self.S
        SMAX = max(self.seqs)
        NKT = SMAX // 128
        with contextlib.ExitStack() as ps:
            sb = lambda n, sh, dt: ps.enter_context(nc.sbuf_tensor("s_" + n, list(sh), dt))
            tbB = sb("p2_tbB", [128, 4, 3, 128], F32)
            TA = sb("p2_TA", [128, 2, 3, 4, 128], F32)
            mk = sb("p2_mk", [128, 3, 128], F32)
            bfull = [sb("p2_bf%d" % i, [128, 6, TB], F32) for i in range(2)]
            kbuf = [sb("p2_k%d" % i, [128, SMAX], BF16) for i in range(2)]
            vbuf = [sb("p2_v%d" % i, [128, NKT * 130], BF16) for i in range(2)]
            qbuf = [sb("p2_q%d" % i, [128, 4 * TB], BF16) for i in range(3)]
            Eb = [sb("p2_E%d" % i, [128, 2 * TB], BF16) for i in range(4)]
            tmp = [sb("p2_tmp%d" % i, [128, 2 * TB], F32) for i in range(3)]
            osb = [sb("p2_osb%d" % i, [128, 2, 4, 129], F32) for i in range(2)]
            osa = [sb("p2_osa%d" % i, [128, 2, 4, 65], F32) for i in range(2)]
            rr = [sb("p2_rr%d" % i, [128, 4], F32) for i in range(2)]
            yf = [sb("p2_yf%d" % i, [128, 128], F32) for i in range(2)]
            yf2 = [sb("p2_yg%d" % i, [128, 128], F32) for i in range(2)]
            junk = [sb("p2_jk%d" % i, [128, 128], F32) for i in range(2)]
            ssq = [sb("p2_ssq%d" % i, [128, 1], F32) for i in range(2)]
            ltb = [sb("p2_lt%d" % i, [128, 1], F32) for i in range(2)]
            rsb = [sb("p2_rs%d" % i, [128, 1], F32) for i in range(2)]
            ynb = [sb("p2_ynb%d" % i, [128, 128], BF16) for i in range(2)]
            yst = [sb("p2_yst%d" % i, [128, TB], BF16) for i in range(2)]
            ystA = [sb("p2_ystA%d" % i, [128, 4, TB], BF16) for i in range(2)]
            ya = [sb("p2_ya%d" % i, [128, 8, 64], BF16) for i in range(2)]
            den = [sb("p2_den%d" % i, [128, 8], F32) for i in range(2)]
            rden = [sb("p2_rden%d" % i, [128, 8], F32) for i in range(2)]
            pbt7 = self.pb[7].bitcast(BF16)
            glob = {"q": 0, "yst": 0, "ystA": 0, "fs": 0, "tmp": 0, "pass": 0}

            S.dma("sp", tbB[:], self.d_tbias[:, 8:12], r=[], w=["tbB"], key="tbB")
            for g in range(2):
                for o in range(3):
                    S.dma("sp", TA[:, g, o], self.d_tbias[:, 4 * g:4 * g + 4, o, :], r=[], w=[("TA", g)], key=("TA", g, o))
            S.dma("sp", mk[:], self.d_maskA, r=[], w=["mk"], key="mk")
            for g in range(2):
                for o in range(3):
                    for i in range(4):
                        S.op("dve", lambda e, g=g, o=o, i=i: e.tensor_tensor(out=TA[:, g, o, i, :], in0=TA[:, g, o, i, :],
                                                                             in1=mk[:, o, :], op=ALU.add),
                             r=[("TA", g), "mk"], w=[("TA", g)])

            def zero_banks(banks):
                for b in banks:
                    S.op("pe", lambda e, b=b: e.matmul(self.pb[b], self.zer[:], self.anyr[:], start=True, stop=True,
                                                       skip_group_check=True),
                         r=["zer", "anyr"], w=[("pb", b)])

            def run_stream(nunits, la, front, back, deferred, rate):
                for i in range(nunits + la):
                    if i < nunits:
                        front(i)
                    if i >= la:
                        back(i - la)
                    for _ in range(rate):
                        if deferred:
                            deferred.pop(0)()
                while deferred:
                    deferred.pop(0)()

            def a_stream(pi, s0, sl):
                ks = pi % 2
                nkt = sl // 128
                S.dma("sp", kbuf[ks][:, 0:sl], self.QKT[4, :, s0:s0 + sl], r=[], w=[("kbuf", ks)], key=("kbuf", ks))
                S.dma("sp", vbuf[ks][:, 0:nkt * 130].rearrange("p (k e) -> p k e", e=130),
                      self.VA[s0:s0 + sl].rearrange("(k p) g e -> p k (g e)", p=128), r=[], w=[("vbuf", ks)],
                      key=("vbuf", ks))
                va = vbuf[ks][:, 0:nkt * 130].rearrange("p (k e) -> p k e", e=130)
                units = []
                for t in range(nkt):
                    us = [(t, g, o) for g in range(2) for o in (-1, 0, 1) if 0 <= t + o < nkt]
                    for k, (t_, g, o) in enumerate(us):
                        units.append(dict(t=t, g=g, o=o, first=(k == 0), last=(k == len(us) - 1),
                                          glast=(o == max(oo for (_, gg, oo) in us if gg == g))))
                deferred = []
                qslot = {}
                yslot = {}

                def front(i):
                    u = units[i]
                    t, g, o = u["t"], u["g"], u["o"]
                    qb, jq = t // 4, t % 4
                    if u["first"] and jq == 0:
                        qs = glob["q"] % 3
                        glob["q"] += 1
                        qslot[qb] = qs
                        t0 = s0 + qb * TB
                        S.dma("pool", qbuf[qs][:].rearrange("p (i t) -> p i t", i=4),
                              self.QKT[0:4, :, t0:t0 + TB].rearrange("i p t -> p i t"), r=[], w=[("qbuf", qs)], key=("qbuf", qs))
                    qs = qslot[qb]
                    q4 = qbuf[qs][:].rearrange("p (i t) -> p i t", i=4)
                    kt = t + o
                    sbk = i % 4
                    S.op("pe", lambda e: e.matmul(self.pb[sbk].rearrange("p (i q) -> p i q", i=4),
                                                  kbuf[ks][64 * g:64 * g + 64, kt * 128:(kt + 1) * 128],
                                                  q4[64 * g:64 * g + 64, :, jq * 128:(jq + 1) * 128], start=True, stop=True),
                         r=[("kbuf", ks), ("qbuf", qs)], w=[("pb", sbk)])
                    ts = glob["tmp"] % 3
                    glob["tmp"] += 1
                    es = i % 4
                    u["es"] = es
                    S.op("dve", lambda e: e.scalar_tensor_tensor(out=tmp[ts][:, 0:TB], in0=self.pb[sbk], scalar=0.125,
                                                                 in1=TA[:, g, o + 1].rearrange("p i q -> p (i q)"),
                                                                 op0=ALU.mult, op1=ALU.add),
                         r=[("pb", sbk), ("TA", g)], w=[("tmp", ts)])
                    S.op("act", lambda e: e.activation(Eb[es][:, 0:TB], tmp[ts][:, 0:TB], AF.Exp), r=[("tmp", ts)], w=[("E", es)])

                def back(i):
                    u = units[i]
                    t, g, o, es = u["t"], u["g"], u["o"], u["es"]
                    qb, jq = t // 4, t % 4
                    kt = t + o
                    if u["first"]:
                        zero_banks((4, 5))
                    for hi in range(4):
                        S.op("pe", lambda e, hi=hi: e.matmul(self.pb[4 + g][:, hi * 65:(hi + 1) * 65], Eb[es][:, hi * 128:(hi + 1) * 128],
                                                             va[:, kt, g * 65:(g + 1) * 65], start=False, stop=u["glast"],
                                                             skip_group_check=True),
                             r=[("E", es), ("vbuf", ks)], w=[("pb", 4 + g)])
                    if not u["last"]:
                        return
                    while deferred:
                        deferred.pop(0)()
                    fs = glob["fs"] % 2
                    glob["fs"] += 1
                    for g2 in range(2):
                        S.op("dve", lambda e, g2=g2: e.tensor_copy(osa[fs][:, g2].rearrange("p i e -> p (i e)"),
                                                                   self.pb[4 + g2][:, 0:260]),
                             r=[("pb", 4 + g2)], w=[("osa", fs, g2)])
                    if jq == 0:
                        yslot[qb] = glob["ystA"] % 2
                        glob["ystA"] += 1
                    ysA = yslot[qb]
                    ok = [("osa", fs, 0), ("osa", fs, 1)]
                    ops = []
                    for g2 in range(2):
                        ops.append(lambda g2=g2: S.op("dve", lambda e: e.tensor_tensor(
                            out=den[fs][:, 4 * g2:4 * g2 + 4], in0=osa[fs][:, g2, :, 64], in1=self.esink[:, 4 * g2:4 * g2 + 4],
                            op=ALU.add), r=ok + ["esink"], w=[("den", fs, g2)]))
                    ops.append(lambda: S.op("dve", lambda e: e.reciprocal(rden[fs][:], den[fs][:]),
                                            r=[("den", fs, 0), ("den", fs, 1)], w=[("rden", fs)]))
                    for g2 in range(2):
                        for hi in range(4):
                            hq = 4 * g2 + hi
                            ops.append(lambda g2=g2, hi=hi, hq=hq: S.op("dve", lambda e: e.tensor_scalar(
                                out=ya[fs][:, hq, :], in0=osa[fs][:, g2, hi, 0:64], scalar1=rden[fs][:, hq:hq + 1], scalar2=None,
                                op0=ALU.mult), r=ok + [("rden", fs)], w=[("ya", fs, hq)]))
                    yav = ya[fs][:].rearrange("p h e -> p (h e)")
                    for m in range(4):
                        ops.append(lambda m=m: S.op("pe", lambda e: e.transpose(pbt7[:, m * 128:(m + 1) * 128],
                                                                                yav[:, m * 128:(m + 1) * 128], self.identb[:]),
                                                    r=[("ya", fs, 2 * m), ("ya", fs, 2 * m + 1), "identb"], w=[("pb", 7)]))
                    ops.append(lambda: S.op("dve", lambda e: e.tensor_copy(ystA[ysA][:, :, jq * 128:(jq + 1) * 128],
                                                                           pbt7[:, 0:512].rearrange("p (m q) -> p m q", m=4)),
                                            r=[("pb", 7)], w=[("ystA", ysA)]))
                    if jq == 3:
                        t0 = s0 + qb * TB
                        ops.append(lambda: S.dma("pool", self.YT[0:4, :, t0:t0 + TB].rearrange("m p t -> p m t"), ystA[ysA][:],
                                                 r=[("ystA", ysA)], w=[("YT", "a", t0)], key=("ystA_st", ysA)))
                    deferred.extend(ops)

                run_stream(len(units), 3, front, back, deferred, 4)

            accs = [(4 + idx // 3, (idx % 3) * 129) for idx in range(8)]

            def b_stream(passes):
                NE = 4
                qbs = []
                for (pi, s0, sl, h) in passes:
                    n = sl // TB
                    for qb in range(n):
                        qbs.append(dict(pi=pi, ks=pi % 2, s0=s0, sl=sl, h=h, qb=qb, nkt=sl // 128, t0=s0 + qb * TB,
                                        first=(qb == 0), last=(qb == n - 1)))
                units = [(Qi, kt) for Qi, Qd in enumerate(qbs) for kt in range(Qd["nkt"])]
                deferred = []
                pidx_of = {p[0]: k for k, p in enumerate(passes)}

                def load_kv(pi, s0, sl, h):
                    ks = pi % 2
                    nkt = sl // 128
                    S.dma("sp", kbuf[ks][:, 0:sl], self.QKT[9 + h, :, s0:s0 + sl], r=[], w=[("kbuf", ks)], key=("kbuf", ks))
                    S.dma("sp", vbuf[ks][:, 0:nkt * 129].rearrange("p (k e) -> p k e", e=129),
                          self.VB[s0:s0 + sl, h, :].rearrange("(k p) e -> p k e", p=128), r=[], w=[("vbuf", ks)],
                          key=("vbuf", ks))
                    bf = bfull[ks]
                    for d in range(-1, 5):
                        for jq in range(4):
                            o = d - jq
                            dst = bf[:, d + 1, jq * 128:(jq + 1) * 128]
                            if -1 <= o <= 1:
                                S.op("pool", lambda e, dst=dst, o=o: e.tensor_copy(dst, tbB[:, h, o + 1, :]),
                                     r=["tbB"], w=[("bfull", ks)])
                            else:
                                cc = self.col("cfar", (8 + h) * 2 + (0 if o < 0 else 1))
                                S.op("pool", lambda e, dst=dst, cc=cc: e.tensor_scalar(out=dst, in0=self.subln08[:], scalar1=0.0,
                                                                                       scalar2=cc, op0=ALU.mult, op1=ALU.add),
                                     r=["cols", "subln08"], w=[("bfull", ks)])

                def front(i):
                    Qi, kt = units[i]
                    Qd = qbs[Qi]
                    if kt == 0:
                        qs = glob["q"] % 3
                        glob["q"] += 1
                        Qd["qs"] = qs
                        S.dma("pool", qbuf[qs][:, 0:TB], self.QKT[5 + Qd["h"], :, Qd["t0"]:Qd["t0"] + TB], r=[],
                              w=[("qbuf", qs)], key=("qbuf", qs))
                    ks, qs, h = Qd["ks"], Qd["qs"], Qd["h"]
                    r2 = i % 2
                    es = i % NE
                    sp2 = self.pbb[r2]
                    for c in range(2):
                        S.op("pe", lambda e, c=c: e.matmul(sp2[:, c * TB:(c + 1) * TB], kbuf[ks][64 * c:64 * c + 64, kt * 128:(kt + 1) * 128],
                                                           qbuf[qs][64 * c:64 * c + 64, 0:TB], start=True, stop=True),
                             r=[("kbuf", ks), ("qbuf", qs)], w=[("pb", 2 * r2 + c)])
                    pk = [("pb", 2 * r2), ("pb", 2 * r2 + 1)]
                    d = kt - 4 * Qd["qb"]
                    if d < -1 or d > 4:
                        cc = self.col("cfar", (8 + h) * 2 + (0 if d < 0 else 1))
                        S.op("act", lambda e: e.activation(Eb[es][:], sp2[:], AF.Exp, bias=cc, scale=0.125),
                             r=pk + ["cols"], w=[("E", es)])
                    else:
                        ts = glob["tmp"] % 3
                        glob["tmp"] += 1
                        for c in range(2):
                            S.op("dve", lambda e, c=c: e.scalar_tensor_tensor(
                                out=tmp[ts][:, c * TB:(c + 1) * TB], in0=sp2[:, c * TB:(c + 1) * TB], scalar=0.125,
                                in1=bfull[ks][:, d + 1, :], op0=ALU.mult, op1=ALU.add),
                                r=[("pb", 2 * r2 + c), ("bfull", ks)], w=[("tmp", ts)])
                        S.op("act", lambda e: e.activation(Eb[es][:], tmp[ts][:], AF.Exp), r=[("tmp", ts)], w=[("E", es)])

                def finalize(Qi):
                    Qd = qbs[Qi]
                    h, t0 = Qd["h"], Qd["t0"]
                    while deferred:
                        deferred.pop(0)()
                    osl = Qi % 2
                    of = osb[osl][:].rearrange("p c j e -> p (c j e)")
                    for bi_, (bank, n) in enumerate(((4, 387), (5, 387), (6, 258))):
                        S.op("dve", lambda e, bi_=bi_, bank=bank, n=n: e.tensor_copy(of[:, bi_ * 387:bi_ * 387 + n],
                                                                                      self.pb[bank][:, 0:n]),
                             r=[("pb", bank)], w=[("osb", osl, bi_)])
                    ok = [("osb", osl, 0), ("osb", osl, 1), ("osb", osl, 2)]
                    ys = glob["yst"] % 2
                    glob["yst"] += 1
                    for jq in range(4):
                        fs = glob["fs"] % 2
                        glob["fs"] += 1

                        def mkops(jq=jq, fs=fs):
                            return [
                                lambda: S.op("dve", lambda e: e.reciprocal(rr[fs][:, 0:2], osb[osl][:, :, jq, 128]),
                                             r=ok, w=[("rr", fs)]),
                                lambda: S.op("dve", lambda e: e.tensor_tensor(out=rr[fs][:, 2:3], in0=rr[fs][:, 1:2],
                                                                              in1=self.neglam[:], op=ALU.mult),
                                             r=[("rr", fs), "neglam"], w=[("rr2", fs)]),
                                lambda: S.op("dve", lambda e: e.tensor_scalar(out=yf[fs][:], in0=osb[osl][:, 0, jq, 0:128],
                                                                              scalar1=rr[fs][:, 0:1], scalar2=None, op0=ALU.mult),
                                             r=ok + [("rr", fs)], w=[("yf", fs)]),
                                lambda: S.op("dve", lambda e: e.scalar_tensor_tensor(
                                    out=yf2[fs][:], in0=osb[osl][:, 1, jq, 0:128], scalar=rr[fs][:, 2:3], in1=yf[fs][:],
                                    op0=ALU.mult, op1=ALU.add), r=ok + [("rr2", fs), ("yf", fs)], w=[("yf2", fs)]),
                                lambda: S.op("dve", lambda e: e.tensor_tensor(out=junk[fs][:], in0=yf2[fs][:], in1=yf2[fs][:],
                                                                              op=ALU.mult),
                                             r=[("yf2", fs)], w=[("junk", fs)]),
                                lambda: S.op("dve", lambda e: e.reduce_sum(out=ssq[fs][:], in_=junk[fs][:], axis=AX.X),
                                             r=[("junk", fs)], w=[("ssq", fs)]),
                                lambda: S.op("act", lambda e: e.activation(ltb[fs][:], ssq[fs][:], AF.Ln, bias=self.col("eps"),
                                                                           scale=1.0 / 128),
                                             r=[("ssq", fs), "cols"], w=[("lt", fs)]),
                                lambda: S.op("act", lambda e: e.activation(rsb[fs][:], ltb[fs][:], AF.Exp, scale=-0.5),
                                             r=[("lt", fs)], w=[("rs", fs)]),
                                lambda: S.op("dve", lambda e: e.scalar_tensor_tensor(
                                    out=ynb[fs][:], in0=yf2[fs][:], scalar=rsb[fs][:, 0:1], in1=self.subln08[:],
                                    op0=ALU.mult, op1=ALU.mult), r=[("yf2", fs), ("rs", fs), "subln08"], w=[("ynb", fs)]),
                                lambda: S.op("pe", lambda e: e.transpose(pbt7[:, jq * 128:(jq + 1) * 128], ynb[fs][:],
                                                                         self.identb[:]),
                                             r=[("ynb", fs), "identb"], w=[("pb", 7)]),
                            ]
                        deferred.extend(mkops())
                    deferred.append(lambda: S.op("dve", lambda e: e.tensor_copy(yst[ys][:], pbt7[:, 0:TB]), r=[("pb", 7)],
                                                 w=[("yst", ys)]))
                    deferred.append(lambda: S.dma("pool", self.YT[4 + h, :, t0:t0 + TB], yst[ys][:], r=[("yst", ys)],
                                                  w=[("YT", h, t0)], key=("yst_st", ys)))

                def back(j):
                    Qi, kt = units[j]
                    Qd = qbs[Qi]
                    ks, nkt = Qd["ks"], Qd["nkt"]
                    es = j % NE
                    if kt == 0:
                        zero_banks((4, 5, 6))
                    vb = vbuf[ks][:, 0:nkt * 129].rearrange("p (k e) -> p k e", e=129)
                    for c in range(2):
                        for jq in range(4):
                            bank, off = accs[c * 4 + jq]
                            S.op("pe", lambda e, c=c, jq=jq, bank=bank, off=off: e.matmul(
                                self.pb[bank][:, off:off + 129], Eb[es][:, c * TB + jq * 128:c * TB + (jq + 1) * 128], vb[:, kt, :],
                                start=False, stop=(kt == nkt - 1), skip_group_check=True),
                                r=[("E", es), ("vbuf", ks)], w=[("pb", bank)])
                    pidx = pidx_of[Qd["pi"]]
                    if kt == 0 and Qd["first"] and pidx == 0 and len(passes) > 1:
                        load_kv(*passes[1])
                    if kt == nkt - 1:
                        finalize(Qi)
                        if Qd["last"] and pidx + 2 < len(passes):
                            load_kv(*passes[pidx + 2])

                load_kv(*passes[0])
                rate = max(2, -(-48 // qbs[0]["nkt"]) + 1)
                run_stream(len(units), 2, front, back, deferred, rate)

            import os
            parts = os.environ.get("P2A_PARTS", "ab")
            pi = 0
            s0 = 0
            for sl in self.seqs:
                if "a" in parts:
                    a_stream(pi, s0, sl)
                    pi += 1
                if "b" in parts:
                    passes = []
                    for h in range(4):
                        passes.append((pi, s0, sl, h))
                        pi += 1
                    b_stream(passes)
                s0 += sl
            S.emit("p2a")

    def ffn_bufs(self, sb, pfx):
        B = {"act": sb(pfx + "act", [128, FC, TB], BF16),
             "sg": [sb(pfx + "sg%d" % i, [128, TB], F32) for i in range(2)],
             "wg": [sb(pfx + "wg%d" % i, [128, 2, KC, 128], BF16) for i in range(3)],
             "wu": [sb(pfx + "wu%d" % i, [128, 2, KC, 128], BF16) for i in range(3)],
             "wd": [sb(pfx + "wd%d" % i, [128, FC, 128], BF16) for i in range(2)],
             "cg": 0, "cd": 0}
        return B

    def ffn(self, l, hT, hTk, xres, xresk, B, bg=None):
        S = self.S
        nbg = (len(bg) // (FC // 2 + KC - 2) + 1) if bg else 0
        Wg, Wu, Wd = self.W["g%d" % l], self.W["u%d" % l], self.W["d%d" % l]
        act, sg = B["act"], B["sg"]
        dslots = {}

        def load_wd(oc):
            ds = B["cd"] % 2
            B["cd"] += 1
            dslots[oc] = ds
            S.dma("sp", B["wd"][ds][:].rearrange("p k n -> p (k n)"), Wd[oc].rearrange("p k n -> p (k n)"), r=[],
                  w=[("wd", ds)], key=("wd", ds))

        for jp in range(FC // 2):
            ws = B["cg"] % 3
            B["cg"] += 1
            wg, wu = B["wg"][ws], B["wu"][ws]
            S.dma("sp", wg[:].rearrange("p j k n -> p j (k n)"), Wg[2 * jp:2 * jp + 2].rearrange("j p k n -> p j (k n)"),
                  r=[], w=[("wg", ws)], key=("wg", ws))
            S.dma("sp", wu[:].rearrange("p j k n -> p j (k n)"), Wu[2 * jp:2 * jp + 2].rearrange("j p k n -> p j (k n)"),
                  r=[], w=[("wu", ws)], key=("wu", ws))
            if bg:
                S.drain(bg, nbg)
            if jp == 8:
                load_wd(0)
            if jp == 10:
                load_wd(1)
            for jj in range(2):
                j = 2 * jp + jj
                gb = (4, 5)[j % 2]
                ub = (6, 7)[j % 2]
                for kc in range(KC):
                    S.op("pe", lambda e, wg=wg, jj=jj, kc=kc, gb=gb: e.matmul(self.pb[gb][:], wg[:, jj, kc, :], hT[:, kc, :],
                                                                             start=(kc == 0), stop=(kc == KC - 1)),
                         r=[(hTk, kc), ("wg", ws)], w=[("pb", gb)])
                for kc in range(KC):
                    S.op("pe", lambda e, wu=wu, jj=jj, kc=kc, ub=ub: e.matmul(self.pb[ub][:], wu[:, jj, kc, :], hT[:, kc, :],
                                                                             start=(kc == 0), stop=(kc == KC - 1)),
                         r=[(hTk, kc), ("wu", ws)], w=[("pb", ub)])
                ss = j % 2
                S.op("act", lambda e, ss=ss, gb=gb: e.activation(sg[ss][:], self.pb[gb][:], AF.Silu),
                     r=[("pb", gb)], w=[("sg", ss)])
                S.op("dve", lambda e, ss=ss, ub=ub, j=j: e.tensor_tensor(out=act[:, j, :], in0=self.pb[ub][:], in1=sg[ss][:],
                                                                         op=ALU.mult),
                     r=[("pb", ub), ("sg", ss)], w=[("act", j)])
        for oc in range(KC):
            if oc not in dslots:
                load_wd(oc)
            if bg and oc < KC - 2:
                S.drain(bg, nbg)
            ds = dslots[oc]
            wd = B["wd"][ds]
            db = (0, 1)[oc % 2]
            for jc in range(FC):
                S.op("pe", lambda e, wd=wd, jc=jc, db=db: e.matmul(self.pb[db][:], wd[:, jc, :], act[:, jc, :],
                                                                   start=(jc == 0), stop=(jc == FC - 1)),
                     r=[("act", jc), ("wd", ds)], w=[("pb", db)])
            S.op("dve", lambda e, oc=oc, db=db: e.tensor_tensor(out=xres[:, oc, :], in0=self.pb[db][:], in1=xres[:, oc, :],
                                                                op=ALU.add),
                 r=[("pb", db), (xresk, oc)], w=[(xresk, oc)])

    def phase2b(self):
        nc, S = self.nc, self.S
        with contextlib.ExitStack() as ps:
            sb = lambda n, sh, dt: ps.enter_context(nc.sbuf_tensor("s_" + n, list(sh), dt))
            xtok = [sb("pb_xtok%d" % i, [128, 4, D], F32) for i in range(2)]
            xT = [sb("pb_xT%d" % i, [128, KC, TB], F32) for i in range(2)]
            ytb = [sb("pb_yt%d" % i, [128, KC, TB], BF16) for i in range(2)]
            hT = [sb("pb_hT%d" % i, [128, KC, TB], BF16) for i in range(2)]
            sqb = [sb("pb_sq%d" % i, [128, TB], BF16) for i in range(2)]
            lnt = sb("pb_lnt", [128, TB], F32)
            rstd = sb("pb_rstd", [128, TB], F32)
            wout = sb("pb_wout", [128, KC, KC, 128], BF16)
            B = self.ffn_bufs(sb, "pb_")
            S.dma("sp", wout[:].rearrange("p j k n -> p j (k n)"), self.W["aout"][:].rearrange("j p k n -> p j (k n)"),
                  r=[], w=["wout"], key="wout")

            def stage_a(bi, t0):
                xs = bi % 2
                xk = ("xT", xs)
                self.load_xT(t0, xtok[xs], ("xtok", xs), xT[xs], xk, tb=(0, 1))
                S.dma("sp", ytb[xs][:], self.YT[:, :, t0:t0 + TB].rearrange("m p t -> p m t"), r=[], w=[("ytb", xs)],
                      key=("ytb", xs))
                for oc in range(KC):
                    bk = (2, 3)[oc % 2]
                    for kc in range(KC):
                        S.op("pe", lambda e, oc=oc, kc=kc, bk=bk: e.matmul(self.pb[bk][:], wout[:, oc, kc, :], ytb[xs][:, kc, :],
                                                                           start=(kc == 0), stop=(kc == KC - 1)),
                             r=[("ytb", xs), "wout"], w=[("pb", bk)])
                    S.op("dve", lambda e, oc=oc, bk=bk: e.tensor_tensor(out=xT[xs][:, oc, :], in0=self.pb[bk][:],
                                                                        in1=xT[xs][:, oc, :], op=ALU.add),
                         r=[("pb", bk), (xk, oc)], w=[(xk, oc)])
                self.rmsnorm(xT[xs], xk, hT[xs], ("hT", xs), "ffng", 0, TB, sqb, 2, (lnt, rstd), "pbn")

            def stage_b(bi, t0, bg):
                xs = bi % 2
                xk = ("xT", xs)
                self.ffn(0, hT[xs], ("hT", xs), xT[xs], xk, B, bg=bg)
                S.drain(bg)
                S.dma("pool", self.X2T[:, :, t0:t0 + TB].rearrange("c p t -> p c t"), xT[xs][:],
                      r=[(xk, c) for c in range(KC)], w=[("X2T", bi)], key=("x2t_st", xs))

            blks = self.blocks()
            stage_a(0, blks[0][3])
            for bi, (s0, sl, b, t0) in enumerate(blks):
                bg = []
                if bi + 1 < len(blks):
                    S.begin_capture()
                    stage_a(bi + 1, blks[bi + 1][3])
                    bg = S.end_capture()
                stage_b(bi, t0, bg)
            S.emit("p2b")

    def phase3(self):
        nc, S = self.nc, self.S
        with contextlib.ExitStack() as ps:
            sb = lambda n, sh, dt: ps.enter_context(nc.sbuf_tensor("s_" + n, list(sh), dt))
            xms = [sb("p3_xm%d" % i, [128, KC, TB], F32) for i in range(2)]
            xhs = [sb("p3_xh%d" % i, [128, KC, 32], F32) for i in range(2)]
            hC = sb("p3_hC", [128, KC, TB], BF16)
            hTm = sb("p3_hTm", [128, KC, TB], BF16)
            hTh = sb("p3_hTh", [128, KC, 32], BF16)
            sqb = [sb("p3_sq%d" % i, [128, TB], BF16) for i in range(2)]
            lnt = sb("p3_lnt", [128, TB], F32)
            rstd = sb("p3_rstd", [128, TB], F32)
            wc = [sb("p3_wc%d" % i, [128, KC, 128], BF16) for i in range(6)]
            ddw = sb("p3_ddw", [128, 4, 31, 128], BF16)
            dsc = sb("p3_dsc", [128, 4, 3, 128], BF16)
            gx = sb("p3_gx", [128, 4, 544], BF16)
            ub = sb("p3_u", [128, 4, 544], BF16)
            gcs = [sb("p3_gcs%d" % i, [128, TB], F32) for i in range(2)]
            hs = [sb("p3_hs%d" % i, [128, 32], F32) for i in range(2)]
            scs = sb("p3_scs", [128, TB], F32)
            vbf = [sb("p3_vbf%d" % i, [128, TB], BF16) for i in range(2)]
            vsq = [sb("p3_vsq%d" % i, [128, TB], BF16) for i in range(2)]
            yT = sb("p3_yT", [128, KC, TB], BF16)
            otok = [sb("p3_otok%d" % i, [128, D], F32) for i in range(2)]
            B = self.ffn_bufs(sb, "p3_")
            actf = B["act"][:].rearrange("p j t -> p (j t)").bitcast(F32)
            vb = actf[:, 0:4 * TB].rearrange("p (c t) -> p c t", c=4)
            vbk = lambda ch: [("act", 2 * ch), ("act", 2 * ch + 1)]
            mean = actf[:, 4 * TB:5 * TB]
            meank = [("act", 8), ("act", 9)]
            msq = actf[:, 5 * TB:6 * TB]
            msqk = [("act", 10), ("act", 11)]
            var = actf[:, 6 * TB:7 * TB]
            vark = [("act", 12), ("act", 13)]
            rs2 = actf[:, 7 * TB:8 * TB]
            rs2k = [("act", 14), ("act", 15)]
            t1 = [actf[:, (8 + i) * TB:(9 + i) * TB] for i in range(2)]
            t1k = [[("act", 16 + 2 * i), ("act", 17 + 2 * i)] for i in range(2)]
            ctr = {"wc": 0, "ot": 0}
            P, Q, H, R, C1, C2, S1, S2 = range(8)

            for ch in range(4):
                for j in range(31):
                    S.op("dve", lambda e, ch=ch, j=j: e.tensor_scalar(out=ddw[:, ch, j, :], in0=self.identb[:],
                                                                      scalar1=self.col("dww", ch * 31 + j), scalar2=None,
                                                                      op0=ALU.mult),
                         r=["identb", "cols"], w=["ddw"])
                for j in range(3):
                    S.op("dve", lambda e, ch=ch, j=j: e.tensor_scalar(out=dsc[:, ch, j, :], in0=self.identb[:],
                                                                      scalar1=self.col("scw", ch * 3 + j), scalar2=None,
                                                                      op0=ALU.mult),
                         r=["identb", "cols"], w=["dsc"])

            def load_w(src):
                i = ctr["wc"] % 6
                ctr["wc"] += 1
                S.dma("sp", wc[i][:].rearrange("p k n -> p (k n)"), src.rearrange("p k n -> p (k n)"), r=[], w=[("wc", i)],
                      key=("wc", i))
                return i

            def stage_a(bi, s0, sl, b, t0):
                xs = bi % 2
                xm, xh = xms[xs], xhs[xs]
                xk = ("xm", xs)
                xhk = ("xh", xs)
                S.dma("sp", xm[:], self.X2T[:, :, t0:t0 + TB].rearrange("c p t -> p c t"), r=[],
                      w=[(xk, c) for c in range(KC)], key=xk)
                hk = [(xhk, c) for c in range(KC)]
                if b > 0:
                    S.dma("sp", xh[:, :, 0:16], self.X2T[:, :, t0 - 16:t0].rearrange("c p t -> p c t"), r=[], w=hk,
                          key=("xhL", xs))
                else:
                    S.op("dve", lambda e: e.memset(xh[:, :, 0:16], 0.0), w=hk)
                if (b + 1) * TB < sl:
                    S.dma("sp", xh[:, :, 16:32], self.X2T[:, :, t0 + TB:t0 + TB + 16].rearrange("c p t -> p c t"), r=[], w=hk,
                          key=("xhR", xs))
                else:
                    S.op("dve", lambda e: e.memset(xh[:, :, 16:32], 0.0), w=hk)
                self.rmsnorm(xm, xk, hC, "hC", "mixg", 8, TB, sqb, H, (lnt, rstd), "p3n")
                self.rmsnorm(xh, xhk, hTh, "hTh", "mixg", 8, 32, sqb, R, (lnt, rstd), "p3n")

            def blk(bi, s0, sl, b, t0, bg):
                xm = xms[bi % 2]
                xk = ("xm", bi % 2)

                def proj(j, bank, hcol):
                    wi = load_w(self.W["cin"][j])
                    w = wc[wi]
                    for kc in range(KC):
                        S.op("pe", lambda e, w=w, kc=kc: e.matmul(self.pb[bank][:], w[:, kc, :], hC[:, kc, :],
                                                                  start=(kc == 0), stop=(kc == KC - 1)),
                             r=[("hC", kc), ("wc", wi)], w=[("pb", bank)])
                    if hcol is not None:
                        for kc in range(KC):
                            S.op("pe", lambda e, w=w, kc=kc: e.matmul(self.pb[H][:, hcol:hcol + 32], w[:, kc, :], hTh[:, kc, :],
                                                                      start=(kc == 0), stop=(kc == KC - 1)),
                                 r=[("hTh", kc), ("wc", wi)], w=[("pb", H)])

                def prod(dst, dk, ch, s, func, pa, pbk, ha, hb):
                    if func is None:
                        S.op("act", lambda e: e.copy(gcs[s][:], self.pb[pbk][:]), r=[("pb", pbk)], w=[("gcs", s)])
                        S.op("act", lambda e: e.copy(hs[s][:], self.pb[H][:, hb:hb + 32]), r=[("pb", H)], w=[("hs", s)])
                    else:
                        S.op("act", lambda e: e.activation(gcs[s][:], self.pb[pbk][:], func), r=[("pb", pbk)], w=[("gcs", s)])
                        S.op("act", lambda e: e.activation(hs[s][:], self.pb[H][:, hb:hb + 32], func), r=[("pb", H)],
                             w=[("hs", s)])
                    S.op("dve", lambda e: e.tensor_tensor(out=dst[:, ch, 16:16 + TB], in0=self.pb[pa][:], in1=gcs[s][:], op=ALU.mult),
                         r=[("pb", pa), ("gcs", s)], w=[(dk, ch)])
                    S.op("dve", lambda e: e.tensor_tensor(out=dst[:, ch, 0:16], in0=self.pb[H][:, ha:ha + 16], in1=hs[s][:, 0:16],
                                                          op=ALU.mult),
                         r=[("pb", H), ("hs", s)], w=[(dk, ch)])
                    S.op("dve", lambda e: e.tensor_tensor(out=dst[:, ch, 16 + TB:32 + TB], in0=self.pb[H][:, ha + 16:ha + 32],
                                                          in1=hs[s][:, 16:32], op=ALU.mult),
                         r=[("pb", H), ("hs", s)], w=[(dk, ch)])

                for ch in range(4):
                    s = ch % 2
                    proj(4 + ch, P, 0)
                    proj(8 + ch, Q, 32)
                    prod(gx, "gx", ch, 0, None, Q, P, 32, 0)
                    proj(12 + ch, P, 64)
                    proj(16 + ch, Q, 96)
                    prod(ub, "u", ch, 1, AF.Sigmoid, P, Q, 64, 96)
                    proj(ch, R, None)
                    for j in range(3):
                        S.op("pe", lambda e, ch=ch, j=j: e.matmul(self.pb[C1][:], dsc[:, ch, j, :], gx[:, ch, 15 + j:15 + j + TB],
                                                                  start=(j == 0), stop=(j == 2)),
                             r=[("gx", ch), "dsc"], w=[("pb", C1)])
                    S.op("act", lambda e: e.copy(scs[:], self.pb[C1][:]), r=[("pb", C1)], w=["scs"])
                    S.op("dve", lambda e, ch=ch: e.tensor_tensor(out=yT[:, ch, :], in0=self.pb[R][:], in1=scs[:], op=ALU.mult),
                         r=[("pb", R), "scs"], w=[("yT", ch)])
                    for j in range(31):
                        S.op("pe", lambda e, ch=ch, j=j: e.matmul(self.pb[C2][:], ddw[:, ch, j, :], ub[:, ch, 1 + j:1 + j + TB],
                                                                  start=(j == 0), stop=(j == 30)),
                             r=[("u", ch), "ddw"], w=[("pb", C2)])
                    bcol = self.col("dwb", ch)
                    S.op("act", lambda e, ch=ch, bcol=bcol: e.activation(vb[:, ch, :], self.pb[C2][:], AF.Identity, bias=bcol),
                         r=[("pb", C2), "cols"], w=vbk(ch))
                    S.op("act", lambda e, s=s, bcol=bcol: e.activation(vbf[s][:], self.pb[C2][:], AF.Identity, bias=bcol),
                         r=[("pb", C2), "cols"], w=[("vbf", s)])
                    S.op("act", lambda e, s=s, bcol=bcol: e.activation(vsq[s][:], self.pb[C2][:], AF.Square, bias=bcol),
                         r=[("pb", C2), "cols"], w=[("vsq", s)])
                    S.op("pe", lambda e, s=s, ch=ch: e.matmul(self.pb[S1][:], self.ones[:], vbf[s][:], start=(ch == 0), stop=(ch == 3)),
                         r=[("vbf", s), "ones"], w=[("pb", S1)])
                    S.op("pe", lambda e, s=s, ch=ch: e.matmul(self.pb[S2][:], self.ones[:], vsq[s][:], start=(ch == 0), stop=(ch == 3)),
                         r=[("vsq", s), "ones"], w=[("pb", S2)])
                S.op("dve", lambda e: e.tensor_scalar(out=mean, in0=self.pb[S1][:], scalar1=1.0 / 512, scalar2=None, op0=ALU.mult),
                     r=[("pb", S1)], w=meank)
                S.op("dve", lambda e: e.tensor_tensor(out=msq, in0=mean, in1=mean, op=ALU.mult), r=meank, w=msqk)
                S.op("dve", lambda e: e.scalar_tensor_tensor(out=var, in0=self.pb[S2][:], scalar=1.0 / 512, in1=msq,
                                                             op0=ALU.mult, op1=ALU.subtract),
                     r=[("pb", S2)] + msqk, w=vark)
                S.op("dve", lambda e: e.tensor_scalar(out=msq, in0=var, scalar1=0.0, scalar2=None, op0=ALU.max), r=vark, w=msqk)
                S.op("act", lambda e: e.activation(var, msq, AF.Ln, bias=self.col("eps")), r=msqk + ["cols"], w=vark)
                S.op("act", lambda e: e.activation(rs2, var, AF.Exp, scale=-0.5), r=vark, w=rs2k)
                for ch in range(4):
                    s = ch % 2
                    S.op("dve", lambda e, ch=ch, s=s: e.tensor_tensor(out=t1[s], in0=vb[:, ch, :], in1=mean, op=ALU.subtract),
                         r=vbk(ch) + meank, w=t1k[s])
                    S.op("dve", lambda e, s=s: e.tensor_tensor(out=gcs[s][:], in0=t1[s], in1=rs2, op=ALU.mult),
                         r=t1k[s] + rs2k, w=[("gcs", s)])
                    S.op("act", lambda e, ch=ch, s=s: e.activation(yT[:, 4 + ch, :], gcs[s][:], AF.Silu,
                                                                   bias=self.col("lnb", ch), scale=self.col("lng", ch)),
                         r=[("gcs", s), "cols"], w=[("yT", 4 + ch)])
                for oc in range(KC):
                    wi = load_w(self.W["cout"][oc])
                    w = wc[wi]
                    bk = (C1, C2)[oc % 2]
                    for kc in range(KC):
                        S.op("pe", lambda e, w=w, kc=kc, bk=bk: e.matmul(self.pb[bk][:], w[:, kc, :], yT[:, kc, :],
                                                                         start=(kc == 0), stop=(kc == KC - 1)),
                             r=[("yT", kc), ("wc", wi)], w=[("pb", bk)])
                    S.op("dve", lambda e, oc=oc, bk=bk: e.tensor_tensor(out=xm[:, oc, :], in0=self.pb[bk][:], in1=xm[:, oc, :],
                                                                        op=ALU.add),
                         r=[("pb", bk), (xk, oc)], w=[(xk, oc)])
                self.rmsnorm(xm, xk, hTm, "hTm", "ffng", 8, TB, sqb, S1, (lnt, rstd), "p3n")
                self.ffn(1, hTm, "hTm", xm, xk, B, bg=bg)
                S.drain(bg)
                for jt in range(4):
                    osl = ctr["ot"] % 2
                    ctr["ot"] += 1
                    for half in range(2):
                        bank = (H, R)[half]
                        for cc in range(4):
                            c = half * 4 + cc
                            S.op("pe", lambda e, bank=bank, cc=cc, c=c, jt=jt: e.transpose(
                                self.pb[bank][:, cc * 128:(cc + 1) * 128], xm[:, c, jt * 128:(jt + 1) * 128], self.ident[:]),
                                r=[(xk, c), "ident"], w=[("pb", bank)])
                        if half:
                            S.op("act", lambda e, bank=bank, osl=osl: e.copy(otok[osl][:, 512:1024], self.pb[bank][:]),
                                 r=[("pb", bank)], w=[("otok", osl, 1)])
                        else:
                            S.op("dve", lambda e, bank=bank, osl=osl: e.tensor_copy(otok[osl][:, 0:512], self.pb[bank][:]),
                                 r=[("pb", bank)], w=[("otok", osl, 0)])
                    S.dma("pool", self.y[t0 + jt * 128:t0 + (jt + 1) * 128, :], otok[osl][:],
                          r=[("otok", osl, 0), ("otok", osl, 1)], w=[("y", bi, jt)], key=("otok_st", osl))

            blks = self.blocks()
            stage_a(0, *blks[0])
            for bi, (s0, sl, b, t0) in enumerate(blks):
                bg = []
                if bi + 1 < len(blks):
                    S.begin_capture()
                    stage_a(bi + 1, *blks[bi + 1])
                    bg = S.end_capture()
                blk(bi, s0, sl, b, t0, bg)
            S.emit("p3")


_PROGRAM_CACHE = {}


def _get_program(seqs):
    key = tuple(seqs)
    if key not in _PROGRAM_CACHE:
        kb = KB(seqs)
        _PROGRAM_CACHE[key] = kb.build()
    return _PROGRAM_CACHE[key]


def _core_inputs(inputs, consts, xcore):
    m = dict(x=xcore,
             w_gate=np.ascontiguousarray(inputs["w_gate"], np.float32),
             w_up=np.ascontiguousarray(inputs["w_up"], np.float32),
             w_down=np.ascontiguousarray(inputs["w_down"], np.float32),
             attn_w_in=np.ascontiguousarray(inputs["attn_w_in"][0], np.float32),
             attn_w_out=np.ascontiguousarray(inputs["attn_w_out"][0], np.float32),
             conv_w_in=np.ascontiguousarray(inputs["conv_w_in"][0], np.float32),
             conv_w_out=np.ascontiguousarray(inputs["conv_w_out"][0], np.float32))
    m.update(consts)
    return m


def kernel(**inputs):
    inputs = {k: np.asarray(v) for k, v in inputs.items()}
    xp = inputs["x_prompt"]
    xs = inputs["x_sample"]
    n = N_CORES
    nsp = xs.shape[0] // n
    seqs = [xp.shape[1]] + [xs.shape[1]] * nsp
    consts = _host_consts(inputs)
    nc = _get_program(seqs)
    in_maps = []
    for c in range(n):
        xcore = np.concatenate([xp[c].reshape(-1, D)] + [xs[c * nsp + i].reshape(-1, D) for i in range(nsp)], axis=0)
        in_maps.append(_core_inputs(inputs, consts, np.ascontiguousarray(xcore, np.float32)))
    res = run_bass_kernel_spmd(nc, in_maps, core_ids=list(range(n)))
    yp = np.empty(xp.shape, np.float32)
    ys = np.empty(xs.shape, np.float32)
    sp = xp.shape[1]
    ss = xs.shape[1]
    for c in range(n):
        y = res.results[c]["y"]
        yp[c] = y[0:sp]
        for i in range(nsp):
            ys[c * nsp + i] = y[sp + i * ss:sp + (i + 1) * ss]
    return (yp, ys)
```

```python
import contextlib
import numpy as np
import ml_dtypes
import concourse.bass as bass
import concourse.mybir as mybir
from concourse.bass_utils import run_bass_kernel_spmd
from concourse.alu_op_type import AluOpType as ALU

F32, BF16 = mybir.dt.float32, mybir.dt.bfloat16
AF = mybir.ActivationFunctionType
AX = mybir.AxisListType

D = 1024
FF = 2816
KC = 8
FC = 22
TB = 512
EPS = 1e-6
ATTN_IN = 2304
CONV_IN = 2560
N_CORES = 8
OPT_CONV_BG = True
OPT_X1T = True
OPT_P3ORD = False
SEQS_FULL = (8192, 2048, 2048, 2048, 2048)

ENGS = ("pe", "act", "dve", "pool", "sp")
ENG_ATTR = {"pe": "tensor", "act": "scalar", "dve": "vector", "pool": "gpsimd", "sp": "sync"}


class Op:
    __slots__ = ("id", "eng", "fn", "deps", "dma_key", "dma_val", "needs_inc", "seq")

    def __init__(self, id, eng, fn):
        self.id = id
        self.eng = eng
        self.fn = fn
        self.deps = set()
        self.dma_key = None
        self.dma_val = 0
        self.needs_inc = False
        self.seq = 0


class Sched:
    def __init__(self, nc, stack, same_engine_sync=True):
        self.nc = nc
        self.stack = stack
        self.same_engine_sync = same_engine_sync
        self.eng_sem = {e: stack.enter_context(nc.semaphore("sem_" + e)) for e in ENGS}
        self.eng_cnt = {e: 0 for e in ENGS}
        self.dma_sem = {}
        self.dma_cnt = {}
        self.waited = {e: {} for e in ENGS}
        self.stats = []
        self.capture = None
        self._reset_phase()

    def _reset_phase(self):
        self.ops = []
        self.last_w = {}
        self.readers = {}

    def begin_capture(self):
        self.capture = []

    def end_capture(self):
        lst, self.capture = self.capture, None
        return lst

    def drain(self, lst, n=None):
        k = len(lst) if n is None else min(n, len(lst))
        for _ in range(k):
            args = lst.pop(0)
            self.op(*args)

    def op(self, eng, fn, r=(), w=(), dma_key=None):
        if self.capture is not None:
            self.capture.append((eng, fn, list(r), list(w), dma_key))
            return None
        o = Op(len(self.ops), eng, fn)
        deps = o.deps
        for k in r:
            p = self.last_w.get(k)
            if p is not None:
                deps.add(p)
            if type(k) is tuple and k[0] == "pb":
                for q in self.readers.get(k, ()):
                    if self.ops[q].eng != eng:
                        deps.add(q)
        for k in w:
            p = self.last_w.get(k)
            if p is not None:
                deps.add(p)
            rd = self.readers.get(k)
            if rd:
                deps.update(rd)
        for k in r:
            self.readers.setdefault(k, []).append(o.id)
        for k in w:
            self.last_w[k] = o.id
            self.readers[k] = []
        deps.discard(o.id)
        if dma_key is not None:
            if dma_key not in self.dma_sem:
                self.dma_sem[dma_key] = self.stack.enter_context(
                    self.nc.semaphore("dsem%d" % len(self.dma_sem)))
                self.dma_cnt[dma_key] = 0
            self.dma_cnt[dma_key] += 1
            o.dma_key = dma_key
            o.dma_val = 16 * self.dma_cnt[dma_key]
        self.ops.append(o)
        return o

    def dma(self, eng, out, in_, r, w, key):
        return self.op(eng, lambda e: e.dma_start(out=out, in_=in_), r=r, w=w, dma_key=key)

    def emit(self, name=""):
        ops = self.ops
        for o in ops:
            latest = {}
            for d in o.deps:
                p = ops[d]
                if p.dma_key is None and (p.eng != o.eng or (self.same_engine_sync and p.eng != "pe")):
                    if d > latest.get(p.eng, -1):
                        latest[p.eng] = d
            for d in latest.values():
                ops[d].needs_inc = True
        per_eng = {e: [] for e in ENGS}
        for o in ops:
            per_eng[o.eng].append(o)
        for e in ENGS:
            c = self.eng_cnt[e]
            for o in per_eng[e]:
                if o.dma_key is None and o.needs_inc:
                    c += 1
                    o.seq = c
            self.eng_cnt[e] = c
        nw = [0]
        with self.nc.Block() as block:
            for e in ENGS:
                lst = per_eng[e]
                if not lst and e != "sp":
                    continue

                def body(eng, e=e, lst=lst):
                    waited = self.waited[e]
                    for o in lst:
                        need = {}
                        for d in o.deps:
                            p = ops[d]
                            if p.dma_key is not None:
                                nm = ("d", p.dma_key)
                                sem = self.dma_sem[p.dma_key]
                                val = p.dma_val
                            else:
                                if p.eng == e and (e == "pe" or not self.same_engine_sync):
                                    continue
                                nm = ("e", p.eng)
                                sem = self.eng_sem[p.eng]
                                val = p.seq
                            if val > need.get(nm, (None, 0))[1]:
                                need[nm] = (sem, val)
                        for nm, (sem, val) in need.items():
                            if waited.get(nm, 0) >= val:
                                continue
                            eng.wait_ge(sem, val)
                            waited[nm] = val
                            nw[0] += 1
                        ins = o.fn(eng)
                        if o.dma_key is not None:
                            ins.then_inc(self.dma_sem[o.dma_key], 16)
                        elif o.needs_inc:
                            ins.then_inc(self.eng_sem[e], 1)
                    if e == "sp":
                        for key, sem in self.dma_sem.items():
                            val = 16 * self.dma_cnt[key]
                            if val and waited.get(("d", key), 0) < val:
                                eng.wait_ge(sem, val)
                                waited[("d", key)] = val
                        for e2 in ENGS:
                            if e2 != "sp" and self.eng_cnt[e2] and waited.get(("e", e2), 0) < self.eng_cnt[e2]:
                                eng.wait_ge(self.eng_sem[e2], self.eng_cnt[e2])
                                waited[("e", e2)] = self.eng_cnt[e2]

                getattr(block, ENG_ATTR[e])(body)
        self.stats.append((name, len(ops), nw[0], {e: len(per_eng[e]) for e in ENGS}))
        self._reset_phase()


def _bucket_table():
    import math
    import jax
    import jax.numpy as jnp
    with jax.default_device(jax.devices("cpu")[0]):
        rel = jnp.arange(-255, 256, dtype=jnp.int32)
        half = 16
        max_exact = 8
        n = jnp.abs(rel)
        large = max_exact + (jnp.log(jnp.maximum(n, 1).astype(jnp.float32) / max_exact)
                             / math.log(128 / max_exact) * (half - max_exact)).astype(jnp.int32)
        large = jnp.minimum(large, half - 1)
        b = jnp.where(rel > 0, half, 0) + jnp.where(n < max_exact, n, large)
        return np.asarray(b)


COL_SPEC = [("mixg", 16), ("ffng", 16), ("aq", 1), ("ak", 1), ("bq", 1), ("bk", 1), ("eps", 1),
            ("cfar", 24), ("sink", 8), ("lam", 256), ("subln", 128), ("scw", 12), ("dww", 124),
            ("dwb", 4), ("lng", 4), ("lnb", 4)]
COL_OFF = {}
_o = 0
for _n, _w in COL_SPEC:
    COL_OFF[_n] = _o
    _o += _w
NCOL = _o


def _pack_cols(inp):
    c = np.zeros((128, NCOL), np.float32)

    def put(name, arr):
        arr = np.asarray(arr, np.float32)
        c[:, COL_OFF[name]:COL_OFF[name] + arr.shape[1]] = arr

    fm = lambda v, nch: np.asarray(v, np.float32).reshape(nch, 128).T
    put("mixg", np.concatenate([fm(inp["mix_norm"][l], 8) for l in range(2)], axis=1))
    put("ffng", np.concatenate([fm(inp["ffn_norm"][l], 8) for l in range(2)], axis=1))
    for nm, key in (("aq", "a_q_norm"), ("ak", "a_k_norm"), ("bq", "b_q_norm"), ("bk", "b_k_norm")):
        v = np.asarray(inp[key], np.float32).reshape(64)
        put(nm, np.concatenate([v, v])[:, None])
    put("eps", np.full((128, 1), EPS, np.float32))
    rb = np.asarray(inp["rel_bias"], np.float32)
    cf = np.stack([rb[15, :], rb[31, :]], axis=1).reshape(1, 24)
    put("cfar", np.broadcast_to(cf, (128, 24)))
    put("sink", np.broadcast_to(np.asarray(inp["a_sink"], np.float32).reshape(1, 8), (128, 8)))
    put("lam", np.broadcast_to(np.asarray(inp["b_lambda"], np.float32).reshape(1, 256), (128, 256)))
    put("subln", np.broadcast_to(np.asarray(inp["b_subln"], np.float32).reshape(1, 128), (128, 128)))
    scw = np.asarray(inp["short_conv_w"], np.float32).reshape(3, 4, 128)
    put("scw", scw.transpose(2, 1, 0).reshape(128, 12))
    dww = np.asarray(inp["conf_dw_w"], np.float32).reshape(31, 4, 128)
    put("dww", dww.transpose(2, 1, 0).reshape(128, 124))
    put("dwb", fm(np.asarray(inp["conf_dw_b"]).reshape(512), 4))
    put("lng", fm(np.asarray(inp["conf_ln_g"]).reshape(512), 4))
    put("lnb", fm(np.asarray(inp["conf_ln_b"]).reshape(512), 4))
    return c


def _host_consts(inp):
    bt = _bucket_table()
    k = np.arange(128)[:, None]
    q = np.arange(128)[None, :]
    rb = np.asarray(inp["rel_bias"], np.float32)
    tb = np.zeros((128, 12, 3, 128), np.float32)
    mk = np.zeros((128, 3, 128), np.float32)
    for oi, o in enumerate((-1, 0, 1)):
        rel = 128 * o + k - q
        idx = bt[rel + 255]
        tb[:, :, oi, :] = rb[idx].transpose(0, 2, 1)
        mk[:, oi, :] = np.where(np.abs(rel) <= 128, 0.0, -1e30)
    return {"cols": _pack_cols(inp), "tbias": tb, "maskA": mk,
            "ident": np.eye(128, dtype=np.float32)}


def _attn_groups():
    g = []
    for i in range(4):
        g += [i, i + 4]
    g += [8, 9]
    g += [10, 11]
    for h in range(4):
        g += [12 + 2 * h, 13 + 2 * h]
    for h in range(4):
        g += [20 + 2 * h, 21 + 2 * h]
    for h in range(4):
        g += [28 + 2 * h, 29 + 2 * h]
    return g


QK_CHUNKS = [0, 1, 2, 3, 4, 6, 7, 8, 9, 10, 11, 12, 13]
QK_GAIN = ["aq"] * 4 + ["ak"] + ["bq"] * 4 + ["bk"] * 4


class KB:
    def __init__(self, seqs, dbg=False, phases=("p0", "p1", "p2a", "p2b", "p3")):
        self.seqs = list(seqs)
        self.ntok = sum(seqs)
        self.dbg = dbg
        self.phases = phases
        self.nc = bass.Bass("TRN2", target_bir_lowering=False)

    def din(self, name, shape, dt=F32):
        return self.nc.dram_tensor(name, list(shape), dt, kind="ExternalInput").ap()

    def dscr(self, name, shape, dt):
        kind = "ExternalOutput" if self.dbg else "Internal"
        return self.nc.dram_tensor(name, list(shape), dt, kind=kind).ap()

    def build(self):
        nc = self.nc
        NT = self.ntok
        self.x = self.din("x", [NT, D])
        self.w_gate = self.din("w_gate", [2, D, FF])
        self.w_up = self.din("w_up", [2, D, FF])
        self.w_down = self.din("w_down", [2, FF, D])
        self.attn_w_in = self.din("attn_w_in", [D, ATTN_IN])
        self.attn_w_out = self.din("attn_w_out", [D, D])
        self.conv_w_in = self.din("conv_w_in", [D, CONV_IN])
        self.conv_w_out = self.din("conv_w_out", [D, D])
        self.d_cols = self.din("cols", [128, NCOL])
        self.d_tbias = self.din("tbias", [128, 12, 3, 128])
        self.d_maskA = self.din("maskA", [128, 3, 128])
        self.d_ident = self.din("ident", [128, 128])
        self.y = nc.dram_tensor("y", [NT, D], F32, kind="ExternalOutput").ap()
        self.W = {
            "ain": self.dscr("wb_ain", [18, 128, KC, 128], BF16),
            "aout": self.dscr("wb_aout", [8, 128, KC, 128], BF16),
            "cin": self.dscr("wb_cin", [20, 128, KC, 128], BF16),
            "cout": self.dscr("wb_cout", [8, 128, KC, 128], BF16),
        }
        for l in range(2):
            self.W["g%d" % l] = self.dscr("wb_g%d" % l, [FC, 128, KC, 128], BF16)
            self.W["u%d" % l] = self.dscr("wb_u%d" % l, [FC, 128, KC, 128], BF16)
            self.W["d%d" % l] = self.dscr("wb_d%d" % l, [8, 128, FC, 128], BF16)
        self.QKT = self.dscr("qkt", [13, 128, NT], BF16)
        self.VB = self.dscr("vbs", [NT, 4, 129], BF16)
        self.VA = self.dscr("vas", [NT, 2, 65], BF16)
        self.YT = self.dscr("yt", [8, 128, NT], BF16)
        self.X2T = self.dscr("x2t", [8, 128, NT], F32)
        self.X1T = self.dscr("x1t", [8, 128, NT], F32)

        with contextlib.ExitStack() as st:
            self.st = st
            self.S = Sched(nc, st)
            sb = lambda n, sh, dt: st.enter_context(nc.sbuf_tensor("s_" + n, list(sh), dt))
            self.pbb = [st.enter_context(nc.psum_tensor("pbb%d" % i, [128, 1024], F32)) for i in range(4)]
            self.pb = [self.pbb[i // 2][:, (i % 2) * 512:(i % 2 + 1) * 512] for i in range(8)]
            self.ident = sb("ident", [128, 128], F32)
            self.identb = sb("identb", [128, 128], BF16)
            self.ones = sb("ones", [128, 128], BF16)
            self.bones = sb("bones", [128, 128], BF16)
            self.zer = sb("zer", [128, 128], BF16)
            self.anyr = sb("anyr", [128, 512], BF16)
            self.cols = sb("cols", [128, NCOL], F32)
            self.neglam = sb("neglam", [128, 1], F32)
            self.subln08 = sb("subln08", [128, 128], F32)
            self.esink = sb("esink", [128, 8], F32)
            if "p0" in self.phases:
                self.phase0()
            if "p1" in self.phases:
                self.phase1()
            if "p2a" in self.phases:
                self.phase2a()
            if "p2b" in self.phases:
                self.phase2b()
            if "p3" in self.phases:
                self.phase3()
        return nc

    def col(self, name, i=0, n=1):
        o = COL_OFF[name] + i
        return self.cols[:, o:o + n]

    def phase0(self, weights=True):
        nc, S = self.nc, self.S
        with contextlib.ExitStack() as ps:
            sb = lambda n, sh, dt: ps.enter_context(nc.sbuf_tensor("s_" + n, list(sh), dt))
            S.dma("sp", self.ident[:], self.d_ident, r=[], w=["ident"], key="ident")
            S.dma("sp", self.cols[:], self.d_cols, r=[], w=["cols"], key="cols")
            S.op("dve", lambda e: e.tensor_copy(self.identb[:], self.ident[:]), r=["ident"], w=["identb"])
            S.op("pool", lambda e: e.memset(self.ones[:], 1.0), w=["ones"])
            S.op("pool", lambda e: e.memset(self.zer[:], 0.0), w=["zer"])
            S.op("pool", lambda e: e.memset(self.anyr[:], 1.0), w=["anyr"])
            S.op("pool", lambda e: e.memset(self.bones[:], 0.0), w=["bones"])
            S.op("pool", lambda e: e.memset(self.bones[0:64, 0:64], 1.0), w=["bones"])
            S.op("pool", lambda e: e.memset(self.bones[64:128, 64:128], 1.0), w=["bones"])
            lt = sb("p0_lt", [128, 2, 64], F32)
            ls = sb("p0_ls", [128, 2], F32)
            le = sb("p0_le", [128, 2], F32)
            lam = self.col("lam", 0, 256)
            for i in range(2):
                S.op("dve", lambda e, i=i: e.tensor_tensor(out=lt[:, i, :], in0=lam[:, (2 * i) * 64:(2 * i + 1) * 64],
                                                          in1=lam[:, (2 * i + 1) * 64:(2 * i + 2) * 64], op=ALU.mult),
                     r=["cols"], w=[("lt", i)])
                S.op("dve", lambda e, i=i: e.reduce_sum(out=ls[:, i:i + 1], in_=lt[:, i, :], axis=AX.X),
                     r=[("lt", i)], w=[("ls", i)])
            S.op("act", lambda e: e.activation(le[:], ls[:], AF.Exp), r=[("ls", 0), ("ls", 1)], w=["le"])
            S.op("dve", lambda e: e.scalar_tensor_tensor(out=self.neglam[:], in0=le[:, 1:2], scalar=-0.2, in1=le[:, 0:1],
                                                         op0=ALU.add, op1=ALU.subtract),
                 r=["le"], w=["neglam"])
            S.op("dve", lambda e: e.tensor_scalar(out=self.subln08[:], in0=self.col("subln", 0, 128), scalar1=0.8, scalar2=None,
                                                  op0=ALU.mult),
                 r=["cols"], w=["subln08"])
            S.op("act", lambda e: e.activation(self.esink[:], self.col("sink", 0, 8), AF.Exp), r=["cols"], w=["esink"])

            if weights:
                stf = [sb("p0_stf%d" % i, [128, 4096], F32) for i in range(2)]
                stb = [sb("p0_stb%d" % i, [128, 4096], BF16) for i in range(2)]
                self.convert(stf, stb, self.attn_w_in, self.W["ain"], 8, groups=_attn_groups(), cast="mix")
                if not OPT_CONV_BG:
                    self.convert(stf, stb, self.attn_w_out, self.W["aout"], 8, cast="mix")
                    for l in range(2):
                        self.convert(stf, stb, self.w_gate[l], self.W["g%d" % l], 8, cast="mix")
                        self.convert(stf, stb, self.w_up[l], self.W["u%d" % l], 8, cast="mix")
                        self.convert(stf, stb, self.w_down[l], self.W["d%d" % l], FC, cast="mix")
                    self.convert(stf, stb, self.conv_w_in, self.W["cin"], 8, cast="mix")
                    self.convert(stf, stb, self.conv_w_out, self.W["cout"], 8, cast="mix")
            S.emit("p0")

    def convert(self, stf, stb, src2d, dst, kc, groups=None, cast="dve"):
        S = self.S
        nsl = len(stf)
        if not hasattr(self, "_cvt"):
            self._cvt = 0
        nch = dst.shape[0]
        G = 4 if kc == 8 else 1
        for j0 in range(0, nch, G):
            g = min(G, nch - j0)
            i = self._cvt % nsl
            self._cvt += 1
            n = g * kc * 128
            f4 = stf[i][:, 0:n].rearrange("p (g k n) -> p g k n", g=g, k=kc)
            if groups is None:
                for gi in range(g):
                    j = j0 + gi
                    src = src2d[:, j * 128:(j + 1) * 128].rearrange("(k p) n -> p k n", p=128)
                    S.dma("sp", f4[:, gi], src, r=[], w=[("stf", i, gi, 0), ("stf", i, gi, 1)], key=("stf", i, gi, 0))
            else:
                for gi in range(g):
                    for hf in range(2):
                        gg = groups[2 * (j0 + gi) + hf]
                        src = src2d[:, gg * 64:(gg + 1) * 64].rearrange("(k p) n -> p k n", p=128)
                        S.dma("sp", f4[:, gi, :, hf * 64:(hf + 1) * 64], src, r=[],
                              w=[("stf", i, gi, hf)], key=("stf", i, gi, hf))
            rk = [("stf", i, gi, hf) for gi in range(g) for hf in range(2)]
            eng = "dve" if (cast == "dve" or self._cvt % 2) else "act"
            if eng == "act":
                S.op("act", lambda e, i=i, n=n: e.copy(stb[i][:, 0:n], stf[i][:, 0:n]), r=rk, w=[("stb", i)])
            else:
                S.op("dve", lambda e, i=i, n=n: e.tensor_copy(stb[i][:, 0:n], stf[i][:, 0:n]), r=rk, w=[("stb", i)])
            dd = dst[j0:j0 + g].rearrange("g p k n -> p g (k n)")
            S.dma("pool", dd, stb[i][:, 0:n].rearrange("p (g m) -> p g m", g=g), r=[("stb", i)],
                  w=[("wscr", self._cvt)], key=("stb_st", i))

    def blocks(self):
        out = []
        s0 = 0
        for sl in self.seqs:
            for b in range(sl // TB):
                out.append((s0, sl, b, s0 + b * TB))
            s0 += sl
        return out

    def load_xT(self, t0, xtok, xtk, xT, xTk, tb=(0, 1)):
        S = self.S
        src = self.x[t0:t0 + TB].rearrange("(j p) f -> p j f", p=128)
        S.dma("sp", xtok[:], src, r=[], w=[xtk], key=xtk)
        for c in range(KC):
            bi = tb[c % 2]
            bank = self.pb[bi]
            for j in range(4):
                S.op("pe", lambda e, bank=bank, j=j, c=c: e.transpose(bank[:, j * 128:(j + 1) * 128],
                                                                     xtok[:, j, c * 128:(c + 1) * 128], self.ident[:]),
                     r=[xtk, "ident"], w=[("pb", bi)])
            S.op("act", lambda e, bank=bank, c=c: e.copy(xT[:, c, :], bank[:]), r=[("pb", bi)], w=[(xTk, c)])

    def rmsnorm(self, xT, xTk, hT, hTk, gname, gidx, N, sqb, ssb, tmp, tmpk):
        S = self.S
        ss = self.pb[ssb]
        lnt, rstd = tmp
        for c in range(KC):
            sq = sqb[c % 2]
            sqk = (tmpk, "sq", c % 2)
            S.op("act", lambda e, sq=sq, c=c: e.activation(sq[:, 0:N], xT[:, c, 0:N], AF.Square), r=[(xTk, c)], w=[sqk])
            S.op("pe", lambda e, sq=sq, c=c: e.matmul(ss[:, 0:N], self.ones[:], sq[:, 0:N], start=(c == 0), stop=(c == KC - 1)),
                 r=[sqk, "ones"], w=[("pb", ssb)])
        S.op("act", lambda e: e.activation(lnt[:, 0:N], ss[:, 0:N], AF.Ln, bias=self.col("eps"), scale=1.0 / D),
             r=[("pb", ssb), "cols"], w=[(tmpk, "lnt")])
        S.op("act", lambda e: e.activation(rstd[:, 0:N], lnt[:, 0:N], AF.Exp, scale=-0.5), r=[(tmpk, "lnt")], w=[(tmpk, "rstd")])
        for c in range(KC):
            S.op("dve", lambda e, c=c: e.scalar_tensor_tensor(out=hT[:, c, 0:N], in0=xT[:, c, 0:N],
                                                              scalar=self.col(gname, gidx + c), in1=rstd[:, 0:N],
                                                              op0=ALU.mult, op1=ALU.mult),
                 r=[(xTk, c), (tmpk, "rstd"), "cols"], w=[(hTk, c)])

    def phase1(self):
        nc, S = self.nc, self.S
        with contextlib.ExitStack() as ps:
            sb = lambda n, sh, dt: ps.enter_context(nc.sbuf_tensor("s_" + n, list(sh), dt))
            xtok = [sb("p1_xtok%d" % i, [128, 4, D], F32) for i in range(2)]
            xT = sb("p1_xT", [128, KC, TB], F32)
            hT = [sb("p1_hT%d" % i, [128, KC, TB], BF16) for i in range(2)]
            sqb = [sb("p1_sq%d" % i, [128, TB], BF16) for i in range(2)]
            lnt = sb("p1_lnt", [128, TB], F32)
            rstd = sb("p1_rstd", [128, TB], F32)
            win = sb("p1_win", [128, 18, KC, 128], BF16)
            sqq = [sb("p1_sqq%d" % i, [128, TB], BF16) for i in range(2)]
            lq = [sb("p1_lq%d" % i, [128, TB], F32) for i in range(2)]
            rq = [sb("p1_rq%d" % i, [128, TB], F32) for i in range(2)]
            qst = [sb("p1_qst%d" % i, [128, TB], BF16) for i in range(3)]
            qf = [sb("p1_qf%d" % i, [128, TB], F32) for i in range(2)]
            vast = [sb("p1_vast%d" % i, [128, 4, 2, 65], BF16) for i in range(2)]
            vbst = [sb("p1_vbst%d" % i, [128, 4, 4, 129], BF16) for i in range(2)]
            cstf = [sb("p1_stf%d" % i, [128, 4096], F32) for i in range(2)]
            cstb = [sb("p1_stb%d" % i, [128, 4096], BF16) for i in range(2)]
            S.begin_capture()
            if OPT_CONV_BG:
                self.convert(cstf, cstb, self.attn_w_out, self.W["aout"], 8)
                for l in range(2):
                    self.convert(cstf, cstb, self.w_gate[l], self.W["g%d" % l], 8)
                    self.convert(cstf, cstb, self.w_up[l], self.W["u%d" % l], 8)
                    self.convert(cstf, cstb, self.w_down[l], self.W["d%d" % l], FC)
                self.convert(cstf, cstb, self.conv_w_in, self.W["cin"], 8)
                self.convert(cstf, cstb, self.conv_w_out, self.W["cout"], 8)
            wbg = S.end_capture()
            nblk = len(self.blocks())
            wrate = -(-len(wbg) // max(1, (nblk - 1) * 18)) if nblk > 1 else len(wbg)
            for j0 in range(0, 18, 6):
                S.dma("sp", win[:, j0:j0 + 6].rearrange("p j k n -> p j (k n)"),
                      self.W["ain"][j0:j0 + 6].rearrange("j p k n -> p j (k n)"), r=[], w=[("win", j0)], key=("win", j0))
            wink = [("win", 0), ("win", 6), ("win", 12)]
            for i in range(2):
                S.op("pool", lambda e, i=i: e.memset(vast[i][:], 1.0), w=[("vast", i)])
                S.op("pool", lambda e, i=i: e.memset(vbst[i][:], 1.0), w=[("vbst", i)])
            T0, T1, SSB, Q0, Q1, PSB, VAB, VBB = range(8)
            def stage_a(bi, t0):
                hk = ("hT", bi % 2)
                h = hT[bi % 2]
                self.load_xT(t0, xtok[bi % 2], ("xtok", bi % 2), xT, "xT", tb=(T0, T1))
                if OPT_X1T:
                    S.dma("pool", self.X1T[:, :, t0:t0 + TB].rearrange("c p t -> p c t"), xT[:],
                          r=[("xT", c) for c in range(KC)], w=[("X1T", bi)], key="x1t_st")
                self.rmsnorm(xT, "xT", h, hk, "mixg", 0, TB, sqb, SSB, (lnt, rstd), "p1n")

            def blk(bi, s0, sl, b, t0, bg):
                hk = ("hT", bi % 2)
                h = hT[bi % 2]
                PSS = (PSB, VAB)

                def front(ci):
                    j = QK_CHUNKS[ci]
                    qi = (Q0, Q1)[ci % 2]
                    s2 = ci % 2
                    for kc in range(KC):
                        S.op("pe", lambda e, kc=kc: e.matmul(self.pb[qi][:], win[:, j, kc, :], h[:, kc, :],
                                                             start=(kc == 0), stop=(kc == KC - 1)),
                             r=[(hk, kc)] + wink, w=[("pb", qi)])
                    S.op("act", lambda e: e.activation(sqq[s2][:], self.pb[qi][:], AF.Square), r=[("pb", qi)], w=[("sqq", s2)])
                    S.op("dve", lambda e: e.tensor_copy(qf[s2][:], self.pb[qi][:]), r=[("pb", qi)], w=[("qf", s2)])
                    S.op("pe", lambda e: e.matmul(self.pb[PSS[s2]][:], self.bones[:], sqq[s2][:], start=True, stop=True),
                         r=[("sqq", s2), "bones"], w=[("pb", PSS[s2])])

                def back(ci):
                    s2 = ci % 2
                    s3 = ci % 3
                    gn = QK_GAIN[ci]
                    S.op("act", lambda e: e.activation(lq[s2][:], self.pb[PSS[s2]][:], AF.Ln, bias=self.col("eps"), scale=1.0 / 64),
                         r=[("pb", PSS[s2]), "cols"], w=[("lq", s2)])
                    S.op("act", lambda e: e.activation(rq[s2][:], lq[s2][:], AF.Exp, scale=-0.5), r=[("lq", s2)], w=[("rq", s2)])
                    S.op("dve", lambda e: e.scalar_tensor_tensor(out=qst[s3][:], in0=qf[s2][:], scalar=self.col(gn), in1=rq[s2][:],
                                                                 op0=ALU.mult, op1=ALU.mult),
                         r=[("qf", s2), ("rq", s2), "cols"], w=[("qst", s3)])
                    S.dma("pool", self.QKT[ci, :, t0:t0 + TB], qst[s3][:], r=[("qst", s3)], w=[("QKT", ci, bi)],
                          key=("qst_st", s3))

                for ci in range(len(QK_CHUNKS) + 1):
                    S.drain(bg, 5)
                    S.drain(wbg, wrate)
                    if ci < len(QK_CHUNKS):
                        front(ci)
                    if ci >= 1:
                        back(ci - 1)
                vs = bi % 2
                for jt in range(4):
                    S.drain(bg, 5)
                    S.drain(wbg, wrate)
                    for kc in range(KC):
                        S.op("pe", lambda e, jt=jt, kc=kc: e.matmul(self.pb[VAB][:, jt * 128:(jt + 1) * 128],
                                                                    h[:, kc, jt * 128:(jt + 1) * 128], win[:, 5, kc, :],
                                                                    start=(kc == 0), stop=(kc == KC - 1)),
                             r=[(hk, kc)] + wink, w=[("pb", VAB)])
                    for kc in range(KC):
                        S.op("pe", lambda e, jt=jt, kc=kc: e.matmul(self.pb[VBB][:].rearrange("p (h n) -> p h n", h=4),
                                                                    h[:, kc, jt * 128:(jt + 1) * 128], win[:, 14:18, kc, :],
                                                                    start=(kc == 0), stop=(kc == KC - 1)),
                             r=[(hk, kc)] + wink, w=[("pb", VBB)])
                    S.op("dve" if jt % 2 else "act",
                         (lambda e, jt=jt, vs=vs: e.tensor_copy(vbst[vs][:, jt, :, 0:128], self.pb[VBB][:].rearrange("p (h n) -> p h n", h=4)))
                         if jt % 2 else
                         (lambda e, jt=jt, vs=vs: e.copy(vbst[vs][:, jt, :, 0:128], self.pb[VBB][:].rearrange("p (h n) -> p h n", h=4))),
                         r=[("pb", VBB)], w=[("vbst", vs)])
                S.op("dve", lambda e, vs=vs: e.tensor_copy(vast[vs][:, :, :, 0:64],
                                                           self.pb[VAB][:].rearrange("p (j g n) -> p j g n", j=4, g=2)),
                     r=[("pb", VAB)], w=[("vast", vs)])
                S.dma("pool", self.VB[t0:t0 + TB].rearrange("(j p) h e -> p j (h e)", p=128),
                      vbst[vs][:].rearrange("p j h e -> p j (h e)"), r=[("vbst", vs)], w=[("VB", bi)], key=("vbst_st", vs))
                S.dma("pool", self.VA[t0:t0 + TB].rearrange("(j p) g e -> p j (g e)", p=128),
                      vast[vs][:].rearrange("p j g e -> p j (g e)"), r=[("vast", vs)], w=[("VA", bi)], key=("vast_st", vs))

                S.drain(bg)

            blks = self.blocks()
            stage_a(0, blks[0][3])
            for bi, (s0, sl, b, t0) in enumerate(blks):
                bg = []
                if bi + 1 < len(blks):
                    S.begin_capture()
                    stage_a(bi + 1, blks[bi + 1][3])
                    bg = S.end_capture()
                blk(bi, s0, sl, b, t0, bg)
            S.drain(wbg)
            S.emit("p1")

    def phase2a(self):
        nc, S = self.nc, self.S
        SMAX = max(self.seqs)
        NKT = SMAX // 128
        with contextlib.ExitStack() as ps:
            sb = lambda n, sh, dt: ps.enter_context(nc.sbuf_tensor("s_" + n, list(sh), dt))
            tbB = sb("p2_tbB", [128, 4, 3, 128], F32)
            TA = sb("p2_TA", [128, 2, 3, 4, 128], F32)
            mk = sb("p2_mk", [128, 3, 128], F32)
            bfull = [sb("p2_bf%d" % i, [128, 6, TB], F32) for i in range(2)]
            kbuf = [sb("p2_k%d" % i, [128, SMAX], BF16) for i in range(2)]
            vbuf = [sb("p2_v%d" % i, [128, NKT * 130], BF16) for i in range(2)]
            qbuf = [sb("p2_q%d" % i, [128, 4 * TB], BF16) for i in range(3)]
            Eb = [sb("p2_E%d" % i, [128, 2 * TB], BF16) for i in range(4)]
            tmp = [sb("p2_tmp%d" % i, [128, 2 * TB], F32) for i in range(3)]
            osb = [sb("p2_osb%d" % i, [128, 2, 4, 129], F32) for i in range(2)]
            osa = [sb("p2_osa%d" % i, [128, 2, 4, 65], F32) for i in range(2)]
            rr = [sb("p2_rr%d" % i, [128, 4], F32) for i in range(2)]
            yf = [sb("p2_yf%d" % i, [128, 128], F32) for i in range(2)]
            yf2 = [sb("p2_yg%d" % i, [128, 128], F32) for i in range(2)]
            junk = [sb("p2_jk%d" % i, [128, 128], F32) for i in range(2)]
            ssq = [sb("p2_ssq%d" % i, [128, 1], F32) for i in range(2)]
            ltb = [sb("p2_lt%d" % i, [128, 1], F32) for i in range(2)]
            rsb = [sb("p2_rs%d" % i, [128, 1], F32) for i in range(2)]
            ynb = [sb("p2_ynb%d" % i, [128, 128], BF16) for i in range(2)]
            yst = [sb("p2_yst%d" % i, [128, TB], BF16) for i in range(2)]
            ystA = [sb("p2_ystA%d" % i, [128, 4, TB], BF16) for i in range(2)]
            ya = [sb("p2_ya%d" % i, [128, 8, 64], BF16) for i in range(2)]
            den = [sb("p2_den%d" % i, [128, 8], F32) for i in range(2)]
            rden = [sb("p2_rden%d" % i, [128, 8], F32) for i in range(2)]
            pbt7 = self.pb[7].bitcast(BF16)
            glob = {"q": 0, "yst": 0, "ystA": 0, "fs": 0, "tmp": 0, "pass": 0}

            S.dma("sp", tbB[:], self.d_tbias[:, 8:12], r=[], w=["tbB"], key="tbB")
            for g in range(2):
                for o in range(3):
                    S.dma("sp", TA[:, g, o], self.d_tbias[:, 4 * g:4 * g + 4, o, :], r=[], w=[("TA", g)], key=("TA", g, o))
            S.dma("sp", mk[:], self.d_maskA, r=[], w=["mk"], key="mk")
            for g in range(2):
                for o in range(3):
                    for i in range(4):
                        S.op("dve", lambda e, g=g, o=o, i=i: e.tensor_tensor(out=TA[:, g, o, i, :], in0=TA[:, g, o, i, :],
                                                                             in1=mk[:, o, :], op=ALU.add),
                             r=[("TA", g), "mk"], w=[("TA", g)])

            def zero_banks(banks):
                for b in banks:
                    S.op("pe", lambda e, b=b: e.matmul(self.pb[b], self.zer[:], self.anyr[:], start=True, stop=True,
                                                       skip_group_check=True),
                         r=["zer", "anyr"], w=[("pb", b)])

            def run_stream(nunits, la, front, back, deferred, rate):
                for i in range(nunits + la):
                    if i < nunits:
                        front(i)
                    if i >= la:
                        back(i - la)
                    for _ in range(rate):
                        if deferred:
                            deferred.pop(0)()
                while deferred:
                    deferred.pop(0)()

            def a_stream(pi, s0, sl):
                ks = pi % 2
                nkt = sl // 128
                S.dma("sp", kbuf[ks][:, 0:sl], self.QKT[4, :, s0:s0 + sl], r=[], w=[("kbuf", ks)], key=("kbuf", ks))
                S.dma("sp", vbuf[ks][:, 0:nkt * 130].rearrange("p (k e) -> p k e", e=130),
                      self.VA[s0:s0 + sl].rearrange("(k p) g e -> p k (g e)", p=128), r=[], w=[("vbuf", ks)],
                      key=("vbuf", ks))
                va = vbuf[ks][:, 0:nkt * 130].rearrange("p (k e) -> p k e", e=130)
                units = []
                for t in range(nkt):
                    us = [(t, g, o) for g in range(2) for o in (-1, 0, 1) if 0 <= t + o < nkt]
                    for k, (t_, g, o) in enumerate(us):
                        units.append(dict(t=t, g=g, o=o, first=(k == 0), last=(k == len(us) - 1),
                                          glast=(o == max(oo for (_, gg, oo) in us if gg == g))))
                deferred = []
                qslot = {}
                yslot = {}

                def front(i):
                    u = units[i]
                    t, g, o = u["t"], u["g"], u["o"]
                    qb, jq = t // 4, t % 4
                    if u["first"] and jq == 0:
                        qs = glob["q"] % 3
                        glob["q"] += 1
                        qslot[qb] = qs
                        t0 = s0 + qb * TB
                        S.dma("pool", qbuf[qs][:].rearrange("p (i t) -> p i t", i=4),
                              self.QKT[0:4, :, t0:t0 + TB].rearrange("i p t -> p i t"), r=[], w=[("qbuf", qs)], key=("qbuf", qs))
                    qs = qslot[qb]
                    q4 = qbuf[qs][:].rearrange("p (i t) -> p i t", i=4)
                    kt = t + o
                    sbk = i % 4
                    S.op("pe", lambda e: e.matmul(self.pb[sbk].rearrange("p (i q) -> p i q", i=4),
                                                  kbuf[ks][64 * g:64 * g + 64, kt * 128:(kt + 1) * 128],
                                                  q4[64 * g:64 * g + 64, :, jq * 128:(jq + 1) * 128], start=True, stop=True),
                         r=[("kbuf", ks), ("qbuf", qs)], w=[("pb", sbk)])
                    ts = glob["tmp"] % 3
                    glob["tmp"] += 1
                    es = i % 4
                    u["es"] = es
                    S.op("dve", lambda e: e.scalar_tensor_tensor(out=tmp[ts][:, 0:TB], in0=self.pb[sbk], scalar=0.125,
                                                                 in1=TA[:, g, o + 1].rearrange("p i q -> p (i q)"),
                                                                 op0=ALU.mult, op1=ALU.add),
                         r=[("pb", sbk), ("TA", g)], w=[("tmp", ts)])
                    S.op("act", lambda e: e.activation(Eb[es][:, 0:TB], tmp[ts][:, 0:TB], AF.Exp), r=[("tmp", ts)], w=[("E", es)])

                def back(i):
                    u = units[i]
                    t, g, o, es = u["t"], u["g"], u["o"], u["es"]
                    qb, jq = t // 4, t % 4
                    kt = t + o
                    if u["first"]:
                        zero_banks((4, 5))
                    for hi in range(4):
                        S.op("pe", lambda e, hi=hi: e.matmul(self.pb[4 + g][:, hi * 65:(hi + 1) * 65], Eb[es][:, hi * 128:(hi + 1) * 128],
                                                             va[:, kt, g * 65:(g + 1) * 65], start=False, stop=u["glast"],
                                                             skip_group_check=True),
                             r=[("E", es), ("vbuf", ks)], w=[("pb", 4 + g)])
                    if not u["last"]:
                        return
                    while deferred:
                        deferred.pop(0)()
                    fs = glob["fs"] % 2
                    glob["fs"] += 1
                    for g2 in range(2):
                        S.op("dve", lambda e, g2=g2: e.tensor_copy(osa[fs][:, g2].rearrange("p i e -> p (i e)"),
                                                                   self.pb[4 + g2][:, 0:260]),
                             r=[("pb", 4 + g2)], w=[("osa", fs, g2)])
                    if jq == 0:
                        yslot[qb] = glob["ystA"] % 2
                        glob["ystA"] += 1
                    ysA = yslot[qb]
                    ok = [("osa", fs, 0), ("osa", fs, 1)]
                    ops = []
                    for g2 in range(2):
                        ops.append(lambda g2=g2: S.op("dve", lambda e: e.tensor_tensor(
                            out=den[fs][:, 4 * g2:4 * g2 + 4], in0=osa[fs][:, g2, :, 64], in1=self.esink[:, 4 * g2:4 * g2 + 4],
                            op=ALU.add), r=ok + ["esink"], w=[("den", fs, g2)]))
                    ops.append(lambda: S.op("dve", lambda e: e.reciprocal(rden[fs][:], den[fs][:]),
                                            r=[("den", fs, 0), ("den", fs, 1)], w=[("rden", fs)]))
                    for g2 in range(2):
                        for hi in range(4):
                            hq = 4 * g2 + hi
                            ops.append(lambda g2=g2, hi=hi, hq=hq: S.op("dve", lambda e: e.tensor_scalar(
                                out=ya[fs][:, hq, :], in0=osa[fs][:, g2, hi, 0:64], scalar1=rden[fs][:, hq:hq + 1], scalar2=None,
                                op0=ALU.mult), r=ok + [("rden", fs)], w=[("ya", fs, hq)]))
                    yav = ya[fs][:].rearrange("p h e -> p (h e)")
                    for m in range(4):
                        ops.append(lambda m=m: S.op("pe", lambda e: e.transpose(pbt7[:, m * 128:(m + 1) * 128],
                                                                                yav[:, m * 128:(m + 1) * 128], self.identb[:]),
                                                    r=[("ya", fs, 2 * m), ("ya", fs, 2 * m + 1), "identb"], w=[("pb", 7)]))
                    ops.append(lambda: S.op("dve", lambda e: e.tensor_copy(ystA[ysA][:, :, jq * 128:(jq + 1) * 128],
                                                                           pbt7[:, 0:512].rearrange("p (m q) -> p m q", m=4)),
                                            r=[("pb", 7)], w=[("ystA", ysA)]))
                    if jq == 3:
                        t0 = s0 + qb * TB
                        ops.append(lambda: S.dma("pool", self.YT[0:4, :, t0:t0 + TB].rearrange("m p t -> p m t"), ystA[ysA][:],
                                                 r=[("ystA", ysA)], w=[("YT", "a", t0)], key=("ystA_st", ysA)))
                    deferred.extend(ops)

                run_stream(len(units), 3, front, back, deferred, 4)

            accs = [(4 + idx // 3, (idx % 3) * 129) for idx in range(8)]

            def b_stream(passes):
                NE = 4
                qbs = []
                for (pi, s0, sl, h) in passes:
                    n = sl // TB
                    for qb in range(n):
                        qbs.append(dict(pi=pi, ks=pi % 2, s0=s0, sl=sl, h=h, qb=qb, nkt=sl // 128, t0=s0 + qb * TB,
                                        first=(qb == 0), last=(qb == n - 1)))
                units = [(Qi, kt) for Qi, Qd in enumerate(qbs) for kt in range(Qd["nkt"])]
                deferred = []
                pidx_of = {p[0]: k for k, p in enumerate(passes)}

                def load_kv(pi, s0, sl, h):
                    ks = pi % 2
                    nkt = sl // 128
                    S.dma("sp", kbuf[ks][:, 0:sl], self.QKT[9 + h, :, s0:s0 + sl], r=[], w=[("kbuf", ks)], key=("kbuf", ks))
                    S.dma("sp", vbuf[ks][:, 0:nkt * 129].rearrange("p (k e) -> p k e", e=129),
                          self.VB[s0:s0 + sl, h, :].rearrange("(k p) e -> p k e", p=128), r=[], w=[("vbuf", ks)],
                          key=("vbuf", ks))
                    bf = bfull[ks]
                    for d in range(-1, 5):
                        for jq in range(4):
                            o = d - jq
                            dst = bf[:, d + 1, jq * 128:(jq + 1) * 128]
                            if -1 <= o <= 1:
                                S.op("pool", lambda e, dst=dst, o=o: e.tensor_copy(dst, tbB[:, h, o + 1, :]),
                                     r=["tbB"], w=[("bfull", ks)])
                            else:
                                cc = self.col("cfar", (8 + h) * 2 + (0 if o < 0 else 1))
                                S.op("pool", lambda e, dst=dst, cc=cc: e.tensor_scalar(out=dst, in0=self.subln08[:], scalar1=0.0,
                                                                                       scalar2=cc, op0=ALU.mult, op1=ALU.add),
                                     r=["cols", "subln08"], w=[("bfull", ks)])

                def front(i):
                    Qi, kt = units[i]
                    Qd = qbs[Qi]
                    if kt == 0:
                        qs = glob["q"] % 3
                        glob["q"] += 1
                        Qd["qs"] = qs
                        S.dma("pool", qbuf[qs][:, 0:TB], self.QKT[5 + Qd["h"], :, Qd["t0"]:Qd["t0"] + TB], r=[],
                              w=[("qbuf", qs)], key=("qbuf", qs))
                    ks, qs, h = Qd["ks"], Qd["qs"], Qd["h"]
                    r2 = i % 2
                    es = i % NE
                    sp2 = self.pbb[r2]
                    for c in range(2):
                        S.op("pe", lambda e, c=c: e.matmul(sp2[:, c * TB:(c + 1) * TB], kbuf[ks][64 * c:64 * c + 64, kt * 128:(kt + 1) * 128],
                                                           qbuf[qs][64 * c:64 * c + 64, 0:TB], start=True, stop=True),
                             r=[("kbuf", ks), ("qbuf", qs)], w=[("pb", 2 * r2 + c)])
                    pk = [("pb", 2 * r2), ("pb", 2 * r2 + 1)]
                    d = kt - 4 * Qd["qb"]
                    if d < -1 or d > 4:
                        cc = self.col("cfar", (8 + h) * 2 + (0 if d < 0 else 1))
                        S.op("act", lambda e: e.activation(Eb[es][:], sp2[:], AF.Exp, bias=cc, scale=0.125),
                             r=pk + ["cols"], w=[("E", es)])
                    else:
                        ts = glob["tmp"] % 3
                        glob["tmp"] += 1
                        for c in range(2):
                            S.op("dve", lambda e, c=c: e.scalar_tensor_tensor(
                                out=tmp[ts][:, c * TB:(c + 1) * TB], in0=sp2[:, c * TB:(c + 1) * TB], scalar=0.125,
                                in1=bfull[ks][:, d + 1, :], op0=ALU.mult, op1=ALU.add),
                                r=[("pb", 2 * r2 + c), ("bfull", ks)], w=[("tmp", ts)])
                        S.op("act", lambda e: e.activation(Eb[es][:], tmp[ts][:], AF.Exp), r=[("tmp", ts)], w=[("E", es)])

                def finalize(Qi):
                    Qd = qbs[Qi]
                    h, t0 = Qd["h"], Qd["t0"]
                    while deferred:
                        deferred.pop(0)()
                    osl = Qi % 2
                    of = osb[osl][:].rearrange("p c j e -> p (c j e)")
                    for bi_, (bank, n) in enumerate(((4, 387), (5, 387), (6, 258))):
                        S.op("dve", lambda e, bi_=bi_, bank=bank, n=n: e.tensor_copy(of[:, bi_ * 387:bi_ * 387 + n],
                                                                                      self.pb[bank][:, 0:n]),
                             r=[("pb", bank)], w=[("osb", osl, bi_)])
                    ok = [("osb", osl, 0), ("osb", osl, 1), ("osb", osl, 2)]
                    ys = glob["yst"] % 2
                    glob["yst"] += 1
                    for jq in range(4):
                        fs = glob["fs"] % 2
                        glob["fs"] += 1

                        def mkops(jq=jq, fs=fs):
                            return [
                                lambda: S.op("dve", lambda e: e.reciprocal(rr[fs][:, 0:2], osb[osl][:, :, jq, 128]),
                                             r=ok, w=[("rr", fs)]),
                                lambda: S.op("dve", lambda e: e.tensor_tensor(out=rr[fs][:, 2:3], in0=rr[fs][:, 1:2],
                                                                              in1=self.neglam[:], op=ALU.mult),
                                             r=[("rr", fs), "neglam"], w=[("rr2", fs)]),
                                lambda: S.op("dve", lambda e: e.tensor_scalar(out=yf[fs][:], in0=osb[osl][:, 0, jq, 0:128],
                                                                              scalar1=rr[fs][:, 0:1], scalar2=None, op0=ALU.mult),
                                             r=ok + [("rr", fs)], w=[("yf", fs)]),
                                lambda: S.op("dve", lambda e: e.scalar_tensor_tensor(
                                    out=yf2[fs][:], in0=osb[osl][:, 1, jq, 0:128], scalar=rr[fs][:, 2:3], in1=yf[fs][:],
                                    op0=ALU.mult, op1=ALU.add), r=ok + [("rr2", fs), ("yf", fs)], w=[("yf2", fs)]),
                                lambda: S.op("dve", lambda e: e.tensor_tensor(out=junk[fs][:], in0=yf2[fs][:], in1=yf2[fs][:],
                                                                              op=ALU.mult),
                                             r=[("yf2", fs)], w=[("junk", fs)]),
                                lambda: S.op("dve", lambda e: e.reduce_sum(out=ssq[fs][:], in_=junk[fs][:], axis=AX.X),
                                             r=[("junk", fs)], w=[("ssq", fs)]),
                                lambda: S.op("act", lambda e: e.activation(ltb[fs][:], ssq[fs][:], AF.Ln, bias=self.col("eps"),
                                                                           scale=1.0 / 128),
                                             r=[("ssq", fs), "cols"], w=[("lt", fs)]),
                                lambda: S.op("act", lambda e: e.activation(rsb[fs][:], ltb[fs][:], AF.Exp, scale=-0.5),
                                             r=[("lt", fs)], w=[("rs", fs)]),
                                lambda: S.op("dve", lambda e: e.scalar_tensor_tensor(
                                    out=ynb[fs][:], in0=yf2[fs][:], scalar=rsb[fs][:, 0:1], in1=self.subln08[:],
                                    op0=ALU.mult, op1=ALU.mult), r=[("yf2", fs), ("rs", fs), "subln08"], w=[("ynb", fs)]),
                                lambda: S.op("pe", lambda e: e.transpose(pbt7[:, jq * 128:(jq + 1) * 128], ynb[fs][:],
                                                                         self.identb[:]),
                                             r=[("ynb", fs), "identb"], w=[("pb", 7)]),
                            ]
                        deferred.extend(mkops())
                    deferred.append(lambda: S.op("dve", lambda e: e.tensor_copy(yst[ys][:], pbt7[:, 0:TB]), r=[("pb", 7)],
                                                 w=[("yst", ys)]))
                    deferred.append(lambda: S.dma("pool", self.YT[4 + h, :, t0:t0 + TB], yst[ys][:], r=[("yst", ys)],
                                                  w=[("YT", h, t0)], key=("yst_st", ys)))

                def back(j):
                    Qi, kt = units[j]
                    Qd = qbs[Qi]
                    ks, nkt = Qd["ks"], Qd["nkt"]
                    es = j % NE
                    if kt == 0:
                        zero_banks((4, 5, 6))
                    vb = vbuf[ks][:, 0:nkt * 129].rearrange("p (k e) -> p k e", e=129)
                    for c in range(2):
                        for jq in range(4):
                            bank, off = accs[c * 4 + jq]
                            S.op("pe", lambda e, c=c, jq=jq, bank=bank, off=off: e.matmul(
                                self.pb[bank][:, off:off + 129], Eb[es][:, c * TB + jq * 128:c * TB + (jq + 1) * 128], vb[:, kt, :],
                                start=False, stop=(kt == nkt - 1), skip_group_check=True),
                                r=[("E", es), ("vbuf", ks)], w=[("pb", bank)])
                    pidx = pidx_of[Qd["pi"]]
                    if kt == 0 and Qd["first"] and pidx == 0 and len(passes) > 1:
                        load_kv(*passes[1])
                    if kt == nkt - 1:
                        finalize(Qi)
                        if Qd["last"] and pidx + 2 < len(passes):
                            load_kv(*passes[pidx + 2])

                load_kv(*passes[0])
                rate = max(2, -(-48 // qbs[0]["nkt"]) + 1)
                run_stream(len(units), 2, front, back, deferred, rate)

            parts = "ab"
            pi = 0
            s0 = 0
            for sl in self.seqs:
                if "a" in parts:
                    a_stream(pi, s0, sl)
                    pi += 1
                if "b" in parts:
                    passes = []
                    for h in range(4):
                        passes.append((pi, s0, sl, h))
                        pi += 1
                    b_stream(passes)
                s0 += sl
            S.emit("p2a")

    def ffn_bufs(self, sb, pfx):
        B = {"act": sb(pfx + "act", [128, FC, TB], BF16),
             "sg": [sb(pfx + "sg%d" % i, [128, TB], F32) for i in range(2)],
             "wg": [sb(pfx + "wg%d" % i, [128, 2, KC, 128], BF16) for i in range(3)],
             "wu": [sb(pfx + "wu%d" % i, [128, 2, KC, 128], BF16) for i in range(3)],
             "wd": [sb(pfx + "wd%d" % i, [128, FC, 128], BF16) for i in range(2)],
             "cg": 0, "cd": 0}
        return B

    def ffn(self, l, hT, hTk, xres, xresk, B, bg=None):
        S = self.S
        nbg = (len(bg) // (FC // 2 + KC - 2) + 1) if bg else 0
        Wg, Wu, Wd = self.W["g%d" % l], self.W["u%d" % l], self.W["d%d" % l]
        act, sg = B["act"], B["sg"]
        dslots = {}

        def load_wd(oc):
            ds = B["cd"] % 2
            B["cd"] += 1
            dslots[oc] = ds
            S.dma("sp", B["wd"][ds][:].rearrange("p k n -> p (k n)"), Wd[oc].rearrange("p k n -> p (k n)"), r=[],
                  w=[("wd", ds)], key=("wd", ds))

        for jp in range(FC // 2):
            ws = B["cg"] % 3
            B["cg"] += 1
            wg, wu = B["wg"][ws], B["wu"][ws]
            S.dma("sp", wg[:].rearrange("p j k n -> p j (k n)"), Wg[2 * jp:2 * jp + 2].rearrange("j p k n -> p j (k n)"),
                  r=[], w=[("wg", ws)], key=("wg", ws))
            S.dma("sp", wu[:].rearrange("p j k n -> p j (k n)"), Wu[2 * jp:2 * jp + 2].rearrange("j p k n -> p j (k n)"),
                  r=[], w=[("wu", ws)], key=("wu", ws))
            if bg:
                S.drain(bg, nbg)
            if jp == 8:
                load_wd(0)
            if jp == 10:
                load_wd(1)
            for jj in range(2):
                j = 2 * jp + jj
                gb = (4, 5)[j % 2]
                ub = (6, 7)[j % 2]
                for kc in range(KC):
                    S.op("pe", lambda e, wg=wg, jj=jj, kc=kc, gb=gb: e.matmul(self.pb[gb][:], wg[:, jj, kc, :], hT[:, kc, :],
                                                                             start=(kc == 0), stop=(kc == KC - 1)),
                         r=[(hTk, kc), ("wg", ws)], w=[("pb", gb)])
                for kc in range(KC):
                    S.op("pe", lambda e, wu=wu, jj=jj, kc=kc, ub=ub: e.matmul(self.pb[ub][:], wu[:, jj, kc, :], hT[:, kc, :],
                                                                             start=(kc == 0), stop=(kc == KC - 1)),
                         r=[(hTk, kc), ("wu", ws)], w=[("pb", ub)])
                ss = j % 2
                S.op("act", lambda e, ss=ss, gb=gb: e.activation(sg[ss][:], self.pb[gb][:], AF.Silu),
                     r=[("pb", gb)], w=[("sg", ss)])
                S.op("dve", lambda e, ss=ss, ub=ub, j=j: e.tensor_tensor(out=act[:, j, :], in0=self.pb[ub][:], in1=sg[ss][:],
                                                                         op=ALU.mult),
                     r=[("pb", ub), ("sg", ss)], w=[("act", j)])
        for oc in range(KC):
            if oc not in dslots:
                load_wd(oc)
            if bg and oc < KC - 2:
                S.drain(bg, nbg)
            ds = dslots[oc]
            wd = B["wd"][ds]
            db = (0, 1)[oc % 2]
            for jc in range(FC):
                S.op("pe", lambda e, wd=wd, jc=jc, db=db: e.matmul(self.pb[db][:], wd[:, jc, :], act[:, jc, :],
                                                                   start=(jc == 0), stop=(jc == FC - 1)),
                     r=[("act", jc), ("wd", ds)], w=[("pb", db)])
            S.op("dve", lambda e, oc=oc, db=db: e.tensor_tensor(out=xres[:, oc, :], in0=self.pb[db][:], in1=xres[:, oc, :],
                                                                op=ALU.add),
                 r=[("pb", db), (xresk, oc)], w=[(xresk, oc)])

    def phase2b(self):
        nc, S = self.nc, self.S
        with contextlib.ExitStack() as ps:
            sb = lambda n, sh, dt: ps.enter_context(nc.sbuf_tensor("s_" + n, list(sh), dt))
            xT = [sb("pb_xT%d" % i, [128, KC, TB], F32) for i in range(2)]
            xtok = [sb("pb_xtok%d" % i, [128, 4, D], F32) for i in range(2)] if not OPT_X1T else None
            ytb = [sb("pb_yt%d" % i, [128, KC, TB], BF16) for i in range(2)]
            hT = [sb("pb_hT%d" % i, [128, KC, TB], BF16) for i in range(2)]
            sqb = [sb("pb_sq%d" % i, [128, TB], BF16) for i in range(2)]
            lnt = sb("pb_lnt", [128, TB], F32)
            rstd = sb("pb_rstd", [128, TB], F32)
            wout = sb("pb_wout", [128, KC, KC, 128], BF16)
            B = self.ffn_bufs(sb, "pb_")
            S.dma("sp", wout[:].rearrange("p j k n -> p j (k n)"), self.W["aout"][:].rearrange("j p k n -> p j (k n)"),
                  r=[], w=["wout"], key="wout")

            def stage_a(bi, t0):
                xs = bi % 2
                xk = ("xT", xs)
                if OPT_X1T:
                    S.dma("sp", xT[xs][:], self.X1T[:, :, t0:t0 + TB].rearrange("c p t -> p c t"), r=[],
                          w=[(xk, c) for c in range(KC)], key=xk)
                else:
                    self.load_xT(t0, xtok[xs], ("xtok", xs), xT[xs], xk, tb=(0, 1))
                S.dma("sp", ytb[xs][:], self.YT[:, :, t0:t0 + TB].rearrange("m p t -> p m t"), r=[], w=[("ytb", xs)],
                      key=("ytb", xs))
                for oc in range(KC):
                    bk = (2, 3)[oc % 2]
                    for kc in range(KC):
                        S.op("pe", lambda e, oc=oc, kc=kc, bk=bk: e.matmul(self.pb[bk][:], wout[:, oc, kc, :], ytb[xs][:, kc, :],
                                                                           start=(kc == 0), stop=(kc == KC - 1)),
                             r=[("ytb", xs), "wout"], w=[("pb", bk)])
                    S.op("dve", lambda e, oc=oc, bk=bk: e.tensor_tensor(out=xT[xs][:, oc, :], in0=self.pb[bk][:],
                                                                        in1=xT[xs][:, oc, :], op=ALU.add),
                         r=[("pb", bk), (xk, oc)], w=[(xk, oc)])
                self.rmsnorm(xT[xs], xk, hT[xs], ("hT", xs), "ffng", 0, TB, sqb, 2, (lnt, rstd), "pbn")

            def stage_b(bi, t0, bg):
                xs = bi % 2
                xk = ("xT", xs)
                self.ffn(0, hT[xs], ("hT", xs), xT[xs], xk, B, bg=bg)
                S.drain(bg)
                S.dma("pool", self.X2T[:, :, t0:t0 + TB].rearrange("c p t -> p c t"), xT[xs][:],
                      r=[(xk, c) for c in range(KC)], w=[("X2T", bi)], key=("x2t_st", xs))

            blks = self.blocks()
            stage_a(0, blks[0][3])
            for bi, (s0, sl, b, t0) in enumerate(blks):
                bg = []
                if bi + 1 < len(blks):
                    S.begin_capture()
                    stage_a(bi + 1, blks[bi + 1][3])
                    bg = S.end_capture()
                stage_b(bi, t0, bg)
            S.emit("p2b")

    def phase3(self):
        nc, S = self.nc, self.S
        with contextlib.ExitStack() as ps:
            sb = lambda n, sh, dt: ps.enter_context(nc.sbuf_tensor("s_" + n, list(sh), dt))
            xms = [sb("p3_xm%d" % i, [128, KC, TB], F32) for i in range(2)]
            xhs = [sb("p3_xh%d" % i, [128, KC, 32], F32) for i in range(2)]
            hC = sb("p3_hC", [128, KC, TB], BF16)
            hTm = sb("p3_hTm", [128, KC, TB], BF16)
            hTh = sb("p3_hTh", [128, KC, 32], BF16)
            sqb = [sb("p3_sq%d" % i, [128, TB], BF16) for i in range(2)]
            lnt = sb("p3_lnt", [128, TB], F32)
            rstd = sb("p3_rstd", [128, TB], F32)
            wc = [sb("p3_wc%d" % i, [128, KC, 128], BF16) for i in range(6)]
            ddw = sb("p3_ddw", [128, 4, 31, 128], BF16)
            dsc = sb("p3_dsc", [128, 4, 3, 128], BF16)
            gx = sb("p3_gx", [128, 4, 544], BF16)
            ub = sb("p3_u", [128, 4, 544], BF16)
            gcs = [sb("p3_gcs%d" % i, [128, TB], F32) for i in range(2)]
            hs = [sb("p3_hs%d" % i, [128, 32], F32) for i in range(2)]
            scs = sb("p3_scs", [128, TB], F32)
            vbf = [sb("p3_vbf%d" % i, [128, TB], BF16) for i in range(2)]
            vsq = [sb("p3_vsq%d" % i, [128, TB], BF16) for i in range(2)]
            yT = sb("p3_yT", [128, KC, TB], BF16)
            otok = [sb("p3_otok%d" % i, [128, D], F32) for i in range(2)]
            B = self.ffn_bufs(sb, "p3_")
            actf = B["act"][:].rearrange("p j t -> p (j t)").bitcast(F32)
            vb = actf[:, 0:4 * TB].rearrange("p (c t) -> p c t", c=4)
            vbk = lambda ch: [("act", 2 * ch), ("act", 2 * ch + 1)]
            mean = actf[:, 4 * TB:5 * TB]
            meank = [("act", 8), ("act", 9)]
            msq = actf[:, 5 * TB:6 * TB]
            msqk = [("act", 10), ("act", 11)]
            var = actf[:, 6 * TB:7 * TB]
            vark = [("act", 12), ("act", 13)]
            rs2 = actf[:, 7 * TB:8 * TB]
            rs2k = [("act", 14), ("act", 15)]
            t1 = [actf[:, (8 + i) * TB:(9 + i) * TB] for i in range(2)]
            t1k = [[("act", 16 + 2 * i), ("act", 17 + 2 * i)] for i in range(2)]
            ctr = {"wc": 0, "ot": 0}
            P, Q, H, R, C1, C2, S1, S2 = range(8)

            for ch in range(4):
                for j in range(31):
                    S.op("dve", lambda e, ch=ch, j=j: e.tensor_scalar(out=ddw[:, ch, j, :], in0=self.identb[:],
                                                                      scalar1=self.col("dww", ch * 31 + j), scalar2=None,
                                                                      op0=ALU.mult),
                         r=["identb", "cols"], w=["ddw"])
                for j in range(3):
                    S.op("dve", lambda e, ch=ch, j=j: e.tensor_scalar(out=dsc[:, ch, j, :], in0=self.identb[:],
                                                                      scalar1=self.col("scw", ch * 3 + j), scalar2=None,
                                                                      op0=ALU.mult),
                         r=["identb", "cols"], w=["dsc"])

            def load_w(src):
                i = ctr["wc"] % 6
                ctr["wc"] += 1
                S.dma("sp", wc[i][:].rearrange("p k n -> p (k n)"), src.rearrange("p k n -> p (k n)"), r=[], w=[("wc", i)],
                      key=("wc", i))
                return i

            def stage_a(bi, s0, sl, b, t0):
                xs = bi % 2
                xm, xh = xms[xs], xhs[xs]
                xk = ("xm", xs)
                xhk = ("xh", xs)
                S.dma("sp", xm[:], self.X2T[:, :, t0:t0 + TB].rearrange("c p t -> p c t"), r=[],
                      w=[(xk, c) for c in range(KC)], key=xk)
                hk = [(xhk, c) for c in range(KC)]
                if b > 0:
                    S.dma("sp", xh[:, :, 0:16], self.X2T[:, :, t0 - 16:t0].rearrange("c p t -> p c t"), r=[], w=hk,
                          key=("xhL", xs))
                else:
                    S.op("dve", lambda e: e.memset(xh[:, :, 0:16], 0.0), w=hk)
                if (b + 1) * TB < sl:
                    S.dma("sp", xh[:, :, 16:32], self.X2T[:, :, t0 + TB:t0 + TB + 16].rearrange("c p t -> p c t"), r=[], w=hk,
                          key=("xhR", xs))
                else:
                    S.op("dve", lambda e: e.memset(xh[:, :, 16:32], 0.0), w=hk)
                self.rmsnorm(xm, xk, hC, "hC", "mixg", 8, TB, sqb, H, (lnt, rstd), "p3n")
                self.rmsnorm(xh, xhk, hTh, "hTh", "mixg", 8, 32, sqb, R, (lnt, rstd), "p3n")

            def blk(bi, s0, sl, b, t0, bg):
                xm = xms[bi % 2]
                xk = ("xm", bi % 2)

                def proj(j, bank, hcol):
                    wi = load_w(self.W["cin"][j])
                    w = wc[wi]
                    for kc in range(KC):
                        S.op("pe", lambda e, w=w, kc=kc: e.matmul(self.pb[bank][:], w[:, kc, :], hC[:, kc, :],
                                                                  start=(kc == 0), stop=(kc == KC - 1)),
                             r=[("hC", kc), ("wc", wi)], w=[("pb", bank)])
                    if hcol is not None:
                        for kc in range(KC):
                            S.op("pe", lambda e, w=w, kc=kc: e.matmul(self.pb[H][:, hcol:hcol + 32], w[:, kc, :], hTh[:, kc, :],
                                                                      start=(kc == 0), stop=(kc == KC - 1)),
                                 r=[("hTh", kc), ("wc", wi)], w=[("pb", H)])

                def prod(dst, dk, ch, s, func, pa, pbk, ha, hb):
                    if func is None:
                        S.op("act", lambda e: e.copy(gcs[s][:], self.pb[pbk][:]), r=[("pb", pbk)], w=[("gcs", s)])
                        S.op("act", lambda e: e.copy(hs[s][:], self.pb[H][:, hb:hb + 32]), r=[("pb", H)], w=[("hs", s)])
                    else:
                        S.op("act", lambda e: e.activation(gcs[s][:], self.pb[pbk][:], func), r=[("pb", pbk)], w=[("gcs", s)])
                        S.op("act", lambda e: e.activation(hs[s][:], self.pb[H][:, hb:hb + 32], func), r=[("pb", H)],
                             w=[("hs", s)])
                    S.op("dve", lambda e: e.tensor_tensor(out=dst[:, ch, 16:16 + TB], in0=self.pb[pa][:], in1=gcs[s][:], op=ALU.mult),
                         r=[("pb", pa), ("gcs", s)], w=[(dk, ch)])
                    S.op("dve", lambda e: e.tensor_tensor(out=dst[:, ch, 0:16], in0=self.pb[H][:, ha:ha + 16], in1=hs[s][:, 0:16],
                                                          op=ALU.mult),
                         r=[("pb", H), ("hs", s)], w=[(dk, ch)])
                    S.op("dve", lambda e: e.tensor_tensor(out=dst[:, ch, 16 + TB:32 + TB], in0=self.pb[H][:, ha + 16:ha + 32],
                                                          in1=hs[s][:, 16:32], op=ALU.mult),
                         r=[("pb", H), ("hs", s)], w=[(dk, ch)])

                if OPT_P3ORD:
                    cbanks = ((P, Q, 64, 96), (R, C1, 0, 32))

                    def conf_proj(ch):
                        pa, pg, ha, hg = cbanks[ch % 2]
                        proj(12 + ch, pa, ha)
                        proj(16 + ch, pg, hg)

                    conf_proj(0)
                    for ch in range(4):
                        s = ch % 2
                        pa, pg, ha, hg = cbanks[ch % 2]
                        if ch + 1 < 4:
                            conf_proj(ch + 1)
                        prod(ub, "u", ch, 1, AF.Sigmoid, pa, pg, ha, hg)
                        for j in range(31):
                            S.op("pe", lambda e, ch=ch, j=j: e.matmul(self.pb[C2][:], ddw[:, ch, j, :], ub[:, ch, 1 + j:1 + j + TB],
                                                                      start=(j == 0), stop=(j == 30)),
                                 r=[("u", ch), "ddw"], w=[("pb", C2)])
                        bcol = self.col("dwb", ch)
                        S.op("act", lambda e, ch=ch, bcol=bcol: e.activation(vb[:, ch, :], self.pb[C2][:], AF.Identity, bias=bcol),
                             r=[("pb", C2), "cols"], w=vbk(ch))
                        S.op("act", lambda e, s=s, bcol=bcol: e.activation(vbf[s][:], self.pb[C2][:], AF.Identity, bias=bcol),
                             r=[("pb", C2), "cols"], w=[("vbf", s)])
                        S.op("act", lambda e, s=s, bcol=bcol: e.activation(vsq[s][:], self.pb[C2][:], AF.Square, bias=bcol),
                             r=[("pb", C2), "cols"], w=[("vsq", s)])
                        S.op("pe", lambda e, s=s, ch=ch: e.matmul(self.pb[S1][:], self.ones[:], vbf[s][:], start=(ch == 0), stop=(ch == 3)),
                             r=[("vbf", s), "ones"], w=[("pb", S1)])
                        S.op("pe", lambda e, s=s, ch=ch: e.matmul(self.pb[S2][:], self.ones[:], vsq[s][:], start=(ch == 0), stop=(ch == 3)),
                             r=[("vsq", s), "ones"], w=[("pb", S2)])
                    S.op("dve", lambda e: e.tensor_scalar(out=mean, in0=self.pb[S1][:], scalar1=1.0 / 512, scalar2=None, op0=ALU.mult),
                         r=[("pb", S1)], w=meank)
                    S.op("dve", lambda e: e.tensor_tensor(out=msq, in0=mean, in1=mean, op=ALU.mult), r=meank, w=msqk)
                    S.op("dve", lambda e: e.scalar_tensor_tensor(out=var, in0=self.pb[S2][:], scalar=1.0 / 512, in1=msq,
                                                                 op0=ALU.mult, op1=ALU.subtract),
                         r=[("pb", S2)] + msqk, w=vark)
                    S.op("dve", lambda e: e.tensor_scalar(out=msq, in0=var, scalar1=0.0, scalar2=None, op0=ALU.max), r=vark, w=msqk)
                    S.op("act", lambda e: e.activation(var, msq, AF.Ln, bias=self.col("eps")), r=msqk + ["cols"], w=vark)
                    S.op("act", lambda e: e.activation(rs2, var, AF.Exp, scale=-0.5), r=vark, w=rs2k)
                    for ch in range(4):
                        s = ch % 2
                        proj(4 + ch, P, 0)
                        proj(8 + ch, Q, 32)
                        prod(gx, "gx", ch, 0, None, Q, P, 32, 0)
                        proj(ch, R, None)
                        for j in range(3):
                            S.op("pe", lambda e, ch=ch, j=j: e.matmul(self.pb[C1][:], dsc[:, ch, j, :], gx[:, ch, 15 + j:15 + j + TB],
                                                                      start=(j == 0), stop=(j == 2)),
                                 r=[("gx", ch), "dsc"], w=[("pb", C1)])
                        S.op("act", lambda e: e.copy(scs[:], self.pb[C1][:]), r=[("pb", C1)], w=["scs"])
                        S.op("dve", lambda e, ch=ch: e.tensor_tensor(out=yT[:, ch, :], in0=self.pb[R][:], in1=scs[:], op=ALU.mult),
                             r=[("pb", R), "scs"], w=[("yT", ch)])
                        S.op("dve", lambda e, ch=ch, s=s: e.tensor_tensor(out=t1[s], in0=vb[:, ch, :], in1=mean, op=ALU.subtract),
                             r=vbk(ch) + meank, w=t1k[s])
                        S.op("dve", lambda e, s=s: e.tensor_tensor(out=gcs[1][:], in0=t1[s], in1=rs2, op=ALU.mult),
                             r=t1k[s] + rs2k, w=[("gcs", 1)])
                        S.op("act", lambda e, ch=ch: e.activation(yT[:, 4 + ch, :], gcs[1][:], AF.Silu,
                                                                  bias=self.col("lnb", ch), scale=self.col("lng", ch)),
                             r=[("gcs", 1), "cols"], w=[("yT", 4 + ch)])
                else:
                    for ch in range(4):
                        s = ch % 2
                        proj(4 + ch, P, 0)
                        proj(8 + ch, Q, 32)
                        prod(gx, "gx", ch, 0, None, Q, P, 32, 0)
                        proj(12 + ch, P, 64)
                        proj(16 + ch, Q, 96)
                        prod(ub, "u", ch, 1, AF.Sigmoid, P, Q, 64, 96)
                        proj(ch, R, None)
                        for j in range(3):
                            S.op("pe", lambda e, ch=ch, j=j: e.matmul(self.pb[C1][:], dsc[:, ch, j, :], gx[:, ch, 15 + j:15 + j + TB],
                                                                      start=(j == 0), stop=(j == 2)),
                                 r=[("gx", ch), "dsc"], w=[("pb", C1)])
                        S.op("act", lambda e: e.copy(scs[:], self.pb[C1][:]), r=[("pb", C1)], w=["scs"])
                        S.op("dve", lambda e, ch=ch: e.tensor_tensor(out=yT[:, ch, :], in0=self.pb[R][:], in1=scs[:], op=ALU.mult),
                             r=[("pb", R), "scs"], w=[("yT", ch)])
                        for j in range(31):
                            S.op("pe", lambda e, ch=ch, j=j: e.matmul(self.pb[C2][:], ddw[:, ch, j, :], ub[:, ch, 1 + j:1 + j + TB],
                                                                      start=(j == 0), stop=(j == 30)),
                                 r=[("u", ch), "ddw"], w=[("pb", C2)])
                        bcol = self.col("dwb", ch)
                        S.op("act", lambda e, ch=ch, bcol=bcol: e.activation(vb[:, ch, :], self.pb[C2][:], AF.Identity, bias=bcol),
                             r=[("pb", C2), "cols"], w=vbk(ch))
                        S.op("act", lambda e, s=s, bcol=bcol: e.activation(vbf[s][:], self.pb[C2][:], AF.Identity, bias=bcol),
                             r=[("pb", C2), "cols"], w=[("vbf", s)])
                        S.op("act", lambda e, s=s, bcol=bcol: e.activation(vsq[s][:], self.pb[C2][:], AF.Square, bias=bcol),
                             r=[("pb", C2), "cols"], w=[("vsq", s)])
                        S.op("pe", lambda e, s=s, ch=ch: e.matmul(self.pb[S1][:], self.ones[:], vbf[s][:], start=(ch == 0), stop=(ch == 3)),
                             r=[("vbf", s), "ones"], w=[("pb", S1)])
                        S.op("pe", lambda e, s=s, ch=ch: e.matmul(self.pb[S2][:], self.ones[:], vsq[s][:], start=(ch == 0), stop=(ch == 3)),
                             r=[("vsq", s), "ones"], w=[("pb", S2)])
                    S.op("dve", lambda e: e.tensor_scalar(out=mean, in0=self.pb[S1][:], scalar1=1.0 / 512, scalar2=None, op0=ALU.mult),
                         r=[("pb", S1)], w=meank)
                    S.op("dve", lambda e: e.tensor_tensor(out=msq, in0=mean, in1=mean, op=ALU.mult), r=meank, w=msqk)
                    S.op("dve", lambda e: e.scalar_tensor_tensor(out=var, in0=self.pb[S2][:], scalar=1.0 / 512, in1=msq,
                                                                 op0=ALU.mult, op1=ALU.subtract),
                         r=[("pb", S2)] + msqk, w=vark)
                    S.op("dve", lambda e: e.tensor_scalar(out=msq, in0=var, scalar1=0.0, scalar2=None, op0=ALU.max), r=vark, w=msqk)
                    S.op("act", lambda e: e.activation(var, msq, AF.Ln, bias=self.col("eps")), r=msqk + ["cols"], w=vark)
                    S.op("act", lambda e: e.activation(rs2, var, AF.Exp, scale=-0.5), r=vark, w=rs2k)
                    for ch in range(4):
                        s = ch % 2
                        S.op("dve", lambda e, ch=ch, s=s: e.tensor_tensor(out=t1[s], in0=vb[:, ch, :], in1=mean, op=ALU.subtract),
                             r=vbk(ch) + meank, w=t1k[s])
                        S.op("dve", lambda e, s=s: e.tensor_tensor(out=gcs[s][:], in0=t1[s], in1=rs2, op=ALU.mult),
                             r=t1k[s] + rs2k, w=[("gcs", s)])
                        S.op("act", lambda e, ch=ch, s=s: e.activation(yT[:, 4 + ch, :], gcs[s][:], AF.Silu,
                                                                       bias=self.col("lnb", ch), scale=self.col("lng", ch)),
                             r=[("gcs", s), "cols"], w=[("yT", 4 + ch)])
                for oc in range(KC):
                    wi = load_w(self.W["cout"][oc])
                    w = wc[wi]
                    bk = (C1, C2)[oc % 2]
                    for kc in range(KC):
                        S.op("pe", lambda e, w=w, kc=kc, bk=bk: e.matmul(self.pb[bk][:], w[:, kc, :], yT[:, kc, :],
                                                                         start=(kc == 0), stop=(kc == KC - 1)),
                             r=[("yT", kc), ("wc", wi)], w=[("pb", bk)])
                    S.op("dve", lambda e, oc=oc, bk=bk: e.tensor_tensor(out=xm[:, oc, :], in0=self.pb[bk][:], in1=xm[:, oc, :],
                                                                        op=ALU.add),
                         r=[("pb", bk), (xk, oc)], w=[(xk, oc)])
                self.rmsnorm(xm, xk, hTm, "hTm", "ffng", 8, TB, sqb, S1, (lnt, rstd), "p3n")
                self.ffn(1, hTm, "hTm", xm, xk, B, bg=bg)
                S.drain(bg)
                for jt in range(4):
                    osl = ctr["ot"] % 2
                    ctr["ot"] += 1
                    for half in range(2):
                        bank = (H, R)[half]
                        for cc in range(4):
                            c = half * 4 + cc
                            S.op("pe", lambda e, bank=bank, cc=cc, c=c, jt=jt: e.transpose(
                                self.pb[bank][:, cc * 128:(cc + 1) * 128], xm[:, c, jt * 128:(jt + 1) * 128], self.ident[:]),
                                r=[(xk, c), "ident"], w=[("pb", bank)])
                        if half:
                            S.op("act", lambda e, bank=bank, osl=osl: e.copy(otok[osl][:, 512:1024], self.pb[bank][:]),
                                 r=[("pb", bank)], w=[("otok", osl, 1)])
                        else:
                            S.op("dve", lambda e, bank=bank, osl=osl: e.tensor_copy(otok[osl][:, 0:512], self.pb[bank][:]),
                                 r=[("pb", bank)], w=[("otok", osl, 0)])
                    S.dma("pool", self.y[t0 + jt * 128:t0 + (jt + 1) * 128, :], otok[osl][:],
                          r=[("otok", osl, 0), ("otok", osl, 1)], w=[("y", bi, jt)], key=("otok_st", osl))

            blks = self.blocks()
            stage_a(0, *blks[0])
            for bi, (s0, sl, b, t0) in enumerate(blks):
                bg = []
                if bi + 1 < len(blks):
                    S.begin_capture()
                    stage_a(bi + 1, *blks[bi + 1])
                    bg = S.end_capture()
                blk(bi, s0, sl, b, t0, bg)
            S.emit("p3")


_PROGRAM_CACHE = {}


def _get_program(seqs):
    key = tuple(seqs)
    if key not in _PROGRAM_CACHE:
        kb = KB(seqs)
        _PROGRAM_CACHE[key] = kb.build()
    return _PROGRAM_CACHE[key]


def _core_inputs(inputs, consts, xcore):
    m = dict(x=xcore,
             w_gate=np.ascontiguousarray(inputs["w_gate"], np.float32),
             w_up=np.ascontiguousarray(inputs["w_up"], np.float32),
             w_down=np.ascontiguousarray(inputs["w_down"], np.float32),
             attn_w_in=np.ascontiguousarray(inputs["attn_w_in"][0], np.float32),
             attn_w_out=np.ascontiguousarray(inputs["attn_w_out"][0], np.float32),
             conv_w_in=np.ascontiguousarray(inputs["conv_w_in"][0], np.float32),
             conv_w_out=np.ascontiguousarray(inputs["conv_w_out"][0], np.float32))
    m.update(consts)
    return m


def kernel(**inputs):
    inputs = {k: np.asarray(v) for k, v in inputs.items()}
    xp = inputs["x_prompt"]
    xs = inputs["x_sample"]
    n = N_CORES
    nsp = xs.shape[0] // n
    seqs = [xp.shape[1]] + [xs.shape[1]] * nsp
    consts = _host_consts(inputs)
    nc = _get_program(seqs)
    in_maps = []
    for c in range(n):
        xcore = np.concatenate([xp[c].reshape(-1, D)] + [xs[c * nsp + i].reshape(-1, D) for i in range(nsp)], axis=0)
        in_maps.append(_core_inputs(inputs, consts, np.ascontiguousarray(xcore, np.float32)))
    res = run_bass_kernel_spmd(nc, in_maps, core_ids=list(range(n)))
    yp = np.empty(xp.shape, np.float32)
    ys = np.empty(xs.shape, np.float32)
    sp = xp.shape[1]
    ss = xs.shape[1]
    for c in range(n):
        y = res.results[c]["y"]
        yp[c] = y[0:sp]
        for i in range(nsp):
            ys[c * nsp + i] = y[sp + i * ss:sp + (i + 1) * ss]
    return (yp, ys)
```
